# Optimizing a Trainium2 kernel written in Bass

```python
import jax, jax.numpy as jnp
from jax import lax
import numpy as np

D_MODEL = 1024
BATCH = 32
SEQ = 2048
DEPTH = 2

CTX_LEN = 256
GRID_W = 64

RWKV_HEADS = 4
RWKV_HEAD = 64
RWKV_DIM = RWKV_HEADS * RWKV_HEAD
DECAY_LORA = 64
AAA_LORA = 64
MV_LORA = 32
GATE_LORA = 128
LNX_EPS = 64e-5
ATT_HEADS = 8
ATT_KV_HEADS = 2
ATT_GROUP = ATT_HEADS // ATT_KV_HEADS
ATT_HEAD = 64
ATT_Q_DIM = ATT_HEADS * ATT_HEAD
ATT_KV_DIM = ATT_KV_HEADS * ATT_HEAD
ATT_SCALE = ATT_HEAD ** -0.5
WINDOW = 128
BLOCK = 128
ROPE_BASE = 10000.0
ROPE_FREQS = ATT_HEAD // 4
NEG_INF = -1e30
FOURIER_GROUPS = 4
FOURIER_GROUP_DIM = 64
FOURIER_DIM = FOURIER_GROUPS * FOURIER_GROUP_DIM
N_BRANCH = 3
D_FF = 2816
EPS = 1e-6

RWKV_COL_SIZES = (RWKV_DIM, RWKV_DIM, RWKV_DIM, DECAY_LORA, DECAY_LORA, AAA_LORA, AAA_LORA, GATE_LORA)
RWKV_COLS = sum(RWKV_COL_SIZES)
IN_SIZES = (RWKV_COLS, ATT_Q_DIM, ATT_KV_DIM, ATT_KV_DIM, FOURIER_DIM, N_BRANCH * D_MODEL)
IN_COLS = sum(IN_SIZES)

kernel_name = "hybrid_rwkv7_swa_fourier_diffusion_block"


def _split(t, sizes):
    return jnp.split(t, [int(i) for i in np.cumsum(sizes)[:-1]], axis=-1)


def rms_norm(x, g):
    xf = x.astype(jnp.float32)
    y = xf * lax.rsqrt(jnp.mean(xf * xf, axis=-1, keepdims=True) + EPS)
    return (y * g.astype(jnp.float32)).astype(x.dtype)


def modulate(x, shift, scale):
    return x * (1 + scale) + shift


def _prev(x):
    return jnp.pad(x[:, :-1], ((0, 0), (1, 0), (0, 0)))


def _next(x):
    return jnp.pad(x[:, 1:], ((0, 0), (0, 1), (0, 0)))


def _heads(t):
    return t.astype(jnp.float32).reshape(t.shape[0], t.shape[1], RWKV_HEADS, RWKV_HEAD)


def rwkv_inputs(u, mu, w0, w2, a0, a2, g2, k_k, k_a, v_first, vres):
    u = u + mu[0] * (_prev(u) - u) + mu[1] * (_next(u) - u)
    r, k, v, xw_f, xw_b, xa_f, xa_b, xg = _split(u, RWKV_COL_SIZES)
    if vres is None:
        v_first = v
    else:
        v0, v1, v2 = vres
        v = v + (v_first - v) * jax.nn.sigmoid(v0 + (v @ v1) @ v2)
    kk = _heads(k * k_k)
    kk = kk * lax.rsqrt(jnp.maximum(jnp.sum(kk * kk, axis=-1, keepdims=True), 1e-24))
    decays, keys, removals = [], [], []
    for d, (xw, xa) in enumerate(((xw_f, xa_f), (xw_b, xa_b))):
        w_log = -jax.nn.softplus(-(w0[d] + jnp.tanh(xw) @ w2[d]).astype(jnp.float32)) - 0.5
        decays.append(_heads(jnp.exp(-jnp.exp(w_log))))
        a = jax.nn.sigmoid(a0[d] + xa @ a2[d])
        keys.append(_heads(k * (1 + (a - 1) * k_a)))
        removals.append(kk * _heads(a))
    g = jax.nn.sigmoid(xg) @ g2
    feat = (_heads(r), _heads(v), kk, tuple(decays), tuple(keys), tuple(removals), g)
    return feat, v_first


def wkv_scan(state0, feat, d, reverse):
    r, v, kk, decays, keys, removals, _ = feat

    def step(S, inp):
        r_t, w_t, k_t, v_t, kk_t, b_t = inp
        sa = jnp.einsum("bhvk,bhk->bhv", S, -kk_t)
        S = S * w_t[:, :, None, :] + sa[..., :, None] * b_t[..., None, :] + v_t[..., :, None] * k_t[..., None, :]
        return S, jnp.einsum("bhvk,bhk->bhv", S, r_t)

    xs = tuple(jnp.swapaxes(t, 0, 1) for t in (r, decays[d], keys[d], v, kk, removals[d]))
    S_last, y = lax.scan(step, state0, xs, reverse=reverse)
    return jnp.swapaxes(y, 0, 1), S_last


def rwkv_readout(y, feat, r_k, lnx_w, lnx_b):
    r, v, kk, decays, keys, removals, g = feat
    B, T = y.shape[:2]
    mean = jnp.mean(y, axis=-1, keepdims=True)
    var = jnp.mean(jnp.square(y - mean), axis=-1, keepdims=True)
    yn = ((y - mean) * lax.rsqrt(var + LNX_EPS)).reshape(B, T, RWKV_DIM) * lnx_w + lnx_b
    k_bar = 0.5 * (keys[0] + keys[1])
    bonus = jnp.sum(r * k_bar * r_k.reshape(RWKV_HEADS, RWKV_HEAD), axis=-1, keepdims=True) * v
    return (yn + bonus.reshape(B, T, RWKV_DIM)) * g


def rwkv_branch(u_ctx, u_lat, vf_ctx, vf_lat, vres, need_ctx, mu, w0, w2, a0, a2, g2, k_k, k_a, r_k, lnx_w, lnx_b):
    fc, vf_ctx = rwkv_inputs(u_ctx, mu, w0, w2, a0, a2, g2, k_k, k_a, vf_ctx, vres)
    fl, vf_lat = rwkv_inputs(u_lat, mu, w0, w2, a0, a2, g2, k_k, k_a, vf_lat, vres)
    S0 = jnp.zeros((u_lat.shape[0], RWKV_HEADS, RWKV_HEAD, RWKV_HEAD), jnp.float32)
    yc_f, Sc_f = wkv_scan(S0, fc, 0, reverse=False)
    yc_b, Sc_b = wkv_scan(S0, fc, 1, reverse=True)
    yl_f, _ = wkv_scan(Sc_f, fl, 0, reverse=False)
    yl_b, _ = wkv_scan(Sc_b, fl, 1, reverse=True)
    out_lat = rwkv_readout(yl_f + yl_b, fl, r_k, lnx_w, lnx_b)
    out_ctx = rwkv_readout(yc_f + yc_b, fc, r_k, lnx_w, lnx_b) if need_ctx else None
    return out_ctx, out_lat, vf_ctx, vf_lat


def axial_rope_tables(rows):
    row_id = jnp.repeat(jnp.arange(rows), GRID_W).astype(jnp.float32)
    col_id = jnp.tile(jnp.arange(GRID_W), rows).astype(jnp.float32)
    inv = ROPE_BASE ** (-jnp.arange(ROPE_FREQS, dtype=jnp.float32) / ROPE_FREQS)
    ang = jnp.stack([row_id[:, None] * inv, col_id[:, None] * inv], axis=1)
    return jnp.cos(ang), jnp.sin(ang)


def apply_axial_rope(x, cos, sin):
    B, T, H, _ = x.shape
    xr = x.astype(jnp.float32).reshape(B, T, H, 2, 2, ROPE_FREQS)
    x1, x2 = xr[..., 0, :], xr[..., 1, :]
    c, s = cos[:, None], sin[:, None]
    out = jnp.stack([x1 * c - x2 * s, x2 * c + x1 * s], axis=-2)
    return out.reshape(x.shape).astype(x.dtype)


def _sink_col(sink, shape):
    return jnp.broadcast_to(sink.astype(jnp.float32).reshape(1, ATT_KV_HEADS, ATT_GROUP, 1, 1), shape)


def context_attention(q, k, v, sink):
    s = jnp.einsum("bqhgd,bkhd->bhgqk", q, k).astype(jnp.float32) * ATT_SCALE
    logits = jnp.concatenate([s, _sink_col(sink, s.shape[:-1] + (1,))], axis=-1)
    p = jax.nn.softmax(logits, axis=-1)[..., :-1].astype(v.dtype)
    o = jnp.einsum("bhgqk,bkhd->bqhgd", p, v)
    return o.reshape(o.shape[0], o.shape[1], ATT_Q_DIM)


def windowed_attention(q, k, v, k_ctx, v_ctx, sink):
    B, T = q.shape[:2]
    n_ctx = k_ctx.shape[1]
    kp = jnp.pad(k, ((0, 0), (BLOCK, BLOCK), (0, 0), (0, 0)))
    vp = jnp.pad(v, ((0, 0), (BLOCK, BLOCK), (0, 0), (0, 0)))

    def block(i):
        start = i * BLOCK
        q_b = lax.dynamic_slice_in_dim(q, start, BLOCK, axis=1)
        k_b = lax.dynamic_slice_in_dim(kp, start, 3 * BLOCK, axis=1)
        v_b = lax.dynamic_slice_in_dim(vp, start, 3 * BLOCK, axis=1)
        q_pos = start + jnp.arange(BLOCK)
        k_pos = start - BLOCK + jnp.arange(3 * BLOCK)
        valid = (jnp.abs(q_pos[:, None] - k_pos[None, :]) <= WINDOW) & (k_pos[None, :] >= 0) & (k_pos[None, :] < T)
        s_w = jnp.einsum("bqhgd,bkhd->bhgqk", q_b, k_b).astype(jnp.float32) * ATT_SCALE
        s_w = jnp.where(valid, s_w, NEG_INF)
        s_c = jnp.einsum("bqhgd,bkhd->bhgqk", q_b, k_ctx).astype(jnp.float32) * ATT_SCALE
        logits = jnp.concatenate([s_c, s_w, _sink_col(sink, s_c.shape[:-1] + (1,))], axis=-1)
        p = jax.nn.softmax(logits, axis=-1).astype(v.dtype)
        p_c, p_w = p[..., :n_ctx], p[..., n_ctx:n_ctx + 3 * BLOCK]
        return jnp.einsum("bhgqk,bkhd->bqhgd", p_c, v_ctx) + jnp.einsum("bhgqk,bkhd->bqhgd", p_w, v_b)

    o = lax.map(block, jnp.arange(T // BLOCK))
    return jnp.moveaxis(o, 0, 1).reshape(B, T, ATT_Q_DIM)


def fourier_mix(u):
    B, T, _ = u.shape
    z = u.astype(jnp.float32).reshape(B, T, FOURIER_GROUPS, FOURIER_GROUP_DIM)
    return jnp.fft.fftn(z, axes=(1, 3), norm="ortho").real.reshape(B, T, FOURIER_DIM).astype(u.dtype)


def merge_branches(y_r, y_a, y_f, gate_logits, wb_r, wb_a, wb_f, w_o):
    g_r, g_a, g_f = jnp.split(jax.nn.sigmoid(gate_logits), N_BRANCH, axis=-1)
    return (g_r * (y_r @ wb_r) + g_a * (y_a @ wb_a) + g_f * (y_f @ wb_f)) @ w_o


def conv_ffn(f, up, conv_w, conv_b, down):
    z_gate, z_val = jnp.split(f @ up, 2, axis=-1)
    z_gate = conv_w[0] * _prev(z_gate) + conv_w[1] * z_gate + conv_w[2] * _next(z_gate) + conv_b
    return (jax.nn.gelu(z_gate, approximate=True) * z_val) @ down


def setup_inputs(seed: int = 0) -> dict:
    key = jax.random.key(seed)
    ks = iter(jax.random.split(key, 48))
    D, L = D_MODEL, DEPTH

    def nrm(shape, scale):
        return scale * jax.random.normal(next(ks), shape, jnp.float32)

    def unif(shape, lo, hi):
        return jax.random.uniform(next(ks), shape, jnp.float32, lo, hi)

    return {
        "x": nrm((BATCH, SEQ, D), 1.0),
        "c": nrm((BATCH, D), 1.0),
        "ctx": nrm((BATCH, CTX_LEN, D), 1.0),
        "c_ctx": nrm((D,), 1.0),
        "mod_w": nrm((L, D, 6 * D), 0.5 * D ** -0.5),
        "mod_b": nrm((L, 6 * D), 0.02),
        "norm_mix_pre": 1.0 + nrm((L, D), 0.02),
        "norm_mix_post": 1.0 + nrm((L, D), 0.02),
        "norm_ffn_pre": 1.0 + nrm((L, D), 0.02),
        "norm_ffn_post": 1.0 + nrm((L, D), 0.02),
        "w_in": nrm((L, D, IN_COLS), D ** -0.5),
        "rwkv_mu": unif((L, 2, RWKV_COLS), 0.0, 0.5),
        "rwkv_w0": unif((L, 2, RWKV_DIM), -5.0, 0.0),
        "rwkv_w2": nrm((L, 2, DECAY_LORA, RWKV_DIM), 0.1 * DECAY_LORA ** -0.5),
        "rwkv_a0": nrm((L, 2, RWKV_DIM), 0.1),
        "rwkv_a2": nrm((L, 2, AAA_LORA, RWKV_DIM), 0.5 * AAA_LORA ** -0.5),
        "rwkv_g2": nrm((L, GATE_LORA, RWKV_DIM), GATE_LORA ** -0.5),
        "rwkv_k_k": 0.85 + nrm((L, RWKV_DIM), 0.02),
        "rwkv_k_a": 1.0 + nrm((L, RWKV_DIM), 0.02),
        "rwkv_r_k": nrm((L, RWKV_DIM), 0.1),
        "rwkv_lnx_w": 1.0 + nrm((L, RWKV_DIM), 0.02),
        "rwkv_lnx_b": nrm((L, RWKV_DIM), 0.02),
        "rwkv_v0": nrm((L - 1, RWKV_DIM), 0.1),
        "rwkv_v1": nrm((L - 1, RWKV_DIM, MV_LORA), RWKV_DIM ** -0.5),
        "rwkv_v2": nrm((L - 1, MV_LORA, RWKV_DIM), 0.5 * MV_LORA ** -0.5),
        "attn_sink": nrm((L, ATT_HEADS), 0.5),
        "w_branch_rwkv": nrm((L, RWKV_DIM, D), RWKV_DIM ** -0.5),
        "w_branch_attn": nrm((L, ATT_Q_DIM, D), ATT_Q_DIM ** -0.5),
        "w_branch_fourier": nrm((L, FOURIER_DIM, D), FOURIER_DIM ** -0.5),
        "w_out": nrm((L, D, D), D ** -0.5),
        "ffn_up": nrm((L, D, 2 * D_FF), D ** -0.5),
        "ffn_conv_w": nrm((L, 3, D_FF), 0.5),
        "ffn_conv_b": nrm((L, D_FF), 0.02),
        "ffn_down": nrm((L, D_FF, D), D_FF ** -0.5),
    }


def reference(x, c, ctx, c_ctx, mod_w, mod_b, norm_mix_pre, norm_mix_post, norm_ffn_pre, norm_ffn_post,
              w_in, rwkv_mu, rwkv_w0, rwkv_w2, rwkv_a0, rwkv_a2, rwkv_g2, rwkv_k_k, rwkv_k_a, rwkv_r_k,
              rwkv_lnx_w, rwkv_lnx_b, rwkv_v0, rwkv_v1, rwkv_v2, attn_sink, w_branch_rwkv, w_branch_attn,
              w_branch_fourier, w_out, ffn_up, ffn_conv_w, ffn_conv_b, ffn_down):
    B, T, _ = x.shape
    n_ctx = ctx.shape[1]
    rows = T // GRID_W
    cos, sin = axial_rope_tables(rows)
    s_lat = jax.nn.silu(c)[:, None, :]
    s_ctx = jax.nn.silu(c_ctx)[None, None, :]
    h, hc = x, ctx
    vf_lat = vf_ctx = None
    for l in range(DEPTH):
        need_ctx = l < DEPTH - 1
        m = jnp.split(s_lat @ mod_w[l] + mod_b[l], 6, axis=-1)
        mc = jnp.split(s_ctx @ mod_w[l] + mod_b[l], 6, axis=-1)
        a = modulate(rms_norm(h, norm_mix_pre[l]), m[0], m[1])
        ac = modulate(rms_norm(hc, norm_mix_pre[l]), mc[0], mc[1])
        u_r, u_q, u_k, u_v, u_f, u_g = _split(a @ w_in[l], IN_SIZES)
        c_r, c_q, c_k, c_v, c_f, c_g = _split(ac @ w_in[l], IN_SIZES)
        vres = None if l == 0 else (rwkv_v0[l - 1], rwkv_v1[l - 1], rwkv_v2[l - 1])
        yr_c, yr, vf_ctx, vf_lat = rwkv_branch(
            c_r, u_r, vf_ctx, vf_lat, vres, need_ctx, rwkv_mu[l], rwkv_w0[l], rwkv_w2[l], rwkv_a0[l],
            rwkv_a2[l], rwkv_g2[l], rwkv_k_k[l], rwkv_k_a[l], rwkv_r_k[l], rwkv_lnx_w[l], rwkv_lnx_b[l])
        q = apply_axial_rope(u_q.reshape(B, T, ATT_HEADS, ATT_HEAD), cos, sin)
        q = q.reshape(B, T, ATT_KV_HEADS, ATT_GROUP, ATT_HEAD)
        k = apply_axial_rope(u_k.reshape(B, T, ATT_KV_HEADS, ATT_HEAD), cos, sin)
        v = u_v.reshape(B, T, ATT_KV_HEADS, ATT_HEAD)
        k_c = c_k.reshape(B, n_ctx, ATT_KV_HEADS, ATT_HEAD)
        v_c = c_v.reshape(B, n_ctx, ATT_KV_HEADS, ATT_HEAD)
        ya = windowed_attention(q, k, v, k_c, v_c, attn_sink[l])
        mix = merge_branches(yr, ya, fourier_mix(u_f), u_g, w_branch_rwkv[l], w_branch_attn[l],
                             w_branch_fourier[l], w_out[l])
        h = h + m[2] * rms_norm(mix, norm_mix_post[l])
        if need_ctx:
            ya_c = context_attention(c_q.reshape(B, n_ctx, ATT_KV_HEADS, ATT_GROUP, ATT_HEAD), k_c, v_c, attn_sink[l])
            mix_c = merge_branches(yr_c, ya_c, fourier_mix(c_f), c_g, w_branch_rwkv[l], w_branch_attn[l],
                                   w_branch_fourier[l], w_out[l])
            hc = hc + mc[2] * rms_norm(mix_c, norm_mix_post[l])
        f = modulate(rms_norm(h, norm_ffn_pre[l]), m[3], m[4])
        h = h + m[5] * rms_norm(conv_ffn(f, ffn_up[l], ffn_conv_w[l], ffn_conv_b[l], ffn_down[l]), norm_ffn_post[l])
        if need_ctx:
            fc = modulate(rms_norm(hc, norm_ffn_pre[l]), mc[3], mc[4])
            hc = hc + mc[5] * rms_norm(conv_ffn(fc, ffn_up[l], ffn_conv_w[l], ffn_conv_b[l], ffn_down[l]), norm_ffn_post[l])
    return h
```

```python
import numpy as np
from contextlib import ExitStack
import ml_dtypes
import concourse.bass as bass
import concourse.mybir as mybir
from concourse.bass_utils import run_bass_kernel_spmd

F32 = mybir.dt.float32
BF16 = mybir.dt.bfloat16
AF = mybir.ActivationFunctionType
ALU = mybir.AluOpType

EPOCH = 30000
NDMA = 8
SAME_ENGINE_SYNC = True

D = 1024
T = 2048
NCTX = 256
TT = T + NCTX
DFF = 2816
NSEQ = 4
EPS = 1e-6
LNX_EPS = 64e-5
DEC = 0.6065306597126334
C_R, C_Q, C_K, C_V, C_F, C_G = 0, 1152, 1664, 1792, 1920, 2176

VO = {}
_o = 0
for _n, _c in (("modb", 48), ("g_mpre", 8), ("g_mpost", 8), ("g_fpre", 8), ("g_fpost", 8), ("mu0", 9), ("mu1", 9),
               ("w0f", 2), ("w0b", 2), ("a0f", 2), ("a0b", 2), ("k_k", 2), ("k_a", 2), ("r_k", 2), ("lnx_w", 2),
               ("lnx_b", 2), ("v0", 2), ("cw0", 22), ("cw1", 22), ("cw2", 22), ("cb", 22), ("sink", 8)):
    VO[_n] = (_o, _c)
    _o += _c
NV = _o


class Buf:
    __slots__ = ("name", "w", "rd")

    def __init__(self, name):
        self.name = name
        self.w = None
        self.rd = {}


class Tile:
    __slots__ = ("ap", "bufs")

    def __init__(self, ap, bufs):
        self.ap = ap
        self.bufs = bufs

    def __getitem__(self, idx):
        return Tile(self.ap[idx], self.bufs)

    def v(self, ap):
        return Tile(ap, self.bufs)

    def bc(self, shape):
        return Tile(self.ap.to_broadcast(list(shape)), self.bufs)


class Prog:
    def __init__(self, nc, dbg=()):
        self.nc = nc
        self.es = ExitStack()
        self.eng = {"pe": nc.tensor, "act": nc.scalar, "dve": nc.vector, "pool": nc.gpsimd, "sp": nc.sync}
        self.cnt = {e: 0 for e in self.eng}
        self.esem = {e: [] for e in self.eng}
        self.wait_e = {e: {} for e in self.eng}
        self.wait_d = {e: {} for e in self.eng}
        self.dsem = {}
        self.duse = {}
        self.dnext = {e: 0 for e in self.eng}
        self.nbuf = 0
        self.psum_i = 0
        self.ninstr = 0
        self.dbg = set(dbg)
        self.dumps = {}

    def sem(self, name):
        return self.es.enter_context(self.nc.semaphore(name))

    def sb(self, name, shape, dtype, es=None):
        self.nbuf += 1
        t = (es or self.es).enter_context(self.nc.sbuf_tensor(f"{name}_s{self.nbuf}", list(shape), dtype))
        return Tile(t[:], [Buf(name)])

    def dram(self, name, shape, dtype, kind="Internal"):
        t = self.nc.dram_tensor(name, list(shape), dtype, kind=kind)
        return Tile(t.ap(), [Buf(name)])

    def sub(self, tile, idx):
        self.nbuf += 1
        return Tile(tile.ap[idx], [Buf(f"sub{self.nbuf}")])

    def init_psum(self):
        t = self.es.enter_context(self.nc.psum_tensor("psall", [128, 8, 512], F32))
        self.ps_all = t[:]
        self.ps_bufs = [Buf(f"psb{i}") for i in range(8)]

    def psum(self):
        i = self.psum_i % 8
        self.psum_i += 1
        return Tile(self.ps_all[:, i, :], [self.ps_bufs[i]])

    def psum_at(self, i, n=1):
        if n == 1:
            return Tile(self.ps_all[:, i, :], [self.ps_bufs[i]])
        return Tile(self.ps_all[:, i:i + n, :], self.ps_bufs[i:i + n])

    def psum2(self):
        if self.psum_i % 2:
            self.psum_i += 1
        i = self.psum_i % 8
        self.psum_i += 2
        return Tile(self.ps_all[:, i:i + 2, :], [self.ps_bufs[i], self.ps_bufs[i + 1]])

    def dump(self, name, tile, shape, dtype=F32):
        if name not in self.dbg:
            return
        k = f"dbg_{name}_{len(self.dumps)}"
        d = self.dram(k, shape, dtype, kind="ExternalOutput")
        self.dumps[k] = name
        self.dma(d, tile, q="sp")

    def _esem(self, e, epoch):
        while len(self.esem[e]) <= epoch:
            self.esem[e].append(self.sem(f"s_{e}_{len(self.esem[e])}"))
        return self.esem[e][epoch]

    def _wait(self, f, ev):
        if ev is None:
            return
        if ev[0] == "e":
            _, e, idx = ev
            if e == f and (not SAME_ENGINE_SYNC or f == "pe"):
                return
            if self.wait_e[f].get(e, 0) >= idx:
                return
            self.wait_e[f][e] = idx
            self.eng[f].wait_ge(self._esem(e, (idx - 1) // EPOCH), (idx - 1) % EPOCH + 1)
        else:
            _, key, val = ev
            if self.wait_d[f].get(key, 0) >= val:
                return
            self.wait_d[f][key] = val
            self.eng[f].wait_ge(self.dsem[key], val)

    def _deps(self, f, r, w):
        for t in r:
            for b in t.bufs:
                self._wait(f, b.w)
        for t in w:
            for b in t.bufs:
                self._wait(f, b.w)
                for ev in list(b.rd.values()):
                    self._wait(f, ev)

    def _commit(self, ev, key, r, w):
        for t in r:
            for b in t.bufs:
                b.rd[key] = ev
        for t in w:
            for b in t.bufs:
                b.w = ev
                b.rd = {}

    def op(self, e, fn, r=(), w=()):
        self._deps(e, r, w)
        ins = fn(self.eng[e])
        self.cnt[e] += 1
        idx = self.cnt[e]
        ins.then_inc(self._esem(e, (idx - 1) // EPOCH), 1)
        self._commit(("e", e, idx), e, r, w)
        self.ninstr += 1
        return ins

    def dma(self, out, in_, q="sp", **kw):
        j = self.dnext[q] % (2 if q == 'pool' else NDMA)
        self.dnext[q] += 1
        key = (q, j)
        if key not in self.dsem:
            self.dsem[key] = self.sem(f"d_{q}_{j}")
            self.duse[key] = 0
        if self.duse[key] > 0:
            self._wait(q, ("d", key, 16 * self.duse[key]))
        self._deps(q, [in_], [out])
        self.duse[key] += 1
        val = 16 * self.duse[key]
        self.eng[q].dma_start(out=out.ap, in_=in_.ap, **kw).then_inc(self.dsem[key], 16)
        self._commit(("d", key, val), ("d", q, j, val), [in_], [out])
        self.ninstr += 1

    def barrier(self):
        for f in self.eng:
            for e in self.eng:
                if e != f and self.cnt[e] > 0:
                    self._wait(f, ("e", e, self.cnt[e]))
            for key, n in self.duse.items():
                if n > 0:
                    self._wait(f, ("d", key, 16 * n))

    def finish(self):
        self.barrier()
        self.es.close()

    def mm(self, out, lhsT, rhs, start=True, stop=True):
        return self.op("pe", lambda e: e.matmul(out.ap, lhsT.ap, rhs.ap, start=start, stop=stop),
                       r=[lhsT, rhs], w=[out])

    def tr(self, out, in_, ident):
        return self.op("pe", lambda e: e.transpose(out.ap, in_.ap, ident.ap), r=[in_, ident], w=[out])

    def act(self, out, in_, func, bias=None, scale=1.0, e="act"):
        r = [in_]
        kw = {}
        if bias is not None:
            if isinstance(bias, Tile):
                r.append(bias)
                kw["bias"] = bias.ap
            else:
                kw["bias"] = bias
        if isinstance(scale, Tile):
            r.append(scale)
            kw["scale"] = scale.ap
        else:
            kw["scale"] = scale
        return self.op(e, lambda en: en.activation(out.ap, in_.ap, func, **kw), r=r, w=[out])

    def tt(self, out, a, b, op, e="dve"):
        return self.op(e, lambda en: en.tensor_tensor(out.ap, a.ap, b.ap, op), r=[a, b], w=[out])

    def ts(self, out, a, s1, s2=None, op0=ALU.mult, op1=None, e="dve"):
        r = [a]
        if isinstance(s1, Tile):
            r.append(s1)
            s1 = s1.ap
        if isinstance(s2, Tile):
            r.append(s2)
            s2 = s2.ap
        if op1 is None:
            return self.op(e, lambda en: en.tensor_scalar(out.ap, a.ap, s1, None, op0), r=r, w=[out])
        return self.op(e, lambda en: en.tensor_scalar(out.ap, a.ap, s1, s2, op0, op1), r=r, w=[out])

    def stt(self, out, in0, scalar, in1, op0, op1, e="dve"):
        r = [in0, in1]
        if isinstance(scalar, Tile):
            r.append(scalar)
            scalar = scalar.ap
        return self.op(e, lambda en: en.scalar_tensor_tensor(out.ap, in0.ap, scalar, in1.ap, op0, op1),
                       r=r, w=[out])

    def rsqrt(self, out, in_, addc):
        self.act(out, in_, AF.Sqrt, bias=addc)
        return self.op("dve", lambda en: en.reciprocal(out.ap, out.ap), r=[out], w=[out])

    def copy(self, out, in_, e="dve"):
        if e == "act":
            return self.op(e, lambda en: en.activation(out.ap, in_.ap, AF.Identity), r=[in_], w=[out])
        return self.op(e, lambda en: en.tensor_copy(out.ap, in_.ap), r=[in_], w=[out])

    def memset(self, out, val, e="dve"):
        return self.op(e, lambda en: en.memset(out.ap, val), r=[], w=[out])


def host_consts():
    c = {}
    c["ident"] = np.eye(128, dtype=np.float32)
    idb = np.zeros((128, 64), np.float32)
    idb[np.arange(128), np.arange(128) % 64] = 1.0
    c["idb"] = idb
    bo = np.zeros((128, 128), np.float32)
    bo[:64, :64] = 1.0
    bo[64:, 64:] = 1.0
    c["blockones"] = bo
    pm = np.zeros((128, 128), np.float32)
    pm[np.arange(128), np.arange(128) ^ 16] = 1.0
    c["pm"] = pm
    p = np.arange(128)
    axis = (p % 64) // 32
    half = (p % 32) // 16
    f = p % 16
    inv = (10000.0 ** (-np.arange(16, dtype=np.float32) / 16)).astype(np.float32)
    t = np.arange(T)
    pos = np.where(axis[:, None] == 0, (t // 64)[None, :], (t % 64)[None, :]).astype(np.float32)
    ang = pos * inv[f][:, None]
    c["ropec"] = np.cos(ang).astype(np.float32)
    c["ropes"] = (np.sin(ang) * np.where(half[:, None] == 0, -1.0, 1.0)).astype(np.float32)
    kk = np.arange(128)[:, None]
    qq = np.arange(128)[None, :]
    am = np.ones((128, 5, 128), np.float32)
    am[:, 2, :] = (kk >= qq)
    am[:, 4, :] = (kk <= qq)
    c["amask"] = am.astype(ml_dtypes.bfloat16)
    i = np.arange(64)[:, None]
    j = np.arange(64)[None, :]
    SL = (j < i).astype(np.float32)
    SU = (j > i).astype(np.float32)
    UI = (j >= i).astype(np.float32)
    LI = (j <= i).astype(np.float32)
    one = np.ones((64, 64), np.float32)
    m2 = np.stack([np.concatenate([one, SL, SL, SU], 1), np.concatenate([one, SU, SU, SL], 1)], 0)
    c["rmask2"] = np.ascontiguousarray(m2.transpose(1, 0, 2))
    m5 = np.stack([np.concatenate([UI, one], 1), np.concatenate([LI, one], 1)], 0)
    c["rmask5"] = np.ascontiguousarray(m5.transpose(1, 0, 2))
    mr = np.ones((2, 128, 128), np.float32)
    mr[0, 64:, :64] = UI
    mr[1, 64:, :64] = LI
    c["rmaskr"] = np.ascontiguousarray(mr.transpose(1, 0, 2))
    rst = np.ones((128, 512), np.float32)
    rst[:, ::64] = 0.0
    c["rst"] = rst
    def dft(n, scale):
        a = 2 * np.pi * (np.outer(np.arange(n), np.arange(n)) % n) / n
        return (np.cos(a) * scale), (-np.sin(a) * scale)
    ct, nst = dft(T, T ** -0.5)
    def blk(a):
        return np.ascontiguousarray(a.reshape(16, 128, 8, 256).transpose(2, 1, 0, 3)).astype(ml_dtypes.bfloat16)
    c["dft_ct"] = blk(ct)
    c["dft_nst"] = blk(nst)
    cc, ncs = dft(NCTX, NCTX ** -0.5)
    c["dft_cc"] = cc.astype(ml_dtypes.bfloat16)
    c["dft_ncs"] = ncs.astype(ml_dtypes.bfloat16)
    c64, ns64 = dft(64, 0.125)
    bdc = np.zeros((256, 256))
    bds = np.zeros((256, 256))
    for g in range(4):
        bdc[g * 64:(g + 1) * 64, g * 64:(g + 1) * 64] = c64
        bds[g * 64:(g + 1) * 64, g * 64:(g + 1) * 64] = -ns64
    c["bdc"] = bdc.astype(ml_dtypes.bfloat16)
    c["bds"] = bds.astype(ml_dtypes.bfloat16)
    return c


CONST_DT = {"amask": BF16, "dft_ct": BF16, "dft_nst": BF16, "dft_cc": BF16, "dft_ncs": BF16, "bdc": BF16, "bds": BF16}

WEIGHT_SHAPES = {
    "mod_w": [2, 1024, 6144], "w_in": [2, 1024, 5248], "rwkv_w2": [2, 2, 64, 256], "rwkv_a2": [2, 2, 64, 256],
    "rwkv_g2": [2, 128, 256], "rwkv_v1": [1, 256, 32], "rwkv_v2": [1, 32, 256], "w_branch_rwkv": [2, 256, 1024],
    "w_branch_attn": [2, 512, 1024], "w_branch_fourier": [2, 256, 1024], "w_out": [2, 1024, 1024],
    "ffn_up": [2, 1024, 5632], "ffn_down": [2, 2816, 1024],
}


HC = None


def build_program(dbg=(), nseq=NSEQ, nlayers=2, stages=("rwkv", "att", "fou", "merge", "ffn")):
    global HC
    if HC is None:
        HC = host_consts()
    nc = bass.Bass("TRN2", target_bir_lowering=False)
    P = Prog(nc, dbg)
    P.init_psum()

    def ext(name, shape, dt=F32):
        return Tile(nc.dram_tensor(name, list(shape), dt, kind="ExternalInput").ap(), [Buf(name)])

    x_d = ext("x", [NSEQ, T, D])
    ctx_d = ext("ctx", [NSEQ, NCTX, D])
    cs_d = ext("cs", [128, 8, 5])
    vecs_d = ext("vecs", [128, 2, NV])
    W = {k: ext(k, s) for k, s in WEIGHT_SHAPES.items()}
    C = {k: ext(k, list(v.shape), CONST_DT.get(k, F32)) for k, v in HC.items()}
    y_d = Tile(nc.dram_tensor("y", [NSEQ, T, D], F32, kind="ExternalOutput").ap(), [Buf("y")])
    hbuf = [P.dram("hT0", [128, 8, TT], F32), P.dram("hT1", [128, 8, TT], F32)]

    def ldc(name, shape, dt=F32, src=None, q="sp"):
        t = P.sb(name, shape, dt)
        P.dma(t, src if src is not None else C[name], q=q)
        return t

    ident = ldc("ident", [128, 128])
    idb = ldc("idb", [128, 64])
    blockones = ldc("blockones", [128, 128])
    amask = ldc("amask", [128, 5, 128], BF16)
    rmask2 = ldc("rmask2", [64, 2, 256])
    rmask5 = ldc("rmask5", [64, 2, 128])
    rmaskr = ldc("rmaskr", [128, 2, 128])
    rst = ldc("rst", [128, 512])
    bdc = ldc("bdc", [128, 2, 256], BF16, C["bdc"].v(C["bdc"].ap.rearrange("(kc p) n -> p kc n", p=128)))
    bds = ldc("bds", [128, 2, 256], BF16, C["bds"].v(C["bds"].ap.rearrange("(kc p) n -> p kc n", p=128)))
    dcc = ldc("dcc", [128, 2, 256], BF16, C["dft_cc"].v(C["dft_cc"].ap.rearrange("(kc p) n -> p kc n", p=128)))
    dncs = ldc("dncs", [128, 2, 256], BF16, C["dft_ncs"].v(C["dft_ncs"].ap.rearrange("(kc p) n -> p kc n", p=128)))
    vecs = ldc("vecs_sb", [128, 2, NV], F32, vecs_d)
    ones_bf = P.sb("ones_bf", [128, 128], BF16)
    P.memset(ones_bf, 1.0)
    ident_bf = P.sb("ident_bf", [128, 128], BF16)
    P.copy(ident_bf, ident)
    mall = P.sb("mall", [128, 2, 5, 48], F32)
    vfT = P.sb("vfT", [128, 2, TT], BF16)

    def V(l, name):
        o, c = VO[name]
        return vecs[:, l, o:o + c]

    def ring(name, shape, dt, n, es):
        tiles = [P.sb(f"{name}{i}", shape, dt, es) for i in range(n)]
        st = {"i": 0}

        def nxt():
            t = tiles[st["i"] % n]
            st["i"] += 1
            return t
        return nxt

    with ExitStack() as es:
        cs = P.sb("cs_sb", [128, 8, 5], F32, es)
        P.dma(cs, cs_d)
        sg = P.sb("cs_sg", [128, 8, 5], F32, es)
        P.act(sg, cs, AF.Sigmoid)
        scs = P.sb("scs", [128, 8, 5], F32, es)
        P.tt(scs, cs, sg, ALU.mult)
        mw = ring("mw", [128, 8, 768], F32, 2, es)
        for l in range(nlayers):
            src = W["mod_w"][l]
            srcv = src.v(src.ap.rearrange("(kc p) n -> p kc n", p=128))
            for t8 in range(8):
                mwt = mw()
                P.dma(mwt, srcv[:, :, t8 * 768:(t8 + 1) * 768])
                for o6 in range(6):
                    oc = t8 * 6 + o6
                    ps = P.psum()
                    for kc in range(8):
                        P.mm(ps[:, 0:5], mwt[:, kc, o6 * 128:(o6 + 1) * 128], scs[:, kc, :], start=(kc == 0), stop=(kc == 7))
                    mb = V(l, "modb")
                    P.ts(mall[:, l, :, oc], ps[:, 0:5], mb[:, oc:oc + 1], None, op0=ALU.add)
        P.barrier()
    P.dump("mall", mall, [128, 2, 5, 48])

    def norm_stats(h, n, es_tiles):
        sq, rs = es_tiles
        sqt = sq()
        P.act(sqt[:, :, 0:n], h, AF.Square)
        ps = P.psum()
        for c in range(8):
            P.mm(ps[:, 0:n], ones_bf, sqt[:, c, 0:n], start=(c == 0), stop=(c == 7))
        rst_ = rs()
        P.rsqrt(rst_[:, 0:n], ps[:, 0:n], 1024 * EPS)
        return rst_

    def bc3(t2, n, k):
        return Tile(t2.ap.unsqueeze(1).to_broadcast([t2.ap.shape[0], k, n]), t2.bufs)

    def gains(name, l, j, gname, scale_i, es, shift_i=None):
        gs = P.sb(name, [128, 8], F32, es)
        g = V(l, gname)
        msc = mall[:, l, j, scale_i * 8:(scale_i + 1) * 8]
        if shift_i is not None:
            P.ts(gs, msc, 1.0, 32.0, op0=ALU.add, op1=ALU.mult)
        else:
            P.ts(gs, msc, 32.0, None, op0=ALU.mult)
        P.tt(gs, gs, g, ALU.mult)
        sh = mall[:, l, j, shift_i * 8:(shift_i + 1) * 8] if shift_i is not None else None
        return gs, sh

    LAT_BLKS = [(NCTX + 512 * i, 512) for i in range(4)]
    ALL_BLKS = [(0, NCTX)] + LAT_BLKS

    def load_seq(s, cur):
        with ExitStack() as es:
            xt = ring("xt", [128, 1024], F32, 2, es)
            stg = ring("stg", [128, 8, 512], F32, 2, es)
            for (t0, n) in ALL_BLKS:
                st = stg()
                for tt in range(n // 128):
                    x = xt()
                    if t0 == 0:
                        P.dma(x, ctx_d[s, tt * 128:(tt + 1) * 128, :])
                    else:
                        r0 = t0 - NCTX + tt * 128
                        P.dma(x, x_d[s, r0:r0 + 128, :])
                    for half in range(2):
                        ps = P.psum()
                        for c4 in range(4):
                            c = half * 4 + c4
                            P.tr(ps[:, c4 * 128:(c4 + 1) * 128], x[:, c * 128:(c + 1) * 128], ident)
                        P.copy(st[:, half * 4:(half + 1) * 4, tt * 128:(tt + 1) * 128],
                               ps.v(ps.ap.rearrange("p (a b) -> p a b", b=128)), e=("act" if half else "dve"))
                P.dma(hbuf[cur][:, :, t0:t0 + n], st[:, :, 0:n], q="pool")
            P.barrier()

    def store_seq(s, cur):
        with ExitStack() as es:
            ht = ring("ht", [128, 8, 128], F32, 2, es)
            ot = ring("ot", [128, 1024], F32, 2, es)
            for tt in range(T // 128):
                h = ht()
                P.dma(h, hbuf[cur][:, :, NCTX + tt * 128:NCTX + (tt + 1) * 128])
                o = ot()
                for half in range(2):
                    ps = P.psum()
                    for c4 in range(4):
                        P.tr(ps[:, c4 * 128:(c4 + 1) * 128], h[:, half * 4 + c4, :], ident)
                    P.copy(o[:, half * 512:(half + 1) * 512], ps, e=("act" if half else "dve"))
                P.dma(y_d[s, tt * 128:(tt + 1) * 128, :], o, q="pool")
            P.barrier()

    def stage_a(l, s, cur, aT, es0):
        with ExitStack() as es:
            gsl, shl = gains("gsl", l, s, "g_mpre", 1, es, 0)
            gsc, shc = gains("gsc", l, 4, "g_mpre", 1, es, 0)
            hb = ring("hb", [128, 8, 512], F32, 2, es)
            sq = ring("sq", [128, 8, 512], BF16, 1, es)
            rs = ring("rs", [128, 512], F32, 2, es)
            tmp = ring("tmpa", [128, 8, 512], F32, 1, es)
            for (t0, n) in ALL_BLKS:
                h = hb()
                P.dma(h[:, :, 0:n], hbuf[cur][:, :, t0:t0 + n])
                r = norm_stats(h[:, :, 0:n], n, (sq, rs))
                tm = tmp()
                P.tt(tm[:, :, 0:n], h[:, :, 0:n], bc3(r[:, 0:n], n, 8), ALU.mult)
                gs, sh = (gsc, shc) if t0 == 0 else (gsl, shl)
                for c in range(8):
                    P.act(aT[:, c, t0:t0 + n], tm[:, c, 0:n], AF.Identity, bias=sh[:, c:c + 1], scale=gs[:, c:c + 1])
            P.barrier()

    def stage_ffn(l, s, cur):
        src, dst = hbuf[cur], hbuf[1 - cur]
        nblk_ctx = 1 if l < nlayers - 1 or nlayers == 1 and False else 0
        with ExitStack() as es:
            up = P.sb("up_sb", [128, 8, 2 * DFF], BF16, es)
            upsrc = W["ffn_up"][l]
            upv = upsrc.v(upsrc.ap.rearrange("(kc p) n -> p kc n", p=128))
            for q4 in range(4):
                P.dma(up[:, :, q4 * 1408:(q4 + 1) * 1408], upv[:, :, q4 * 1408:(q4 + 1) * 1408], q="pool")
            dnr = ring("dn_sb", [128, 22, 128], BF16, 2, es)
            dsrc = W["ffn_down"][l]
            dnv = dsrc.v(dsrc.ap.rearrange("(kc p) n -> p kc n", p=128))
            gains_l = gains("fgsl", l, s, "g_fpre", 4, es, 3)
            gains_c = gains("fgsc", l, 4, "g_fpre", 4, es, 3)
            gpost_l, _ = gains("fgpl", l, s, "g_fpost", 5, es)
            gpost_c, _ = gains("fgpc", l, 4, "g_fpost", 5, es)
            cw0, cw1, cw2, cb = V(l, "cw0"), V(l, "cw1"), V(l, "cw2"), V(l, "cb")
            hb = ring("fhb", [128, 8, 258], F32, 1, es)
            sq = ring("fsq", [128, 8, 258], BF16, 1, es)
            rs = ring("frs", [128, 258], F32, 2, es)
            tmp = ring("ftmp", [128, 258], F32, 2, es)
            fT = ring("fT", [128, 8, 258], BF16, 1, es)
            zg = ring("zg", [128, 258], F32, 2, es)
            cz = ring("cz", [128, 256], F32, 2, es)
            gz = ring("gz", [128, 256], F32, 2, es)
            hid = ring("hid", [128, 22, 256], BF16, 1, es)
            mo = ring("fmo", [128, 8, 256], F32, 1, es)
            blks = [(256 * bi, 256) for bi in range(TT // 256)]
            for (t0, n) in blks:
                isctx = (t0 == 0)
                if isctx and l == nlayers - 1:
                    continue
                seq_lo, seq_hi = (0, NCTX) if isctx else (NCTX, TT)
                lo, hi = max(t0 - 1, seq_lo), min(t0 + n + 1, seq_hi)
                c0 = lo - (t0 - 1)
                ncol = hi - lo
                h = hb()
                P.dma(h[:, :, c0:c0 + ncol], src[:, :, lo:hi])
                hv = h[:, :, c0:c0 + ncol]
                sqt = sq()
                P.act(sqt[:, :, 0:ncol], hv, AF.Square)
                ps = P.psum()
                for c in range(8):
                    P.mm(ps[:, 0:ncol], ones_bf, sqt[:, c, 0:ncol], start=(c == 0), stop=(c == 7))
                r = rs()
                P.rsqrt(r[:, 0:ncol], ps[:, 0:ncol], 1024 * EPS)
                gs, sh = gains_c if isctx else gains_l
                f = fT()
                for c in range(8):
                    tm = tmp()
                    P.stt(tm[:, 0:ncol], h[:, c, c0:c0 + ncol], gs[:, c:c + 1], r[:, 0:ncol], ALU.mult, ALU.mult)
                    P.act(f[:, c, c0:c0 + ncol], tm[:, 0:ncol], AF.Identity, bias=sh[:, c:c + 1])
                hd = hid()
                for j in range(22):
                    psg = P.psum()
                    for kc in range(8):
                        P.mm(psg[:, 0:ncol], up[:, kc, j * 128:(j + 1) * 128], f[:, kc, c0:c0 + ncol], start=(kc == 0), stop=(kc == 7))
                    psv = P.psum()
                    for kc in range(8):
                        P.mm(psv[:, 0:n], up[:, kc, DFF + j * 128:DFF + (j + 1) * 128], f[:, kc, 1:1 + n], start=(kc == 0), stop=(kc == 7))
                    z = zg()
                    if c0 == 1:
                        P.memset(z[:, 0:1], 0.0)
                    if c0 + ncol < n + 2:
                        P.memset(z[:, n + 1:n + 2], 0.0)
                    P.copy(z[:, c0:c0 + ncol], psg[:, 0:ncol], e="act")
                    c_ = cz()
                    P.ts(c_, z[:, 1:1 + n], cw1[:, j:j + 1], cb[:, j:j + 1], op0=ALU.mult, op1=ALU.add)
                    P.stt(c_, z[:, 0:n], cw0[:, j:j + 1], c_, ALU.mult, ALU.add)
                    P.stt(c_, z[:, 2:2 + n], cw2[:, j:j + 1], c_, ALU.mult, ALU.add)
                    g_ = gz()
                    P.act(g_, c_, AF.Gelu_apprx_tanh)
                    P.tt(hd[:, j, :], g_, psv[:, 0:n], ALU.mult)
                m = mo()
                for oc in range(8):
                    dn = dnr()
                    P.dma(dn, dnv[:, :, oc * 128:(oc + 1) * 128], q="pool")
                    ps = P.psum()
                    for j in range(22):
                        P.mm(ps[:, 0:n], dn[:, j, :], hd[:, j, :], start=(j == 0), stop=(j == 21))
                    P.copy(m[:, oc, :], ps[:, 0:n], e="act")
                s2 = sq()
                P.act(s2[:, :, 0:n], m, AF.Square)
                ps = P.psum()
                for c in range(8):
                    P.mm(ps[:, 0:n], ones_bf, s2[:, c, 0:n], start=(c == 0), stop=(c == 7))
                r2 = rs()
                P.rsqrt(r2[:, 0:n], ps[:, 0:n], 1024 * EPS)
                P.tt(m, m, bc3(r2[:, 0:n], n, 8), ALU.mult)
                gp = gpost_c if isctx else gpost_l
                for c in range(8):
                    P.stt(m[:, c, :], m[:, c, :], gp[:, c:c + 1], h[:, c, 1:1 + n], ALU.mult, ALU.add)
                P.dma(dst[:, :, t0:t0 + n], m, q="pool")
            P.barrier()

    B = dict(P=P, W=W, C=C, V=V, ring=ring, mall=mall, hbuf=hbuf, vfT=vfT, ident=ident, ident_bf=ident_bf, idb=idb,
             blockones=blockones, amask=amask, rmask2=rmask2, rmask5=rmask5, rmaskr=rmaskr, rst=rst, bdc=bdc, bds=bds,
             dcc=dcc, dncs=dncs, ones_bf=ones_bf, bc3=bc3, gains=gains, norm_stats=norm_stats, nlayers=nlayers,
             LAT_BLKS=LAT_BLKS, ALL_BLKS=ALL_BLKS)

    for s in range(nseq):
        cur = 0
        load_seq(s, cur)
        for l in range(nlayers):
            with ExitStack() as esl:
                aT = P.sb("aT", [128, 8, TT], BF16, esl)
                stage_a(l, s, cur, aT, esl)
                if s == 0:
                    P.dump(f"aT{l}", aT, [128, 8, TT], BF16)
                yrT = P.sb("yrT", [128, 2, TT], BF16, esl)
                if "rwkv" in stages:
                    stage_rwkv(B, l, s, aT, yrT)
                    if s == 0:
                        P.dump(f"yrT{l}", yrT, [128, 2, TT], BF16)
                yaT = P.sb("yaT", [128, 4, TT], BF16, esl)
                yfT = P.sb("yfT", [128, 2, TT], BF16, esl)
                if "att" in stages:
                    stage_att(B, l, s, aT, yaT)
                    if s == 0:
                        P.dump(f"yaT{l}", yaT, [128, 4, TT], BF16)
                if "fou" in stages:
                    stage_fou(B, l, s, aT, yfT)
                    if s == 0:
                        P.dump(f"yfT{l}", yfT, [128, 2, TT], BF16)
                if "merge" in stages:
                    stage_merge(B, l, s, cur, aT, yrT, yaT, yfT)
                P.barrier()
            if s == 0:
                P.dump(f"hmix{l}", hbuf[cur], [128, 8, TT])
            if "ffn" in stages:
                stage_ffn(l, s, cur)
                cur = 1 - cur
            if s == 0:
                P.dump(f"hffn{l}", hbuf[cur], [128, 8, TT])
        store_seq(s, cur)
    P.finish()
    return nc, P


def _wsrc(B, l):
    src = B["W"]["w_in"][l]
    return src.v(src.ap.rearrange("(kc p) n -> p kc n", p=128))


def stage_fou(B, l, s, aT, yfT):
    P, ring, C = B["P"], B["ring"], B["C"]
    need_ctx = l < B["nlayers"] - 1
    bdc, bds, dcc, dncs = B["bdc"], B["bds"], B["dcc"], B["dncs"]
    with ExitStack() as es:
        wf = P.sb("wf", [128, 8, 256], BF16, es)
        P.dma(wf, _wsrc(B, l)[:, :, C_F:C_F + 256], q="pool")
        ufT = P.sb("ufT", [128, 2, TT], BF16, es)
        blks = B["ALL_BLKS"] if need_ctx else B["LAT_BLKS"]
        import os
        if os.environ.get("FOU_CUT") == "2":
            P.barrier()
            return
        for (t0, n) in blks:
            for j in range(2):
                ps = P.psum()
                for kc in range(8):
                    P.mm(ps[:, 0:n], wf[:, kc, j * 128:(j + 1) * 128], aT[:, kc, t0:t0 + n], start=(kc == 0), stop=(kc == 7))
                P.copy(ufT[:, j, t0:t0 + n], ps[:, 0:n], e=("act" if j else "dve"))
        if os.environ.get("FOU_CUT") == "3":
            P.barrier()
            return
        Zc = P.sb("Zc", [128, 18, 256], BF16, es)
        Zs = P.sb("Zs", [128, 18, 256], BF16, es)
        for tt in range(0 if need_ctx else 2, int(os.environ.get("FOU_NT", "18"))):
            ps = P.psum()
            for kc in range(2):
                P.mm(ps[:, 0:256], ufT[:, kc, tt * 128:(tt + 1) * 128], bdc[:, kc, :], start=(kc == 0), stop=(kc == 1))
            psb = P.psum()
            for kc in range(2):
                P.mm(psb[:, 0:256], ufT[:, kc, tt * 128:(tt + 1) * 128], bds[:, kc, :], start=(kc == 0), stop=(kc == 1))
            P.copy(Zc[:, tt, :], ps[:, 0:256], e="act")
            P.copy(Zs[:, tt, :], psb[:, 0:256], e="dve")
        import os
        if os.environ.get("FOU_CUT") == "1":
            P.barrier()
            return
        dct = ring("dct", [128, 16, 256], BF16, 2, es)
        dnst = ring("dnst", [128, 16, 256], BF16, 2, es)
        for nb in range(8):
            ct = dct()
            P.dma(ct, C["dft_ct"][nb])
            st = dnst()
            P.dma(st, C["dft_nst"][nb])
            for ch in range(2):
                ps = P.psum()
                for tt in range(16):
                    P.mm(ps[:, 0:256], Zc[:, 2 + tt, ch * 128:(ch + 1) * 128], ct[:, tt, :], start=(tt == 0), stop=False)
                for tt in range(16):
                    P.mm(ps[:, 0:256], Zs[:, 2 + tt, ch * 128:(ch + 1) * 128], st[:, tt, :], start=False, stop=(tt == 15))
                P.copy(yfT[:, ch, NCTX + nb * 256:NCTX + (nb + 1) * 256], ps[:, 0:256], e=("act" if ch else "dve"))
        if need_ctx:
            for ch in range(2):
                ps = P.psum()
                for tt in range(2):
                    P.mm(ps[:, 0:256], Zc[:, tt, ch * 128:(ch + 1) * 128], dcc[:, tt, :], start=(tt == 0), stop=False)
                for tt in range(2):
                    P.mm(ps[:, 0:256], Zs[:, tt, ch * 128:(ch + 1) * 128], dncs[:, tt, :], start=False, stop=(tt == 1))
                P.copy(yfT[:, ch, 0:256], ps[:, 0:256], e=("act" if ch else "dve"))
        P.barrier()


def stage_merge(B, l, s, cur, aT, yrT, yaT, yfT):
    P, ring, W = B["P"], B["ring"], B["W"]
    need_ctx = l < B["nlayers"] - 1
    blks = B["ALL_BLKS"] if need_ctx else B["LAT_BLKS"]
    hb_d = B["hbuf"][cur]
    with ExitStack() as es0:
        mixpre = P.sb("mixpre", [128, 8, TT], BF16, es0)
        with ExitStack() as es:
            wb = P.sb("wb", [128, 8, 1024], BF16, es)
            for nm, k0, nk in (("w_branch_rwkv", 0, 2), ("w_branch_attn", 2, 4), ("w_branch_fourier", 6, 2)):
                src = W[nm][l]
                P.dma(wb[:, k0:k0 + nk, :], src.v(src.ap.rearrange("(kc p) n -> p kc n", p=128)), q="pool")
            wgr = ring("wg", [128, 8, 3, 128], BF16, 2, es)
            sgr = ring("sgm", [128, 512], F32, 3, es)
            accr = ring("accm", [128, 512], F32, 2, es)
            srcv = _wsrc(B, l)
            for oc in range(8):
                wg = wgr()
                for br in range(3):
                    c0 = C_G + br * 1024 + oc * 128
                    P.dma(wg[:, :, br, :], srcv[:, :, c0:c0 + 128], q="pool")
                for (t0, n) in blks:
                    acc = accr()
                    for br, (yt, k0, nk) in enumerate(((yrT, 0, 2), (yaT, 2, 4), (yfT, 6, 2))):
                        psg = P.psum()
                        for kc in range(8):
                            P.mm(psg[:, 0:n], wg[:, kc, br, :], aT[:, kc, t0:t0 + n], start=(kc == 0), stop=(kc == 7))
                        sg = sgr()
                        P.act(sg[:, 0:n], psg[:, 0:n], AF.Sigmoid)
                        psp = P.psum()
                        for kc in range(nk):
                            P.mm(psp[:, 0:n], wb[:, k0 + kc, oc * 128:(oc + 1) * 128], yt[:, kc, t0:t0 + n],
                                 start=(kc == 0), stop=(kc == nk - 1))
                        if br == 0:
                            P.tt(acc[:, 0:n], sg[:, 0:n], psp[:, 0:n], ALU.mult)
                        else:
                            P.tt(sg[:, 0:n], sg[:, 0:n], psp[:, 0:n], ALU.mult)
                            if br == 1:
                                P.tt(acc[:, 0:n], acc[:, 0:n], sg[:, 0:n], ALU.add)
                            else:
                                P.tt(mixpre[:, oc, t0:t0 + n], acc[:, 0:n], sg[:, 0:n], ALU.add)
            P.barrier()
        with ExitStack() as es:
            wo = P.sb("wo", [128, 8, 1024], BF16, es)
            src = W["w_out"][l]
            P.dma(wo, src.v(src.ap.rearrange("(kc p) n -> p kc n", p=128)), q="pool")
            gm_l, _ = B["gains"]("gml", l, s, "g_mpost", 2, es)
            gm_c, _ = B["gains"]("gmc", l, 4, "g_mpost", 2, es)
            mo = ring("mmo", [128, 8, 512], F32, 1, es)
            hb = ring("mhb", [128, 8, 512], F32, 1, es)
            sq = ring("msq", [128, 8, 512], BF16, 1, es)
            rs = ring("mrs", [128, 512], F32, 2, es)
            for (t0, n) in blks:
                h = hb()
                P.dma(h[:, :, 0:n], hb_d[:, :, t0:t0 + n])
                m = mo()
                for oc in range(8):
                    ps = P.psum()
                    for kc in range(8):
                        P.mm(ps[:, 0:n], wo[:, kc, oc * 128:(oc + 1) * 128], mixpre[:, kc, t0:t0 + n], start=(kc == 0), stop=(kc == 7))
                    P.copy(m[:, oc, 0:n], ps[:, 0:n], e=("act" if oc % 2 else "dve"))
                r = B["norm_stats"](m[:, :, 0:n], n, (sq, rs))
                P.tt(m[:, :, 0:n], m[:, :, 0:n], B["bc3"](r[:, 0:n], n, 8), ALU.mult)
                gm = gm_c if t0 == 0 else gm_l
                for c in range(8):
                    P.stt(h[:, c, 0:n], m[:, c, 0:n], gm[:, c:c + 1], h[:, c, 0:n], ALU.mult, ALU.add)
                P.dma(hb_d[:, :, t0:t0 + n], h[:, :, 0:n], q="pool")
            P.barrier()


def stage_att(B, l, s, aT, yaT):
    P, ring, C, V = B["P"], B["ring"], B["C"], B["V"]
    need_ctx = l < B["nlayers"] - 1
    amask, ident_bf = B["amask"], B["ident_bf"]
    with ExitStack() as es:
        ropec = P.sb("ropec", [128, T], F32, es)
        P.dma(ropec, C["ropec"])
        ropes = P.sb("ropes", [128, T], F32, es)
        P.dma(ropes, C["ropes"])
        pm = P.sb("pm", [128, 128], F32, es)
        P.dma(pm, C["pm"])
        srcv = _wsrc(B, l)
        wq = P.sb("wq", [128, 8, 512], BF16, es)
        P.dma(wq, srcv[:, :, C_Q:C_Q + 512], q="pool")
        wkd = P.sb("wkd", [128, 8, 2, 128], BF16, es)
        for kv in range(2):
            for dup in range(2):
                P.dma(wkd[:, :, kv, dup * 64:(dup + 1) * 64], srcv[:, :, C_K + kv * 64:C_K + (kv + 1) * 64], q="pool")
        wv = P.sb("wv", [128, 8, 128], BF16, es)
        P.dma(wv, srcv[:, :, C_V:C_V + 128], q="pool")
        qrT = P.sb("qrT", [128, 4, TT], BF16, es)
        kdT = P.sb("kdT", [128, 2, TT], BF16, es)
        Vaug = P.sb("Vaug", [128, 18, 2, 65], BF16, es)
        P.memset(Vaug[:, :, :, 64:65], 1.0)
        esink = P.sb("esink", [128, 8], F32, es)
        P.act(esink, V(l, "sink"), AF.Exp)
        qsb = ring("qsb", [128, 512], F32, 2, es)
        t1 = ring("ropet1", [128, 512], F32, 2, es)
        t2 = ring("ropet2", [128, 512], F32, 2, es)
        for (t0, n) in B["ALL_BLKS"]:
            isctx = (t0 == 0)
            for m in range(6):
                if isctx and m < 4 and not need_ctx:
                    continue
                ps = P.psum()
                for kc in range(8):
                    lhsT = wq[:, kc, m * 128:(m + 1) * 128] if m < 4 else wkd[:, kc, m - 4, :]
                    P.mm(ps[:, 0:n], lhsT, aT[:, kc, t0:t0 + n], start=(kc == 0), stop=(kc == 7))
                dst = qrT[:, m, t0:t0 + n] if m < 4 else kdT[:, m - 4, t0:t0 + n]
                if isctx:
                    P.copy(dst, ps[:, 0:n], e="act")
                else:
                    q = qsb()
                    P.copy(q[:, 0:n], ps[:, 0:n], e="act")
                    ps2 = P.psum()
                    P.mm(ps2[:, 0:n], pm, q[:, 0:n])
                    p0 = t0 - NCTX
                    a = t1()
                    P.tt(a[:, 0:n], q[:, 0:n], ropec[:, p0:p0 + n], ALU.mult)
                    b = t2()
                    P.tt(b[:, 0:n], ps2[:, 0:n], ropes[:, p0:p0 + n], ALU.mult)
                    P.tt(dst, a[:, 0:n], b[:, 0:n], ALU.add)
            for tt in range(n // 128):
                ti = t0 // 128 + tt
                ps = P.psum()
                for kc in range(8):
                    P.mm(ps[:, 0:128], aT[:, kc, t0 + tt * 128:t0 + (tt + 1) * 128], wv[:, kc, :], start=(kc == 0), stop=(kc == 7))
                P.copy(Vaug[:, ti, :, 0:64], ps.v(ps.ap[:, 0:128].rearrange("p (a b) -> p a b", b=64)), e="dve")
        if s == 0:
            P.dump(f"qrT{l}", qrT, [128, 4, TT], BF16)
            P.dump(f"kdT{l}", kdT, [128, 2, TT], BF16)
            P.dump(f"Vaug{l}", Vaug, [128, 18, 2, 65], BF16)
        PTr = ring("PT", [128, 5, 128], BF16, 3, es)
        yat = ring("yatok", [128, 8, 64], BF16, 2, es)
        den = ring("den", [128, 8], F32, 2, es)
        qblocks = ([("c", 0), ("c", 1)] if need_ctx else []) + [("l", i) for i in range(16)]
        cnt = {"pss": 0, "pso": 0, "pst": 0}
        for kind, i in qblocks:
            q0 = i * 128 if kind == "c" else NCTX + i * 128
            slots = [(0, 0), (1, 1)]
            if kind == "l":
                if i > 0:
                    slots.append((2, 2 + i - 1))
                slots.append((3, 2 + i))
                if i < 15:
                    slots.append((4, 2 + i + 1))
            ya = yat()
            dn = den()
            for kvg in range(2):
                pso = P.psum_at(4 + cnt["pso"] % 2)
                cnt["pso"] += 1
                pso_v = pso.v(pso.ap[:, 0:260].rearrange("p (h d) -> p h d", d=65))
                for hh in range(4):
                    h = kvg * 4 + hh
                    m = h // 2
                    po = 64 * (h % 2)
                    pss = P.psum_at(2 * (cnt["pss"] % 2), 2)
                    cnt["pss"] += 1
                    pss_v = pss.v(pss.ap.rearrange("p a (b n) -> p (a b) n", n=128))
                    for (sl, kt) in slots:
                        P.mm(pss_v[:, sl, :], kdT[po:po + 64, kvg, kt * 128:(kt + 1) * 128], qrT[po:po + 64, m, q0:q0 + 128])
                    pt = PTr()
                    P.act(pt, pss_v[:, 0:5, :], AF.Exp, scale=0.125)
                    if kind == "l":
                        P.tt(pt, pt, amask, ALU.mult, e="pool")
                    for si, (sl, kt) in enumerate(slots):
                        P.mm(pso_v[:, hh, :], pt[:, sl, :], Vaug[:, kt, kvg, :], start=(si == 0), stop=(si == len(slots) - 1))
                dn4 = dn[:, kvg * 4:(kvg + 1) * 4]
                P.tt(dn4, pso_v[:, :, 64], esink[:, kvg * 4:(kvg + 1) * 4], ALU.add)
                P.op("dve", lambda en: en.reciprocal(dn4.ap, dn4.ap), r=[dn4], w=[dn4])
                P.tt(ya[:, kvg * 4:(kvg + 1) * 4, :], pso_v[:, :, 0:64],
                     Tile(dn4.ap.unsqueeze(2).to_broadcast([128, 4, 64]), dn4.bufs), ALU.mult)
            pst = P.psum_at(6 + cnt["pst"] % 2)
            cnt["pst"] += 1
            pst_bf = pst.v(pst.ap.bitcast(BF16))
            ya_f = ya.v(ya.ap.rearrange("p h d -> p (h d)"))
            for m in range(4):
                P.tr(pst_bf[:, m * 128:(m + 1) * 128], ya_f[:, m * 128:(m + 1) * 128], ident_bf)
            P.copy(yaT[:, :, q0:q0 + 128], pst_bf.v(pst_bf.ap[:, 0:512].rearrange("p (a b) -> p a b", b=128)), e="act")
        P.barrier()


NLOCK = 2


def stage_rwkv(B, l, s, aT, yrT):
    P, ring, W, V = B["P"], B["ring"], B["W"], B["V"]
    nlayers = B["nlayers"]
    need_ctx = l < nlayers - 1
    idb, blockones, rst, vfT = B["idb"], B["blockones"], B["rst"], B["vfT"]
    rmask2, rmask5, rmaskr = B["rmask2"], B["rmask5"], B["rmaskr"]
    with ExitStack() as es:
        w_r = P.sb("w_r", [128, 8, 1152], BF16, es)
        srcv = _wsrc(B, l)
        P.dma(w_r[:, :, 0:576], srcv[:, :, 0:576], q="pool")
        P.dma(w_r[:, :, 576:1152], srcv[:, :, 576:1152], q="pool")
        w2sb = P.sb("w2sb", [128, 256], F32, es)
        P.dma(w2sb, W["rwkv_w2"][l].v(W["rwkv_w2"][l].ap.rearrange("d k n -> (d k) n")))
        a2sb = P.sb("a2sb", [128, 256], F32, es)
        P.dma(a2sb, W["rwkv_a2"][l].v(W["rwkv_a2"][l].ap.rearrange("d k n -> (d k) n")))
        g2sb = P.sb("g2sb", [128, 256], F32, es)
        P.dma(g2sb, W["rwkv_g2"][l])
        if l > 0:
            v1sb = P.sb("v1sb", [128, 2, 32], F32, es)
            P.dma(v1sb, W["rwkv_v1"][l - 1].v(W["rwkv_v1"][l - 1].ap.rearrange("(g p) m -> p g m", p=128)))
            v2sb = P.sb("v2sb", [32, 256], F32, es)
            P.dma(v2sb, W["rwkv_v2"][l - 1])
        mu0, mu1 = V(l, "mu0"), V(l, "mu1")
        k_k, k_a, r_k, lnx_w, lnx_b, v0 = V(l, "k_k"), V(l, "k_a"), V(l, "r_k"), V(l, "lnx_w"), V(l, "lnx_b"), V(l, "v0")
        w0 = [V(l, "w0f"), V(l, "w0b")]
        a0 = [V(l, "a0f"), V(l, "a0b")]
        omka = P.sb("omka", [128, 2], F32, es)
        P.ts(omka, k_a, -1.0, 1.0, op0=ALU.mult, op1=ALU.add)
        kah = P.sb("kah", [128, 2], F32, es)
        P.ts(kah, k_a, 0.5, None, op0=ALU.mult)
        yfwd = P.sb("yfwd", [128, 2, TT], BF16, es)
        u_blk = P.sb("u_blk", [128, 9, 258], F32, es)
        us = P.sb("us", [128, 9, 256], F32, es)
        dtm = ring("dtm", [128, 256], F32, 2, es)
        A = {}
        for nm in ("vv", "kap", "sg", "aa", "aaf", "key", "bb", "cs", "cs2", "e1", "e2", "eex", "e3", "rt", "kt", "bh",
                   "kh", "bck", "kck", "tmp1", "tmp2"):
            A[nm] = P.sb("r_" + nm, [128, 2, 256], F32, es)
        th = P.sb("r_th", [128, 256], F32, es)
        sgx = P.sb("r_sgx", [128, 256], F32, es)
        vv1 = P.sb("r_vv1", [32, 256], F32, es)
        etot = P.sb("r_etot", [128, 2, 4], F32, es)
        tot8 = P.sb("r_tot8", [128, 2, 4], F32, es)
        XA = [P.sb(f"XA{i}", [64, 4, 256], F32, es) for i in range(NLOCK)]
        Wb = [P.sb(f"Wb{i}", [64, 4, 128], F32, es) for i in range(NLOCK)]
        Pw = [P.sb(f"Pw{i}", [64, 4, 128], F32, es) for i in range(NLOCK)]
        RB = [P.sb(f"RB{i}", [64, 4, 128], F32, es) for i in range(NLOCK)]
        Rt = [P.sb(f"Rt{i}", [128, 4, 128], F32, es) for i in range(NLOCK)]
        Dg = [P.sb(f"Dg{i}", [128, 2, 64], F32, es) for i in range(NLOCK)]
        Zr = []
        for i in range(8):
            z = P.sb(f"Zr{i}", [128, 4, 64], F32, es)
            zs = Tile(z.ap[0:64], [Buf(f"Zs{i}")])
            zv = Tile(z.ap[64:128], [Buf(f"Zv{i}")])
            Zr.append((Tile(z.ap, zs.bufs + zv.bufs), zs, zv))
        pa = {"i": 0}

        def psA():
            pa["i"] += 1
            return P.psum_at(pa["i"] % 2)

        def fm(arr, h, c):
            po = 64 * (h % 2)
            return arr[po:po + 64, h // 2, c * 64:(c + 1) * 64]

        def v4(ps, n):
            return ps.v(ps.ap.rearrange("p (h n) -> p h n", n=n))

        def prep(bi, d, full):
            t0 = 256 * bi
            seq_lo, seq_hi = (0, NCTX) if bi == 0 else (NCTX, TT)
            lo, hi = max(t0 - 1, seq_lo), min(t0 + 257, seq_hi)
            c0 = lo - (t0 - 1)
            ncol = hi - lo
            if c0 == 1:
                P.memset(u_blk[:, :, 0:1], 0.0)
            if c0 + ncol < 258:
                P.memset(u_blk[:, :, 257:258], 0.0)
            for j in range(9):
                ps = psA()
                for kc in range(8):
                    P.mm(ps[:, 0:ncol], w_r[:, kc, j * 128:(j + 1) * 128], aT[:, kc, lo:hi], start=(kc == 0), stop=(kc == 7))
                P.copy(u_blk[:, j, c0:c0 + ncol], ps[:, 0:ncol], e=("act" if j % 2 else "dve"))
            for j in range(9):
                d0 = dtm()
                P.tt(d0, u_blk[:, j, 0:256], u_blk[:, j, 1:257], ALU.subtract)
                P.stt(us[:, j, :], d0, mu0[:, j:j + 1], u_blk[:, j, 1:257], ALU.mult, ALU.add)
                d1 = dtm()
                P.tt(d1, u_blk[:, j, 2:258], u_blk[:, j, 1:257], ALU.subtract)
                P.stt(us[:, j, :], d1, mu1[:, j:j + 1], us[:, j, :], ALU.mult, ALU.add)
            r_, k_, v_ = us[:, 0:2, :], us[:, 2:4, :], us[:, 4:6, :]
            vv = A["vv"]
            if l == 0:
                P.copy(vv, v_, e="act")
                if d == 0:
                    P.copy(vfT[:, :, t0:t0 + 256], v_, e="act")
            else:
                ps = psA()
                for g in range(2):
                    P.mm(ps[0:32, 0:256], v1sb[:, g, :], us[:, 4 + g, :], start=(g == 0), stop=(g == 1))
                P.copy(vv1, ps[0:32, 0:256], e="act")
                for g in range(2):
                    ps = psA()
                    P.mm(ps[:, 0:256], v2sb[:, g * 128:(g + 1) * 128], vv1)
                    P.act(A["tmp1"][:, g, :], ps[:, 0:256], AF.Sigmoid, bias=v0[:, g:g + 1])
                P.tt(A["tmp2"], vfT[:, :, t0:t0 + 256], v_, ALU.subtract)
                P.tt(A["tmp2"], A["tmp2"], A["tmp1"], ALU.mult)
                P.tt(vv, v_, A["tmp2"], ALU.add)
            kap = A["kap"]
            for g in range(2):
                P.ts(kap[:, g, :], k_[:, g, :], k_k[:, g:g + 1], None, op0=ALU.mult)
            P.tt(A["tmp1"], kap, kap, ALU.mult)
            for g in range(2):
                ps = psA()
                P.mm(ps[:, 0:256], blockones, A["tmp1"][:, g, :])
                P.ts(A["tmp2"][:, g, :], ps[:, 0:256], 1e-24, None, op0=ALU.max)
            P.rsqrt(A["tmp2"], A["tmp2"], 0.0)
            P.tt(kap, kap, A["tmp2"], ALU.mult)
            P.act(th, us[:, 6, :], AF.Tanh)
            sg, aa = A["sg"], A["aa"]
            for g in range(2):
                ps = psA()
                P.mm(ps[:, 0:256], w2sb[64 * d:64 * d + 64, g * 128:(g + 1) * 128], th[64 * d:64 * d + 64, :])
                P.act(sg[:, g, :], ps[:, 0:256], AF.Sigmoid, bias=w0[d][:, g:g + 1])
            for g in range(2):
                ps = psA()
                P.mm(ps[:, 0:256], a2sb[64 * d:64 * d + 64, g * 128:(g + 1) * 128], us[64 * d:64 * d + 64, 7, :])
                P.act(aa[:, g, :], ps[:, 0:256], AF.Sigmoid, bias=a0[d][:, g:g + 1])
            if full:
                for g in range(2):
                    ps = psA()
                    P.mm(ps[:, 0:256], a2sb[0:64, g * 128:(g + 1) * 128], us[0:64, 7, :])
                    P.act(A["aaf"][:, g, :], ps[:, 0:256], AF.Sigmoid, bias=a0[0][:, g:g + 1])
            key, bb = A["key"], A["bb"]
            for g in range(2):
                P.ts(A["tmp1"][:, g, :], aa[:, g, :], k_a[:, g:g + 1], omka[:, g:g + 1], op0=ALU.mult, op1=ALU.add)
            P.tt(key, k_, A["tmp1"], ALU.mult)
            P.tt(bb, kap, aa, ALU.mult)
            cs = A["cs"]
            csf = cs.v(cs.ap.rearrange("p g t -> p (g t)"))
            sgf = sg.v(sg.ap.rearrange("p g t -> p (g t)"))
            P.op("dve", lambda en: en.tensor_tensor_scan(csf.ap, rst.ap, sgf.ap, 0.0, ALU.mult, ALU.add), r=[rst, sg], w=[cs])
            cs4 = cs.v(cs.ap.rearrange("p g (c t) -> p g c t", t=64))
            P.copy(tot8, cs4[:, :, :, 63], e="dve")
            totb = Tile(tot8.ap.unsqueeze(3).to_broadcast([128, 2, 4, 64]), tot8.bufs)
            if d == 1:
                c2 = A["cs2"]
                c24 = c2.v(c2.ap.rearrange("p g (c t) -> p g c t", t=64))
                P.tt(c24, totb, cs4, ALU.subtract)
                P.tt(c2, c2, sg, ALU.add)
                cs = c2
                cs4 = c24
            P.act(etot, tot8, AF.Exp, scale=-DEC)
            P.act(A["e1"], cs, AF.Exp, scale=-DEC)
            P.act(A["e2"], cs, AF.Exp, scale=DEC)
            P.tt(A["tmp1"], cs, sg, ALU.subtract)
            P.act(A["eex"], A["tmp1"], AF.Exp, scale=-DEC)
            t24 = A["tmp2"].v(A["tmp2"].ap.rearrange("p g (c t) -> p g c t", t=64))
            P.tt(t24, totb, cs4, ALU.subtract)
            P.act(A["e3"], A["tmp2"], AF.Exp, scale=-DEC)
            P.tt(A["rt"], r_, A["e1"], ALU.mult)
            P.tt(A["kt"], kap, A["eex"], ALU.mult)
            P.tt(A["bh"], bb, A["e2"], ALU.mult)
            P.tt(A["kh"], key, A["e2"], ALU.mult)
            P.tt(A["bck"], bb, A["e3"], ALU.mult)
            P.tt(A["kck"], key, A["e3"], ALU.mult)

        def hs(h):
            return (h % 2) * 2 + h // 2

        def pair(ps, n):
            return ps.v(ps.ap[:, :, 0:2 * n].rearrange("p a (b n) -> p a b n", n=n))

        def sv(t, n0=None, n1=None):
            ap = t.ap if n0 is None else t.ap[:, :, n0:n1]
            return t.v(ap.rearrange("p (a b) n -> p a b n", a=2))

        def bc4(m, d, p, n):
            return Tile(m.ap[:, d, :].unsqueeze(1).unsqueeze(1).to_broadcast([p, 2, 2, n]), m.bufs)

        def machinery(sl, c, d, zk):
            rt, kt, bh, kh, bck, kck, vv = A["rt"], A["kt"], A["bh"], A["kh"], A["bck"], A["kck"], A["vv"]
            b0 = 2 + 2 * sl
            ps = P.psum_at(b0, 2)
            pv = pair(ps, 256)
            for h in range(4):
                po, a, b = 64 * (h % 2), h % 2, h // 2
                P.mm(pv[0:64, a, b, 0:64], fm(kt, h, c), idb[po:po + 64, :])
                P.mm(pv[0:64, a, b, 64:128], fm(kt, h, c), fm(kh, h, c))
                P.mm(pv[0:64, a, b, 128:192], fm(kt, h, c), fm(bh, h, c))
                P.mm(pv[0:64, a, b, 192:256], fm(bh, h, c), fm(kt, h, c))
            P.tt(sv(XA[sl]), pv[0:64], bc4(rmask2, d, 64, 256), ALU.mult)
            ps = P.psum_at(b0)
            pv = v4(ps, 128)
            for h in range(4):
                P.mm(pv[0:64, hs(h), :], XA[sl][:, hs(h), 192:256], XA[sl][:, hs(h), 0:128])
            P.tt(Wb[sl], XA[sl][:, :, 0:128], pv[0:64], ALU.subtract)
            for lev in range(5):
                ps = P.psum_at(b0 + 1)
                pv = v4(ps, 128)
                for h in range(4):
                    q = hs(h)
                    if lev == 0:
                        Ah, ATh = XA[sl][:, q, 128:192], XA[sl][:, q, 192:256]
                    else:
                        Ah, ATh = Pw[sl][:, q, 0:64], Pw[sl][:, q, 64:128]
                    P.mm(pv[0:64, q, 0:64], ATh, Ah)
                    P.mm(pv[0:64, q, 64:128], Ah, ATh)
                P.copy(Pw[sl], pv[0:64], e="act")
                ps = P.psum_at(b0)
                pv = v4(ps, 128)
                for h in range(4):
                    q = hs(h)
                    P.mm(pv[0:64, q, :], Pw[sl][:, q, 64:128], Wb[sl][:, q, :])
                P.tt(Wb[sl], Wb[sl], pv[0:64], ALU.add)
            ps = P.psum_at(b0, 2)
            pv = pair(ps, 128)
            for h in range(4):
                po, a, b = 64 * (h % 2), h % 2, h // 2
                P.mm(pv[0:64, a, b, 0:64], fm(bh, h, c), fm(rt, h, c))
                P.mm(pv[0:64, a, b, 64:128], fm(bck, h, c), idb[po:po + 64, :])
            P.tt(sv(RB[sl]), pv[0:64], bc4(rmask5, d, 64, 128), ALU.mult)
            for g in range(2):
                P.ts(Dg[sl][:, g, :], idb, etot[:, g, c:c + 1], None, op0=ALU.mult)
            ps = P.psum_at(b0, 2)
            pv = pair(ps, 128)
            for h in range(4):
                po, a, b = 64 * (h % 2), h % 2, h // 2
                P.mm(pv[0:64, a, b, 0:64], idb[po:po + 64, :], fm(rt, h, c))
                P.mm(pv[0:64, a, b, 64:128], idb[po:po + 64, :], Dg[sl][po:po + 64, h // 2, :])
                P.mm(pv[64:128, a, b, 0:64], fm(kh, h, c), fm(rt, h, c))
                P.mm(pv[64:128, a, b, 64:128], fm(kck, h, c), idb[po:po + 64, :])
            P.tt(sv(Rt[sl]), pv, bc4(rmaskr, d, 128, 128), ALU.mult)
            ps = P.psum_at(b0 + 1)
            pv = v4(ps, 128)
            for h in range(4):
                q = hs(h)
                P.mm(pv[:, q, :], Wb[sl][:, q, :], RB[sl][:, q, :])
            P.tt(Rt[sl], Rt[sl], pv, ALU.subtract)
            ps = P.psum_at(b0, 2)
            pv = pair(ps, 64)
            for h in range(4):
                po, a, b = 64 * (h % 2), h % 2, h // 2
                P.mm(pv[64:128, a, b, :], fm(vv, h, c), idb[po:po + 64, :])
            zv = Zr[zk % 8][2]
            P.copy(zv.v(zv.ap.rearrange("p (a b) n -> p a b n", a=2)), pv[64:128], e="act")

        def sequential(sl, c, d, zk, t0, emit_y, ysum):
            Z = Zr[zk % 8][0]
            if emit_y:
                ps = P.psum_at(6)
                pv = ps.v(ps.ap[:, 0:128].rearrange("p (g n) -> p g n", n=64))
                for h in range(4):
                    po = 64 * (h % 2)
                    P.mm(pv[po:po + 64, h // 2, :], Z[:, hs(h), :], Rt[sl][:, hs(h), 0:64])
                tk = t0 + c * 64
                if d == 0:
                    P.copy(yfwd[:, :, tk:tk + 64], pv, e="act")
                else:
                    P.tt(ysum[:, :, c * 64:(c + 1) * 64], pv, yfwd[:, :, tk:tk + 64], ALU.add)
            ps = P.psum_at(7)
            pv = ps.v(ps.ap[:, 0:256].rearrange("p (h n) -> p h n", n=64))
            for h in range(4):
                P.mm(pv[0:64, hs(h), :], Rt[sl][:, hs(h), 64:128], Z[:, hs(h), :])
            P.copy(Zr[(zk + 1) % 8][1], pv[0:64], e="dve")

        def readout(bi):
            t0 = 256 * bi
            r_, k_ = us[:, 0:2, :], us[:, 2:4, :]
            ysum, yc, sq, rstd = A["e1"], A["e2"], A["eex"], A["e3"]
            for g in range(2):
                ps = psA()
                P.mm(ps[:, 0:256], blockones, ysum[:, g, :])
                P.stt(yc[:, g, :], ps[:, 0:256], -1.0 / 64, ysum[:, g, :], ALU.mult, ALU.add)
            P.tt(sq, yc, yc, ALU.mult)
            for g in range(2):
                ps = psA()
                P.mm(ps[:, 0:256], blockones, sq[:, g, :])
                P.ts(rstd[:, g, :], ps[:, 0:256], 1.0 / 64, None, op0=ALU.mult)
            P.rsqrt(rstd, rstd, LNX_EPS)
            P.tt(yc, yc, rstd, ALU.mult)
            for g in range(2):
                P.ts(yc[:, g, :], yc[:, g, :], lnx_w[:, g:g + 1], lnx_b[:, g:g + 1], op0=ALU.mult, op1=ALU.add)
            tk = A["tmp1"]
            P.tt(tk, A["aa"], A["aaf"], ALU.add)
            for g in range(2):
                P.ts(tk[:, g, :], tk[:, g, :], kah[:, g:g + 1], omka[:, g:g + 1], op0=ALU.mult, op1=ALU.add)
            rk = A["tmp2"]
            P.tt(rk, r_, k_, ALU.mult)
            P.tt(rk, rk, tk, ALU.mult)
            for g in range(2):
                P.ts(rk[:, g, :], rk[:, g, :], r_k[:, g:g + 1], None, op0=ALU.mult)
            for g in range(2):
                ps = psA()
                P.mm(ps[:, 0:256], blockones, rk[:, g, :])
                P.tt(sq[:, g, :], ps[:, 0:256], A["vv"][:, g, :], ALU.mult)
            P.tt(yc, yc, sq, ALU.add)
            P.act(sgx, us[:, 8, :], AF.Sigmoid)
            for g in range(2):
                ps = psA()
                P.mm(ps[:, 0:256], g2sb[:, g * 128:(g + 1) * 128], sgx)
                P.tt(yrT[:, g, t0:t0 + 256], yc[:, g, :], ps[:, 0:256], ALU.mult)

        nblk = TT // 256
        import os
        RWC = int(os.environ.get("RW_CUT", "99"))
        if RWC < 99:
            prep(0, 0, False)
            if RWC >= 2:
                machinery(0, 0, 0, 0)
            if RWC >= 3:
                sequential(0, 0, 0, 0, 0, True, A["e1"])
            if RWC >= 4:
                prep(1, 1, True)
                machinery(0, 0, 1, 0)
                sequential(0, 0, 1, 0, 256, True, A["e1"])
                readout(1)
            P.barrier()
            return
        for d in range(2):
            order = list(range(nblk)) if d == 0 else [0] + list(range(nblk - 1, 0, -1))
            zk = 0
            P.memset(Zr[0][1], 0.0)
            for bi in order:
                emit_y = need_ctx or bi > 0
                prep(bi, d, full=(d == 1))
                chunks = [0, 1, 2, 3] if d == 0 else [3, 2, 1, 0]
                for g0 in range(0, 4, NLOCK):
                    grp = chunks[g0:g0 + NLOCK]
                    for sl, c in enumerate(grp):
                        machinery(sl, c, d, zk + sl)
                    for sl, c in enumerate(grp):
                        sequential(sl, c, d, zk + sl, 256 * bi, emit_y, A["e1"])
                    zk += len(grp)
                if d == 1 and emit_y:
                    readout(bi)
        P.barrier()


def pack_vecs(inp):
    v = np.zeros((128, 2, NV), np.float32)

    def put(l, name, arr):
        o, c = VO[name]
        a = np.asarray(arr, np.float32).reshape(c, 128).T
        v[:, l, o:o + c] = a

    for l in range(2):
        put(l, "modb", inp["mod_b"][l])
        put(l, "g_mpre", inp["norm_mix_pre"][l])
        put(l, "g_mpost", inp["norm_mix_post"][l])
        put(l, "g_fpre", inp["norm_ffn_pre"][l])
        put(l, "g_fpost", inp["norm_ffn_post"][l])
        put(l, "mu0", inp["rwkv_mu"][l, 0])
        put(l, "mu1", inp["rwkv_mu"][l, 1])
        put(l, "w0f", inp["rwkv_w0"][l, 0])
        put(l, "w0b", inp["rwkv_w0"][l, 1])
        put(l, "a0f", inp["rwkv_a0"][l, 0])
        put(l, "a0b", inp["rwkv_a0"][l, 1])
        put(l, "k_k", inp["rwkv_k_k"][l])
        put(l, "k_a", inp["rwkv_k_a"][l])
        put(l, "r_k", inp["rwkv_r_k"][l])
        put(l, "lnx_w", inp["rwkv_lnx_w"][l])
        put(l, "lnx_b", inp["rwkv_lnx_b"][l])
        if l > 0:
            put(l, "v0", inp["rwkv_v0"][l - 1])
        put(l, "cw0", inp["ffn_conv_w"][l, 0])
        put(l, "cw1", inp["ffn_conv_w"][l, 1])
        put(l, "cw2", inp["ffn_conv_w"][l, 2])
        put(l, "cb", inp["ffn_conv_b"][l])
        o, c = VO["sink"]
        v[:, l, o:o + c] = np.broadcast_to(np.asarray(inp["attn_sink"][l], np.float32)[None, :], (128, 8))
    return v


def make_in_maps(inp, ncores=8):
    global HC
    if HC is None:
        HC = host_consts()
    vecs = pack_vecs(inp)
    shared = {k: np.ascontiguousarray(np.asarray(inp[k], np.float32)) for k in WEIGHT_SHAPES}
    shared.update(HC)
    shared["vecs"] = vecs
    maps = []
    c = np.asarray(inp["c"], np.float32)
    cc = np.asarray(inp["c_ctx"], np.float32)
    for ci in range(ncores):
        b0 = ci * NSEQ
        cs = np.zeros((128, 8, 5), np.float32)
        for j in range(NSEQ):
            cs[:, :, j] = c[b0 + j].reshape(8, 128).T
        cs[:, :, 4] = cc.reshape(8, 128).T
        m = dict(shared)
        m["x"] = np.ascontiguousarray(np.asarray(inp["x"][b0:b0 + NSEQ], np.float32))
        m["ctx"] = np.ascontiguousarray(np.asarray(inp["ctx"][b0:b0 + NSEQ], np.float32))
        m["cs"] = cs
        maps.append(m)
    return maps


DEFAULT_STAGES = ("rwkv", "att", "fou", "merge", "ffn")


def kernel(**inputs):
    nc, P = build_program(stages=DEFAULT_STAGES)
    maps = make_in_maps(inputs)
    res = run_bass_kernel_spmd(nc, maps, core_ids=list(range(8)))
    return np.concatenate([np.asarray(r["y"], np.float32) for r in res.results], axis=0)
```

```python
import numpy as np
from contextlib import ExitStack
import ml_dtypes
import concourse.bass as bass
import concourse.mybir as mybir
from concourse.bass_utils import run_bass_kernel_spmd

F32 = mybir.dt.float32
BF16 = mybir.dt.bfloat16
AF = mybir.ActivationFunctionType
ALU = mybir.AluOpType

EPOCH = 30000
NDMA = 8
SAME_ENGINE_SYNC = True

D = 1024
T = 2048
NCTX = 256
TT = T + NCTX
DFF = 2816
NSEQ = 4
EPS = 1e-6
LNX_EPS = 64e-5
DEC = 0.6065306597126334
C_R, C_Q, C_K, C_V, C_F, C_G = 0, 1152, 1664, 1792, 1920, 2176

VO = {}
_o = 0
for _n, _c in (("modb", 48), ("g_mpre", 8), ("g_mpost", 8), ("g_fpre", 8), ("g_fpost", 8), ("mu0", 9), ("mu1", 9),
               ("w0f", 2), ("w0b", 2), ("a0f", 2), ("a0b", 2), ("k_k", 2), ("k_a", 2), ("r_k", 2), ("lnx_w", 2),
               ("lnx_b", 2), ("v0", 2), ("cw0", 22), ("cw1", 22), ("cw2", 22), ("cb", 22), ("sink", 8)):
    VO[_n] = (_o, _c)
    _o += _c
NV = _o


class Buf:
    __slots__ = ("name", "w", "rd")

    def __init__(self, name):
        self.name = name
        self.w = None
        self.rd = {}


class Tile:
    __slots__ = ("ap", "bufs")

    def __init__(self, ap, bufs):
        self.ap = ap
        self.bufs = bufs

    def __getitem__(self, idx):
        return Tile(self.ap[idx], self.bufs)

    def v(self, ap):
        return Tile(ap, self.bufs)

    def bc(self, shape):
        return Tile(self.ap.to_broadcast(list(shape)), self.bufs)


class Prog:
    def __init__(self, nc, dbg=()):
        self.nc = nc
        self.es = ExitStack()
        self.eng = {"pe": nc.tensor, "act": nc.scalar, "dve": nc.vector, "pool": nc.gpsimd, "sp": nc.sync}
        self.cnt = {e: 0 for e in self.eng}
        self.esem = {e: [] for e in self.eng}
        self.wait_e = {e: {} for e in self.eng}
        self.wait_d = {e: {} for e in self.eng}
        self.dsem = {}
        self.duse = {}
        self.dnext = {e: 0 for e in self.eng}
        self.nbuf = 0
        self.psum_i = 0
        self.ninstr = 0
        self.dbg = set(dbg)
        self.dumps = {}

    def sem(self, name):
        return self.es.enter_context(self.nc.semaphore(name))

    def sb(self, name, shape, dtype, es=None):
        self.nbuf += 1
        t = (es or self.es).enter_context(self.nc.sbuf_tensor(f"{name}_s{self.nbuf}", list(shape), dtype))
        return Tile(t[:], [Buf(name)])

    def dram(self, name, shape, dtype, kind="Internal"):
        t = self.nc.dram_tensor(name, list(shape), dtype, kind=kind)
        return Tile(t.ap(), [Buf(name)])

    def sub(self, tile, idx):
        self.nbuf += 1
        return Tile(tile.ap[idx], [Buf(f"sub{self.nbuf}")])

    def init_psum(self):
        t = self.es.enter_context(self.nc.psum_tensor("psall", [128, 8, 512], F32))
        self.ps_all = t[:]
        self.ps_bufs = [Buf(f"psb{i}") for i in range(8)]

    def psum(self):
        i = self.psum_i % 8
        self.psum_i += 1
        return Tile(self.ps_all[:, i, :], [self.ps_bufs[i]])

    def psum_at(self, i, n=1):
        if n == 1:
            return Tile(self.ps_all[:, i, :], [self.ps_bufs[i]])
        return Tile(self.ps_all[:, i:i + n, :], self.ps_bufs[i:i + n])

    def psum2(self):
        if self.psum_i % 2:
            self.psum_i += 1
        i = self.psum_i % 8
        self.psum_i += 2
        return Tile(self.ps_all[:, i:i + 2, :], [self.ps_bufs[i], self.ps_bufs[i + 1]])

    def dump(self, name, tile, shape, dtype=F32):
        if name not in self.dbg:
            return
        k = f"dbg_{name}_{len(self.dumps)}"
        d = self.dram(k, shape, dtype, kind="ExternalOutput")
        self.dumps[k] = name
        self.dma(d, tile, q="sp")

    def _esem(self, e, epoch):
        while len(self.esem[e]) <= epoch:
            self.esem[e].append(self.sem(f"s_{e}_{len(self.esem[e])}"))
        return self.esem[e][epoch]

    def _wait(self, f, ev):
        if ev is None:
            return
        if ev[0] == "e":
            _, e, idx = ev
            if e == f and (not SAME_ENGINE_SYNC or f == "pe"):
                return
            if self.wait_e[f].get(e, 0) >= idx:
                return
            self.wait_e[f][e] = idx
            self.eng[f].wait_ge(self._esem(e, (idx - 1) // EPOCH), (idx - 1) % EPOCH + 1)
        else:
            _, key, val = ev
            if self.wait_d[f].get(key, 0) >= val:
                return
            self.wait_d[f][key] = val
            self.eng[f].wait_ge(self.dsem[key], val)

    def _deps(self, f, r, w):
        for t in r:
            for b in t.bufs:
                self._wait(f, b.w)
        for t in w:
            for b in t.bufs:
                self._wait(f, b.w)
                for ev in list(b.rd.values()):
                    self._wait(f, ev)

    def _commit(self, ev, key, r, w):
        for t in r:
            for b in t.bufs:
                b.rd[key] = ev
        for t in w:
            for b in t.bufs:
                b.w = ev
                b.rd = {}

    def op(self, e, fn, r=(), w=()):
        self._deps(e, r, w)
        ins = fn(self.eng[e])
        self.cnt[e] += 1
        idx = self.cnt[e]
        ins.then_inc(self._esem(e, (idx - 1) // EPOCH), 1)
        self._commit(("e", e, idx), e, r, w)
        self.ninstr += 1
        return ins

    def dma(self, out, in_, q="sp", **kw):
        j = self.dnext[q] % (2 if q == 'pool' else NDMA)
        self.dnext[q] += 1
        key = (q, j)
        if key not in self.dsem:
            self.dsem[key] = self.sem(f"d_{q}_{j}")
            self.duse[key] = 0
        if self.duse[key] > 0:
            self._wait(q, ("d", key, 16 * self.duse[key]))
        self._deps(q, [in_], [out])
        self.duse[key] += 1
        val = 16 * self.duse[key]
        self.eng[q].dma_start(out=out.ap, in_=in_.ap, **kw).then_inc(self.dsem[key], 16)
        self._commit(("d", key, val), ("d", q, j, val), [in_], [out])
        self.ninstr += 1

    def barrier(self):
        for f in self.eng:
            for e in self.eng:
                if e != f and self.cnt[e] > 0:
                    self._wait(f, ("e", e, self.cnt[e]))
            for key, n in self.duse.items():
                if n > 0:
                    self._wait(f, ("d", key, 16 * n))

    def finish(self):
        self.barrier()
        self.es.close()

    def mm(self, out, lhsT, rhs, start=True, stop=True):
        return self.op("pe", lambda e: e.matmul(out.ap, lhsT.ap, rhs.ap, start=start, stop=stop),
                       r=[lhsT, rhs], w=[out])

    def tr(self, out, in_, ident):
        return self.op("pe", lambda e: e.transpose(out.ap, in_.ap, ident.ap), r=[in_, ident], w=[out])

    def act(self, out, in_, func, bias=None, scale=1.0, e="act"):
        r = [in_]
        kw = {}
        if bias is not None:
            if isinstance(bias, Tile):
                r.append(bias)
                kw["bias"] = bias.ap
            else:
                kw["bias"] = bias
        if isinstance(scale, Tile):
            r.append(scale)
            kw["scale"] = scale.ap
        else:
            kw["scale"] = scale
        return self.op(e, lambda en: en.activation(out.ap, in_.ap, func, **kw), r=r, w=[out])

    def tt(self, out, a, b, op, e="dve"):
        return self.op(e, lambda en: en.tensor_tensor(out.ap, a.ap, b.ap, op), r=[a, b], w=[out])

    def ts(self, out, a, s1, s2=None, op0=ALU.mult, op1=None, e="dve"):
        r = [a]
        if isinstance(s1, Tile):
            r.append(s1)
            s1 = s1.ap
        if isinstance(s2, Tile):
            r.append(s2)
            s2 = s2.ap
        if op1 is None:
            return self.op(e, lambda en: en.tensor_scalar(out.ap, a.ap, s1, None, op0), r=r, w=[out])
        return self.op(e, lambda en: en.tensor_scalar(out.ap, a.ap, s1, s2, op0, op1), r=r, w=[out])

    def stt(self, out, in0, scalar, in1, op0, op1, e="dve"):
        r = [in0, in1]
        if isinstance(scalar, Tile):
            r.append(scalar)
            scalar = scalar.ap
        return self.op(e, lambda en: en.scalar_tensor_tensor(out.ap, in0.ap, scalar, in1.ap, op0, op1),
                       r=r, w=[out])

    def rsqrt(self, out, in_, addc):
        self.act(out, in_, AF.Sqrt, bias=addc)
        return self.op("dve", lambda en: en.reciprocal(out.ap, out.ap), r=[out], w=[out])

    def copy(self, out, in_, e="dve"):
        if e == "act":
            return self.op(e, lambda en: en.activation(out.ap, in_.ap, AF.Identity), r=[in_], w=[out])
        return self.op(e, lambda en: en.tensor_copy(out.ap, in_.ap), r=[in_], w=[out])

    def memset(self, out, val, e="dve"):
        return self.op(e, lambda en: en.memset(out.ap, val), r=[], w=[out])


def host_consts():
    c = {}
    c["ident"] = np.eye(128, dtype=np.float32)
    idb = np.zeros((128, 64), np.float32)
    idb[np.arange(128), np.arange(128) % 64] = 1.0
    c["idb"] = idb
    bo = np.zeros((128, 128), np.float32)
    bo[:64, :64] = 1.0
    bo[64:, 64:] = 1.0
    c["blockones"] = bo
    pm = np.zeros((128, 128), np.float32)
    pm[np.arange(128), np.arange(128) ^ 16] = 1.0
    c["pm"] = pm
    p = np.arange(128)
    axis = (p % 64) // 32
    half = (p % 32) // 16
    f = p % 16
    inv = (10000.0 ** (-np.arange(16, dtype=np.float32) / 16)).astype(np.float32)
    t = np.arange(T)
    pos = np.where(axis[:, None] == 0, (t // 64)[None, :], (t % 64)[None, :]).astype(np.float32)
    ang = pos * inv[f][:, None]
    c["ropec"] = np.cos(ang).astype(np.float32)
    c["ropes"] = (np.sin(ang) * np.where(half[:, None] == 0, -1.0, 1.0)).astype(np.float32)
    kk = np.arange(128)[:, None]
    qq = np.arange(128)[None, :]
    am = np.ones((128, 5, 128), np.float32)
    am[:, 2, :] = (kk >= qq)
    am[:, 4, :] = (kk <= qq)
    c["amask"] = am.astype(ml_dtypes.bfloat16)
    i = np.arange(64)[:, None]
    j = np.arange(64)[None, :]
    SL = (j < i).astype(np.float32)
    SU = (j > i).astype(np.float32)
    UI = (j >= i).astype(np.float32)
    LI = (j <= i).astype(np.float32)
    one = np.ones((64, 64), np.float32)
    m2 = np.stack([np.concatenate([one, SL, SL, SU], 1), np.concatenate([one, SU, SU, SL], 1)], 0)
    c["rmask2"] = np.ascontiguousarray(m2.transpose(1, 0, 2))
    m5 = np.stack([np.concatenate([UI, one], 1), np.concatenate([LI, one], 1)], 0)
    c["rmask5"] = np.ascontiguousarray(m5.transpose(1, 0, 2))
    mr = np.ones((2, 128, 128), np.float32)
    mr[0, 64:, :64] = UI
    mr[1, 64:, :64] = LI
    c["rmaskr"] = np.ascontiguousarray(mr.transpose(1, 0, 2))
    rst = np.ones((128, 512), np.float32)
    rst[:, ::64] = 0.0
    c["rst"] = rst
    def dft(n, scale):
        a = 2 * np.pi * (np.outer(np.arange(n), np.arange(n)) % n) / n
        return (np.cos(a) * scale), (-np.sin(a) * scale)
    ct, nst = dft(T, T ** -0.5)
    def blk(a):
        return np.ascontiguousarray(a.reshape(16, 128, 8, 256).transpose(2, 1, 0, 3)).astype(ml_dtypes.bfloat16)
    c["dft_ct"] = blk(ct)
    c["dft_nst"] = blk(nst)
    cc, ncs = dft(NCTX, NCTX ** -0.5)
    c["dft_cc"] = cc.astype(ml_dtypes.bfloat16)
    c["dft_ncs"] = ncs.astype(ml_dtypes.bfloat16)
    c64, ns64 = dft(64, 0.125)
    bdc = np.zeros((256, 256))
    bds = np.zeros((256, 256))
    for g in range(4):
        bdc[g * 64:(g + 1) * 64, g * 64:(g + 1) * 64] = c64
        bds[g * 64:(g + 1) * 64, g * 64:(g + 1) * 64] = -ns64
    c["bdc"] = bdc.astype(ml_dtypes.bfloat16)
    c["bds"] = bds.astype(ml_dtypes.bfloat16)
    return c


CONST_DT = {"amask": BF16, "dft_ct": BF16, "dft_nst": BF16, "dft_cc": BF16, "dft_ncs": BF16, "bdc": BF16, "bds": BF16}

WEIGHT_SHAPES = {
    "mod_w": [2, 1024, 6144], "w_in": [2, 1024, 5248], "rwkv_w2": [2, 2, 64, 256], "rwkv_a2": [2, 2, 64, 256],
    "rwkv_g2": [2, 128, 256], "rwkv_v1": [1, 256, 32], "rwkv_v2": [1, 32, 256], "w_branch_rwkv": [2, 256, 1024],
    "w_branch_attn": [2, 512, 1024], "w_branch_fourier": [2, 256, 1024], "w_out": [2, 1024, 1024],
    "ffn_up": [2, 1024, 5632], "ffn_down": [2, 2816, 1024],
}


HC = None


def build_program(dbg=(), nseq=NSEQ, nlayers=2, stages=("rwkv", "att", "fou", "merge", "ffn")):
    global HC
    if HC is None:
        HC = host_consts()
    nc = bass.Bass("TRN2", target_bir_lowering=False)
    P = Prog(nc, dbg)
    P.init_psum()

    def ext(name, shape, dt=F32):
        return Tile(nc.dram_tensor(name, list(shape), dt, kind="ExternalInput").ap(), [Buf(name)])

    x_d = ext("x", [NSEQ, T, D])
    ctx_d = ext("ctx", [NSEQ, NCTX, D])
    cs_d = ext("cs", [128, 8, 5])
    vecs_d = ext("vecs", [128, 2, NV])
    W = {k: ext(k, s) for k, s in WEIGHT_SHAPES.items()}
    C = {k: ext(k, list(v.shape), CONST_DT.get(k, F32)) for k, v in HC.items()}
    y_d = Tile(nc.dram_tensor("y", [NSEQ, T, D], F32, kind="ExternalOutput").ap(), [Buf("y")])
    hbuf = [P.dram("hT0", [128, 8, TT], F32), P.dram("hT1", [128, 8, TT], F32)]

    def ldc(name, shape, dt=F32, src=None, q="sp"):
        t = P.sb(name, shape, dt)
        P.dma(t, src if src is not None else C[name], q=q)
        return t

    ident = ldc("ident", [128, 128])
    idb = ldc("idb", [128, 64])
    blockones = ldc("blockones", [128, 128])
    amask = ldc("amask", [128, 5, 128], BF16)
    rmask2 = ldc("rmask2", [64, 2, 256])
    rmask5 = ldc("rmask5", [64, 2, 128])
    rmaskr = ldc("rmaskr", [128, 2, 128])
    rst = ldc("rst", [128, 512])
    bdc = ldc("bdc", [128, 2, 256], BF16, C["bdc"].v(C["bdc"].ap.rearrange("(kc p) n -> p kc n", p=128)))
    bds = ldc("bds", [128, 2, 256], BF16, C["bds"].v(C["bds"].ap.rearrange("(kc p) n -> p kc n", p=128)))
    dcc = ldc("dcc", [128, 2, 256], BF16, C["dft_cc"].v(C["dft_cc"].ap.rearrange("(kc p) n -> p kc n", p=128)))
    dncs = ldc("dncs", [128, 2, 256], BF16, C["dft_ncs"].v(C["dft_ncs"].ap.rearrange("(kc p) n -> p kc n", p=128)))
    vecs = ldc("vecs_sb", [128, 2, NV], F32, vecs_d)
    ones_bf = P.sb("ones_bf", [128, 128], BF16)
    P.memset(ones_bf, 1.0)
    ident_bf = P.sb("ident_bf", [128, 128], BF16)
    P.copy(ident_bf, ident)
    mall = P.sb("mall", [128, 2, 5, 48], F32)
    vfT = P.sb("vfT", [128, 2, TT], BF16)

    def V(l, name):
        o, c = VO[name]
        return vecs[:, l, o:o + c]

    def ring(name, shape, dt, n, es):
        tiles = [P.sb(f"{name}{i}", shape, dt, es) for i in range(n)]
        st = {"i": 0}

        def nxt():
            t = tiles[st["i"] % n]
            st["i"] += 1
            return t
        return nxt

    with ExitStack() as es:
        cs = P.sb("cs_sb", [128, 8, 5], F32, es)
        P.dma(cs, cs_d)
        sg = P.sb("cs_sg", [128, 8, 5], F32, es)
        P.act(sg, cs, AF.Sigmoid)
        scs = P.sb("scs", [128, 8, 5], F32, es)
        P.tt(scs, cs, sg, ALU.mult)
        mw = ring("mw", [128, 8, 768], F32, 2, es)
        for l in range(nlayers):
            src = W["mod_w"][l]
            srcv = src.v(src.ap.rearrange("(kc p) n -> p kc n", p=128))
            for t8 in range(8):
                mwt = mw()
                P.dma(mwt, srcv[:, :, t8 * 768:(t8 + 1) * 768])
                for o6 in range(6):
                    oc = t8 * 6 + o6
                    ps = P.psum()
                    for kc in range(8):
                        P.mm(ps[:, 0:5], mwt[:, kc, o6 * 128:(o6 + 1) * 128], scs[:, kc, :], start=(kc == 0), stop=(kc == 7))
                    mb = V(l, "modb")
                    P.ts(mall[:, l, :, oc], ps[:, 0:5], mb[:, oc:oc + 1], None, op0=ALU.add)
        P.barrier()
    P.dump("mall", mall, [128, 2, 5, 48])

    def norm_stats(h, n, es_tiles):
        sq, rs = es_tiles
        sqt = sq()
        P.act(sqt[:, :, 0:n], h, AF.Square)
        ps = P.psum()
        for c in range(8):
            P.mm(ps[:, 0:n], ones_bf, sqt[:, c, 0:n], start=(c == 0), stop=(c == 7))
        rst_ = rs()
        P.rsqrt(rst_[:, 0:n], ps[:, 0:n], 1024 * EPS)
        return rst_

    def bc3(t2, n, k):
        return Tile(t2.ap.unsqueeze(1).to_broadcast([t2.ap.shape[0], k, n]), t2.bufs)

    def gains(name, l, j, gname, scale_i, es, shift_i=None):
        gs = P.sb(name, [128, 8], F32, es)
        g = V(l, gname)
        msc = mall[:, l, j, scale_i * 8:(scale_i + 1) * 8]
        if shift_i is not None:
            P.ts(gs, msc, 1.0, 32.0, op0=ALU.add, op1=ALU.mult)
        else:
            P.ts(gs, msc, 32.0, None, op0=ALU.mult)
        P.tt(gs, gs, g, ALU.mult)
        sh = mall[:, l, j, shift_i * 8:(shift_i + 1) * 8] if shift_i is not None else None
        return gs, sh

    LAT_BLKS = [(NCTX + 512 * i, 512) for i in range(4)]
    ALL_BLKS = [(0, NCTX)] + LAT_BLKS

    def load_seq(s, cur):
        with ExitStack() as es:
            xt = ring("xt", [128, 1024], F32, 2, es)
            stg = ring("stg", [128, 8, 512], F32, 2, es)
            for (t0, n) in ALL_BLKS:
                st = stg()
                for tt in range(n // 128):
                    x = xt()
                    if t0 == 0:
                        P.dma(x, ctx_d[s, tt * 128:(tt + 1) * 128, :])
                    else:
                        r0 = t0 - NCTX + tt * 128
                        P.dma(x, x_d[s, r0:r0 + 128, :])
                    for half in range(2):
                        ps = P.psum()
                        for c4 in range(4):
                            c = half * 4 + c4
                            P.tr(ps[:, c4 * 128:(c4 + 1) * 128], x[:, c * 128:(c + 1) * 128], ident)
                        P.copy(st[:, half * 4:(half + 1) * 4, tt * 128:(tt + 1) * 128],
                               ps.v(ps.ap.rearrange("p (a b) -> p a b", b=128)), e=("act" if half else "dve"))
                P.dma(hbuf[cur][:, :, t0:t0 + n], st[:, :, 0:n], q="pool")
            P.barrier()

    def store_seq(s, cur):
        with ExitStack() as es:
            ht = ring("ht", [128, 8, 128], F32, 2, es)
            ot = ring("ot", [128, 1024], F32, 2, es)
            for tt in range(T // 128):
                h = ht()
                P.dma(h, hbuf[cur][:, :, NCTX + tt * 128:NCTX + (tt + 1) * 128])
                o = ot()
                for half in range(2):
                    ps = P.psum()
                    for c4 in range(4):
                        P.tr(ps[:, c4 * 128:(c4 + 1) * 128], h[:, half * 4 + c4, :], ident)
                    P.copy(o[:, half * 512:(half + 1) * 512], ps, e=("act" if half else "dve"))
                P.dma(y_d[s, tt * 128:(tt + 1) * 128, :], o, q="pool")
            P.barrier()

    def stage_a(l, s, cur, aT, es0):
        with ExitStack() as es:
            gsl, shl = gains("gsl", l, s, "g_mpre", 1, es, 0)
            gsc, shc = gains("gsc", l, 4, "g_mpre", 1, es, 0)
            hb = ring("hb", [128, 8, 512], F32, 2, es)
            sq = ring("sq", [128, 8, 512], BF16, 1, es)
            rs = ring("rs", [128, 512], F32, 2, es)
            tmp = ring("tmpa", [128, 8, 512], F32, 1, es)
            for (t0, n) in ALL_BLKS:
                h = hb()
                P.dma(h[:, :, 0:n], hbuf[cur][:, :, t0:t0 + n])
                r = norm_stats(h[:, :, 0:n], n, (sq, rs))
                tm = tmp()
                P.tt(tm[:, :, 0:n], h[:, :, 0:n], bc3(r[:, 0:n], n, 8), ALU.mult)
                gs, sh = (gsc, shc) if t0 == 0 else (gsl, shl)
                for c in range(8):
                    P.act(aT[:, c, t0:t0 + n], tm[:, c, 0:n], AF.Identity, bias=sh[:, c:c + 1], scale=gs[:, c:c + 1])
            P.barrier()

    def stage_ffn(l, s, cur):
        src, dst = hbuf[cur], hbuf[1 - cur]
        nblk_ctx = 1 if l < nlayers - 1 or nlayers == 1 and False else 0
        with ExitStack() as es:
            up = P.sb("up_sb", [128, 8, 2 * DFF], BF16, es)
            upsrc = W["ffn_up"][l]
            upv = upsrc.v(upsrc.ap.rearrange("(kc p) n -> p kc n", p=128))
            for q4 in range(4):
                P.dma(up[:, :, q4 * 1408:(q4 + 1) * 1408], upv[:, :, q4 * 1408:(q4 + 1) * 1408], q="pool")
            dnr = ring("dn_sb", [128, 22, 128], BF16, 2, es)
            dsrc = W["ffn_down"][l]
            dnv = dsrc.v(dsrc.ap.rearrange("(kc p) n -> p kc n", p=128))
            gains_l = gains("fgsl", l, s, "g_fpre", 4, es, 3)
            gains_c = gains("fgsc", l, 4, "g_fpre", 4, es, 3)
            gpost_l, _ = gains("fgpl", l, s, "g_fpost", 5, es)
            gpost_c, _ = gains("fgpc", l, 4, "g_fpost", 5, es)
            cw0, cw1, cw2, cb = V(l, "cw0"), V(l, "cw1"), V(l, "cw2"), V(l, "cb")
            hb = ring("fhb", [128, 8, 258], F32, 1, es)
            sq = ring("fsq", [128, 8, 258], BF16, 1, es)
            rs = ring("frs", [128, 258], F32, 2, es)
            tmp = ring("ftmp", [128, 258], F32, 2, es)
            fT = ring("fT", [128, 8, 258], BF16, 1, es)
            zg = ring("zg", [128, 258], F32, 2, es)
            cz = ring("cz", [128, 256], F32, 2, es)
            gz = ring("gz", [128, 256], F32, 2, es)
            hid = ring("hid", [128, 22, 256], BF16, 1, es)
            mo = ring("fmo", [128, 8, 256], F32, 1, es)
            blks = [(256 * bi, 256) for bi in range(TT // 256)]
            for (t0, n) in blks:
                isctx = (t0 == 0)
                if isctx and l == nlayers - 1:
                    continue
                seq_lo, seq_hi = (0, NCTX) if isctx else (NCTX, TT)
                lo, hi = max(t0 - 1, seq_lo), min(t0 + n + 1, seq_hi)
                c0 = lo - (t0 - 1)
                ncol = hi - lo
                h = hb()
                P.dma(h[:, :, c0:c0 + ncol], src[:, :, lo:hi])
                hv = h[:, :, c0:c0 + ncol]
                sqt = sq()
                P.act(sqt[:, :, 0:ncol], hv, AF.Square)
                ps = P.psum()
                for c in range(8):
                    P.mm(ps[:, 0:ncol], ones_bf, sqt[:, c, 0:ncol], start=(c == 0), stop=(c == 7))
                r = rs()
                P.rsqrt(r[:, 0:ncol], ps[:, 0:ncol], 1024 * EPS)
                gs, sh = gains_c if isctx else gains_l
                f = fT()
                for c in range(8):
                    tm = tmp()
                    P.stt(tm[:, 0:ncol], h[:, c, c0:c0 + ncol], gs[:, c:c + 1], r[:, 0:ncol], ALU.mult, ALU.mult)
                    P.act(f[:, c, c0:c0 + ncol], tm[:, 0:ncol], AF.Identity, bias=sh[:, c:c + 1])
                hd = hid()
                for j in range(22):
                    psg = P.psum()
                    for kc in range(8):
                        P.mm(psg[:, 0:ncol], up[:, kc, j * 128:(j + 1) * 128], f[:, kc, c0:c0 + ncol], start=(kc == 0), stop=(kc == 7))
                    psv = P.psum()
                    for kc in range(8):
                        P.mm(psv[:, 0:n], up[:, kc, DFF + j * 128:DFF + (j + 1) * 128], f[:, kc, 1:1 + n], start=(kc == 0), stop=(kc == 7))
                    z = zg()
                    if c0 == 1:
                        P.memset(z[:, 0:1], 0.0)
                    if c0 + ncol < n + 2:
                        P.memset(z[:, n + 1:n + 2], 0.0)
                    P.copy(z[:, c0:c0 + ncol], psg[:, 0:ncol], e="act")
                    c_ = cz()
                    P.ts(c_, z[:, 1:1 + n], cw1[:, j:j + 1], cb[:, j:j + 1], op0=ALU.mult, op1=ALU.add)
                    P.stt(c_, z[:, 0:n], cw0[:, j:j + 1], c_, ALU.mult, ALU.add)
                    P.stt(c_, z[:, 2:2 + n], cw2[:, j:j + 1], c_, ALU.mult, ALU.add)
                    g_ = gz()
                    P.act(g_, c_, AF.Gelu_apprx_tanh)
                    P.tt(hd[:, j, :], g_, psv[:, 0:n], ALU.mult)
                m = mo()
                for oc in range(8):
                    dn = dnr()
                    P.dma(dn, dnv[:, :, oc * 128:(oc + 1) * 128], q="pool")
                    ps = P.psum()
                    for j in range(22):
                        P.mm(ps[:, 0:n], dn[:, j, :], hd[:, j, :], start=(j == 0), stop=(j == 21))
                    P.copy(m[:, oc, :], ps[:, 0:n], e="act")
                s2 = sq()
                P.act(s2[:, :, 0:n], m, AF.Square)
                ps = P.psum()
                for c in range(8):
                    P.mm(ps[:, 0:n], ones_bf, s2[:, c, 0:n], start=(c == 0), stop=(c == 7))
                r2 = rs()
                P.rsqrt(r2[:, 0:n], ps[:, 0:n], 1024 * EPS)
                P.tt(m, m, bc3(r2[:, 0:n], n, 8), ALU.mult)
                gp = gpost_c if isctx else gpost_l
                for c in range(8):
                    P.stt(m[:, c, :], m[:, c, :], gp[:, c:c + 1], h[:, c, 1:1 + n], ALU.mult, ALU.add)
                P.dma(dst[:, :, t0:t0 + n], m, q="pool")
            P.barrier()

    B = dict(P=P, W=W, C=C, V=V, ring=ring, mall=mall, hbuf=hbuf, vfT=vfT, ident=ident, ident_bf=ident_bf, idb=idb,
             blockones=blockones, amask=amask, rmask2=rmask2, rmask5=rmask5, rmaskr=rmaskr, rst=rst, bdc=bdc, bds=bds,
             dcc=dcc, dncs=dncs, ones_bf=ones_bf, bc3=bc3, gains=gains, norm_stats=norm_stats, nlayers=nlayers,
             LAT_BLKS=LAT_BLKS, ALL_BLKS=ALL_BLKS)

    for s in range(nseq):
        cur = 0
        load_seq(s, cur)
        for l in range(nlayers):
            with ExitStack() as esl:
                aT = P.sb("aT", [128, 8, TT], BF16, esl)
                stage_a(l, s, cur, aT, esl)
                if s == 0:
                    P.dump(f"aT{l}", aT, [128, 8, TT], BF16)
                yrT = P.sb("yrT", [128, 2, TT], BF16, esl)
                if "rwkv" in stages:
                    stage_rwkv(B, l, s, aT, yrT)
                    if s == 0:
                        P.dump(f"yrT{l}", yrT, [128, 2, TT], BF16)
                yaT = P.sb("yaT", [128, 4, TT], BF16, esl)
                yfT = P.sb("yfT", [128, 2, TT], BF16, esl)
                if "att" in stages:
                    stage_att(B, l, s, aT, yaT)
                    if s == 0:
                        P.dump(f"yaT{l}", yaT, [128, 4, TT], BF16)
                if "fou" in stages:
                    stage_fou(B, l, s, aT, yfT)
                    if s == 0:
                        P.dump(f"yfT{l}", yfT, [128, 2, TT], BF16)
                if "merge" in stages:
                    stage_merge(B, l, s, cur, aT, yrT, yaT, yfT)
                P.barrier()
            if s == 0:
                P.dump(f"hmix{l}", hbuf[cur], [128, 8, TT])
            if "ffn" in stages:
                stage_ffn(l, s, cur)
                cur = 1 - cur
            if s == 0:
                P.dump(f"hffn{l}", hbuf[cur], [128, 8, TT])
        store_seq(s, cur)
    P.finish()
    return nc, P


def _wsrc(B, l):
    src = B["W"]["w_in"][l]
    return src.v(src.ap.rearrange("(kc p) n -> p kc n", p=128))


def stage_fou(B, l, s, aT, yfT):
    P, ring, C = B["P"], B["ring"], B["C"]
    need_ctx = l < B["nlayers"] - 1
    bdc, bds, dcc, dncs = B["bdc"], B["bds"], B["dcc"], B["dncs"]
    with ExitStack() as es:
        wf = P.sb("wf", [128, 8, 256], BF16, es)
        P.dma(wf, _wsrc(B, l)[:, :, C_F:C_F + 256], q="pool")
        ufT = P.sb("ufT", [128, 2, TT], BF16, es)
        blks = B["ALL_BLKS"] if need_ctx else B["LAT_BLKS"]
        import os
        if os.environ.get("FOU_CUT") == "2":
            P.barrier()
            return
        for (t0, n) in blks:
            for j in range(2):
                ps = P.psum()
                for kc in range(8):
                    P.mm(ps[:, 0:n], wf[:, kc, j * 128:(j + 1) * 128], aT[:, kc, t0:t0 + n], start=(kc == 0), stop=(kc == 7))
                P.copy(ufT[:, j, t0:t0 + n], ps[:, 0:n], e=("act" if j else "dve"))
        if os.environ.get("FOU_CUT") == "3":
            P.barrier()
            return
        Zc = P.sb("Zc", [128, 18, 256], BF16, es)
        Zs = P.sb("Zs", [128, 18, 256], BF16, es)
        for tt in range(0 if need_ctx else 2, int(os.environ.get("FOU_NT", "18"))):
            ps = P.psum()
            for kc in range(2):
                P.mm(ps[:, 0:256], ufT[:, kc, tt * 128:(tt + 1) * 128], bdc[:, kc, :], start=(kc == 0), stop=(kc == 1))
            psb = P.psum()
            for kc in range(2):
                P.mm(psb[:, 0:256], ufT[:, kc, tt * 128:(tt + 1) * 128], bds[:, kc, :], start=(kc == 0), stop=(kc == 1))
            P.copy(Zc[:, tt, :], ps[:, 0:256], e="act")
            P.copy(Zs[:, tt, :], psb[:, 0:256], e="dve")
        import os
        if os.environ.get("FOU_CUT") == "1":
            P.barrier()
            return
        dct = ring("dct", [128, 16, 256], BF16, 2, es)
        dnst = ring("dnst", [128, 16, 256], BF16, 2, es)
        for nb in range(8):
            ct = dct()
            P.dma(ct, C["dft_ct"][nb])
            st = dnst()
            P.dma(st, C["dft_nst"][nb])
            for ch in range(2):
                ps = P.psum()
                for tt in range(16):
                    P.mm(ps[:, 0:256], Zc[:, 2 + tt, ch * 128:(ch + 1) * 128], ct[:, tt, :], start=(tt == 0), stop=False)
                for tt in range(16):
                    P.mm(ps[:, 0:256], Zs[:, 2 + tt, ch * 128:(ch + 1) * 128], st[:, tt, :], start=False, stop=(tt == 15))
                P.copy(yfT[:, ch, NCTX + nb * 256:NCTX + (nb + 1) * 256], ps[:, 0:256], e=("act" if ch else "dve"))
        if need_ctx:
            for ch in range(2):
                ps = P.psum()
                for tt in range(2):
                    P.mm(ps[:, 0:256], Zc[:, tt, ch * 128:(ch + 1) * 128], dcc[:, tt, :], start=(tt == 0), stop=False)
                for tt in range(2):
                    P.mm(ps[:, 0:256], Zs[:, tt, ch * 128:(ch + 1) * 128], dncs[:, tt, :], start=False, stop=(tt == 1))
                P.copy(yfT[:, ch, 0:256], ps[:, 0:256], e=("act" if ch else "dve"))
        P.barrier()


def stage_merge(B, l, s, cur, aT, yrT, yaT, yfT):
    P, ring, W = B["P"], B["ring"], B["W"]
    need_ctx = l < B["nlayers"] - 1
    blks = B["ALL_BLKS"] if need_ctx else B["LAT_BLKS"]
    hb_d = B["hbuf"][cur]
    with ExitStack() as es0:
        mixpre = P.sb("mixpre", [128, 8, TT], BF16, es0)
        with ExitStack() as es:
            wb = P.sb("wb", [128, 8, 1024], BF16, es)
            for nm, k0, nk in (("w_branch_rwkv", 0, 2), ("w_branch_attn", 2, 4), ("w_branch_fourier", 6, 2)):
                src = W[nm][l]
                P.dma(wb[:, k0:k0 + nk, :], src.v(src.ap.rearrange("(kc p) n -> p kc n", p=128)), q="pool")
            wgr = ring("wg", [128, 8, 3, 128], BF16, 2, es)
            sgr = ring("sgm", [128, 512], F32, 3, es)
            accr = ring("accm", [128, 512], F32, 2, es)
            srcv = _wsrc(B, l)
            for oc in range(8):
                wg = wgr()
                for br in range(3):
                    c0 = C_G + br * 1024 + oc * 128
                    P.dma(wg[:, :, br, :], srcv[:, :, c0:c0 + 128], q="pool")
                for (t0, n) in blks:
                    acc = accr()
                    for br, (yt, k0, nk) in enumerate(((yrT, 0, 2), (yaT, 2, 4), (yfT, 6, 2))):
                        psg = P.psum()
                        for kc in range(8):
                            P.mm(psg[:, 0:n], wg[:, kc, br, :], aT[:, kc, t0:t0 + n], start=(kc == 0), stop=(kc == 7))
                        sg = sgr()
                        P.act(sg[:, 0:n], psg[:, 0:n], AF.Sigmoid)
                        psp = P.psum()
                        for kc in range(nk):
                            P.mm(psp[:, 0:n], wb[:, k0 + kc, oc * 128:(oc + 1) * 128], yt[:, kc, t0:t0 + n],
                                 start=(kc == 0), stop=(kc == nk - 1))
                        if br == 0:
                            P.tt(acc[:, 0:n], sg[:, 0:n], psp[:, 0:n], ALU.mult)
                        else:
                            P.tt(sg[:, 0:n], sg[:, 0:n], psp[:, 0:n], ALU.mult)
                            if br == 1:
                                P.tt(acc[:, 0:n], acc[:, 0:n], sg[:, 0:n], ALU.add)
                            else:
                                P.tt(mixpre[:, oc, t0:t0 + n], acc[:, 0:n], sg[:, 0:n], ALU.add)
            P.barrier()
        with ExitStack() as es:
            wo = P.sb("wo", [128, 8, 1024], BF16, es)
            src = W["w_out"][l]
            P.dma(wo, src.v(src.ap.rearrange("(kc p) n -> p kc n", p=128)), q="pool")
            gm_l, _ = B["gains"]("gml", l, s, "g_mpost", 2, es)
            gm_c, _ = B["gains"]("gmc", l, 4, "g_mpost", 2, es)
            mo = ring("mmo", [128, 8, 512], F32, 1, es)
            hb = ring("mhb", [128, 8, 512], F32, 1, es)
            sq = ring("msq", [128, 8, 512], BF16, 1, es)
            rs = ring("mrs", [128, 512], F32, 2, es)
            for (t0, n) in blks:
                h = hb()
                P.dma(h[:, :, 0:n], hb_d[:, :, t0:t0 + n])
                m = mo()
                for oc in range(8):
                    ps = P.psum()
                    for kc in range(8):
                        P.mm(ps[:, 0:n], wo[:, kc, oc * 128:(oc + 1) * 128], mixpre[:, kc, t0:t0 + n], start=(kc == 0), stop=(kc == 7))
                    P.copy(m[:, oc, 0:n], ps[:, 0:n], e=("act" if oc % 2 else "dve"))
                r = B["norm_stats"](m[:, :, 0:n], n, (sq, rs))
                P.tt(m[:, :, 0:n], m[:, :, 0:n], B["bc3"](r[:, 0:n], n, 8), ALU.mult)
                gm = gm_c if t0 == 0 else gm_l
                for c in range(8):
                    P.stt(h[:, c, 0:n], m[:, c, 0:n], gm[:, c:c + 1], h[:, c, 0:n], ALU.mult, ALU.add)
                P.dma(hb_d[:, :, t0:t0 + n], h[:, :, 0:n], q="pool")
            P.barrier()


def stage_att(B, l, s, aT, yaT):
    P, ring, C, V = B["P"], B["ring"], B["C"], B["V"]
    need_ctx = l < B["nlayers"] - 1
    amask, ident_bf = B["amask"], B["ident_bf"]
    with ExitStack() as es:
        ropec = P.sb("ropec", [128, T], F32, es)
        P.dma(ropec, C["ropec"])
        ropes = P.sb("ropes", [128, T], F32, es)
        P.dma(ropes, C["ropes"])
        pm = P.sb("pm", [128, 128], F32, es)
        P.dma(pm, C["pm"])
        srcv = _wsrc(B, l)
        wq = P.sb("wq", [128, 8, 512], BF16, es)
        P.dma(wq, srcv[:, :, C_Q:C_Q + 512], q="pool")
        wkd = P.sb("wkd", [128, 8, 2, 128], BF16, es)
        for kv in range(2):
            for dup in range(2):
                P.dma(wkd[:, :, kv, dup * 64:(dup + 1) * 64], srcv[:, :, C_K + kv * 64:C_K + (kv + 1) * 64], q="pool")
        wv = P.sb("wv", [128, 8, 128], BF16, es)
        P.dma(wv, srcv[:, :, C_V:C_V + 128], q="pool")
        qrT = P.sb("qrT", [128, 4, TT], BF16, es)
        kdT = P.sb("kdT", [128, 2, TT], BF16, es)
        Vaug = P.sb("Vaug", [128, 18, 2, 65], BF16, es)
        P.memset(Vaug[:, :, :, 64:65], 1.0)
        esink = P.sb("esink", [128, 8], F32, es)
        P.act(esink, V(l, "sink"), AF.Exp)
        qsb = ring("qsb", [128, 512], F32, 2, es)
        t1 = ring("ropet1", [128, 512], F32, 2, es)
        t2 = ring("ropet2", [128, 512], F32, 2, es)
        for (t0, n) in B["ALL_BLKS"]:
            isctx = (t0 == 0)
            for m in range(6):
                if isctx and m < 4 and not need_ctx:
                    continue
                ps = P.psum()
                for kc in range(8):
                    lhsT = wq[:, kc, m * 128:(m + 1) * 128] if m < 4 else wkd[:, kc, m - 4, :]
                    P.mm(ps[:, 0:n], lhsT, aT[:, kc, t0:t0 + n], start=(kc == 0), stop=(kc == 7))
                dst = qrT[:, m, t0:t0 + n] if m < 4 else kdT[:, m - 4, t0:t0 + n]
                if isctx:
                    P.copy(dst, ps[:, 0:n], e="act")
                else:
                    q = qsb()
                    P.copy(q[:, 0:n], ps[:, 0:n], e="act")
                    ps2 = P.psum()
                    P.mm(ps2[:, 0:n], pm, q[:, 0:n])
                    p0 = t0 - NCTX
                    a = t1()
                    P.tt(a[:, 0:n], q[:, 0:n], ropec[:, p0:p0 + n], ALU.mult)
                    b = t2()
                    P.tt(b[:, 0:n], ps2[:, 0:n], ropes[:, p0:p0 + n], ALU.mult)
                    P.tt(dst, a[:, 0:n], b[:, 0:n], ALU.add)
            for tt in range(n // 128):
                ti = t0 // 128 + tt
                ps = P.psum()
                for kc in range(8):
                    P.mm(ps[:, 0:128], aT[:, kc, t0 + tt * 128:t0 + (tt + 1) * 128], wv[:, kc, :], start=(kc == 0), stop=(kc == 7))
                P.copy(Vaug[:, ti, :, 0:64], ps.v(ps.ap[:, 0:128].rearrange("p (a b) -> p a b", b=64)), e="dve")
        if s == 0:
            P.dump(f"qrT{l}", qrT, [128, 4, TT], BF16)
            P.dump(f"kdT{l}", kdT, [128, 2, TT], BF16)
            P.dump(f"Vaug{l}", Vaug, [128, 18, 2, 65], BF16)
        PTr = ring("PT", [128, 5, 128], BF16, 3, es)
        yat = ring("yatok", [128, 8, 64], BF16, 2, es)
        den = ring("den", [128, 8], F32, 2, es)
        qblocks = ([("c", 0), ("c", 1)] if need_ctx else []) + [("l", i) for i in range(16)]
        cnt = {"pss": 0, "pso": 0, "pst": 0}
        for kind, i in qblocks:
            q0 = i * 128 if kind == "c" else NCTX + i * 128
            slots = [(0, 0), (1, 1)]
            if kind == "l":
                if i > 0:
                    slots.append((2, 2 + i - 1))
                slots.append((3, 2 + i))
                if i < 15:
                    slots.append((4, 2 + i + 1))
            ya = yat()
            dn = den()
            for kvg in range(2):
                pso = P.psum_at(4 + cnt["pso"] % 2)
                cnt["pso"] += 1
                pso_v = pso.v(pso.ap[:, 0:260].rearrange("p (h d) -> p h d", d=65))
                for hh in range(4):
                    h = kvg * 4 + hh
                    m = h // 2
                    po = 64 * (h % 2)
                    pss = P.psum_at(2 * (cnt["pss"] % 2), 2)
                    cnt["pss"] += 1
                    pss_v = pss.v(pss.ap.rearrange("p a (b n) -> p (a b) n", n=128))
                    for (sl, kt) in slots:
                        P.mm(pss_v[:, sl, :], kdT[po:po + 64, kvg, kt * 128:(kt + 1) * 128], qrT[po:po + 64, m, q0:q0 + 128])
                    pt = PTr()
                    P.act(pt, pss_v[:, 0:5, :], AF.Exp, scale=0.125)
                    if kind == "l":
                        P.tt(pt, pt, amask, ALU.mult, e="pool")
                    for si, (sl, kt) in enumerate(slots):
                        P.mm(pso_v[:, hh, :], pt[:, sl, :], Vaug[:, kt, kvg, :], start=(si == 0), stop=(si == len(slots) - 1))
                dn4 = dn[:, kvg * 4:(kvg + 1) * 4]
                P.tt(dn4, pso_v[:, :, 64], esink[:, kvg * 4:(kvg + 1) * 4], ALU.add)
                P.op("dve", lambda en: en.reciprocal(dn4.ap, dn4.ap), r=[dn4], w=[dn4])
                P.tt(ya[:, kvg * 4:(kvg + 1) * 4, :], pso_v[:, :, 0:64],
                     Tile(dn4.ap.unsqueeze(2).to_broadcast([128, 4, 64]), dn4.bufs), ALU.mult)
            pst = P.psum_at(6 + cnt["pst"] % 2)
            cnt["pst"] += 1
            pst_bf = pst.v(pst.ap.bitcast(BF16))
            ya_f = ya.v(ya.ap.rearrange("p h d -> p (h d)"))
            for m in range(4):
                P.tr(pst_bf[:, m * 128:(m + 1) * 128], ya_f[:, m * 128:(m + 1) * 128], ident_bf)
            P.copy(yaT[:, :, q0:q0 + 128], pst_bf.v(pst_bf.ap[:, 0:512].rearrange("p (a b) -> p a b", b=128)), e="act")
        P.barrier()


NLOCK = 2


def stage_rwkv(B, l, s, aT, yrT):
    P, ring, W, V = B["P"], B["ring"], B["W"], B["V"]
    nlayers = B["nlayers"]
    need_ctx = l < nlayers - 1
    idb, blockones, rst, vfT = B["idb"], B["blockones"], B["rst"], B["vfT"]
    rmask2, rmask5, rmaskr = B["rmask2"], B["rmask5"], B["rmaskr"]
    with ExitStack() as es:
        w_r = P.sb("w_r", [128, 8, 1152], BF16, es)
        srcv = _wsrc(B, l)
        P.dma(w_r[:, :, 0:576], srcv[:, :, 0:576], q="pool")
        P.dma(w_r[:, :, 576:1152], srcv[:, :, 576:1152], q="pool")
        w2sb = P.sb("w2sb", [128, 256], F32, es)
        P.dma(w2sb, W["rwkv_w2"][l].v(W["rwkv_w2"][l].ap.rearrange("d k n -> (d k) n")))
        a2sb = P.sb("a2sb", [128, 256], F32, es)
        P.dma(a2sb, W["rwkv_a2"][l].v(W["rwkv_a2"][l].ap.rearrange("d k n -> (d k) n")))
        g2sb = P.sb("g2sb", [128, 256], F32, es)
        P.dma(g2sb, W["rwkv_g2"][l])
        if l > 0:
            v1sb = P.sb("v1sb", [128, 2, 32], F32, es)
            P.dma(v1sb, W["rwkv_v1"][l - 1].v(W["rwkv_v1"][l - 1].ap.rearrange("(g p) m -> p g m", p=128)))
            v2sb = P.sb("v2sb", [32, 256], F32, es)
            P.dma(v2sb, W["rwkv_v2"][l - 1])
        mu0, mu1 = V(l, "mu0"), V(l, "mu1")
        k_k, k_a, r_k, lnx_w, lnx_b, v0 = V(l, "k_k"), V(l, "k_a"), V(l, "r_k"), V(l, "lnx_w"), V(l, "lnx_b"), V(l, "v0")
        w0 = [V(l, "w0f"), V(l, "w0b")]
        a0 = [V(l, "a0f"), V(l, "a0b")]
        omka = P.sb("omka", [128, 2], F32, es)
        P.ts(omka, k_a, -1.0, 1.0, op0=ALU.mult, op1=ALU.add)
        kah = P.sb("kah", [128, 2], F32, es)
        P.ts(kah, k_a, 0.5, None, op0=ALU.mult)
        yfwd = P.sb("yfwd", [128, 2, TT], BF16, es)
        u_blk = P.sb("u_blk", [128, 9, 258], F32, es)
        us = P.sb("us", [128, 9, 256], F32, es)
        dtm = ring("dtm", [128, 256], F32, 2, es)
        A = {}
        for nm in ("vv", "kap", "sg", "aa", "aaf", "key", "bb", "cs", "cs2", "e1", "e2", "eex", "e3", "rt", "kt", "bh",
                   "kh", "bck", "kck", "tmp1", "tmp2"):
            A[nm] = P.sb("r_" + nm, [128, 2, 256], BF16 if nm in ("rt", "kt", "bh", "kh", "bck", "kck") else F32, es)
        idbb = P.sb("idb_bf", [128, 64], BF16, es)
        P.copy(idbb, idb)
        th = P.sb("r_th", [128, 256], F32, es)
        sgx = P.sb("r_sgx", [128, 256], F32, es)
        vv1 = P.sb("r_vv1", [32, 256], F32, es)
        etot = P.sb("r_etot", [128, 2, 4], F32, es)
        tot8 = P.sb("r_tot8", [128, 2, 4], F32, es)
        XA = [P.sb(f"XA{i}", [64, 4, 256], BF16, es) for i in range(NLOCK)]
        Wb = [P.sb(f"Wb{i}", [64, 4, 128], BF16, es) for i in range(NLOCK)]
        W32 = [P.sb(f"W32{i}", [64, 4, 128], F32, es) for i in range(NLOCK)]
        Pw = [P.sb(f"Pw{i}", [64, 4, 128], BF16, es) for i in range(NLOCK)]
        RB = [P.sb(f"RB{i}", [64, 4, 128], BF16, es) for i in range(NLOCK)]
        Rt = [P.sb(f"Rt{i}", [128, 4, 128], F32, es) for i in range(NLOCK)]
        Dg = [P.sb(f"Dg{i}", [128, 2, 64], BF16, es) for i in range(NLOCK)]
        Zr = []
        for i in range(8):
            z = P.sb(f"Zr{i}", [128, 4, 64], F32, es)
            zs = Tile(z.ap[0:64], [Buf(f"Zs{i}")])
            zv = Tile(z.ap[64:128], [Buf(f"Zv{i}")])
            Zr.append((Tile(z.ap, zs.bufs + zv.bufs), zs, zv))
        pa = {"i": 0}

        def psA():
            pa["i"] += 1
            return P.psum_at(pa["i"] % 2)

        def fm(arr, h, c):
            po = 64 * (h % 2)
            return arr[po:po + 64, h // 2, c * 64:(c + 1) * 64]

        def v4(ps, n):
            return ps.v(ps.ap.rearrange("p (h n) -> p h n", n=n))

        def prep(bi, d, full):
            t0 = 256 * bi
            seq_lo, seq_hi = (0, NCTX) if bi == 0 else (NCTX, TT)
            lo, hi = max(t0 - 1, seq_lo), min(t0 + 257, seq_hi)
            c0 = lo - (t0 - 1)
            ncol = hi - lo
            if c0 == 1:
                P.memset(u_blk[:, :, 0:1], 0.0)
            if c0 + ncol < 258:
                P.memset(u_blk[:, :, 257:258], 0.0)
            for j in range(9):
                ps = psA()
                for kc in range(8):
                    P.mm(ps[:, 0:ncol], w_r[:, kc, j * 128:(j + 1) * 128], aT[:, kc, lo:hi], start=(kc == 0), stop=(kc == 7))
                P.copy(u_blk[:, j, c0:c0 + ncol], ps[:, 0:ncol], e=("act" if j % 2 else "dve"))
            for j in range(9):
                d0 = dtm()
                P.tt(d0, u_blk[:, j, 0:256], u_blk[:, j, 1:257], ALU.subtract)
                P.stt(us[:, j, :], d0, mu0[:, j:j + 1], u_blk[:, j, 1:257], ALU.mult, ALU.add)
                d1 = dtm()
                P.tt(d1, u_blk[:, j, 2:258], u_blk[:, j, 1:257], ALU.subtract)
                P.stt(us[:, j, :], d1, mu1[:, j:j + 1], us[:, j, :], ALU.mult, ALU.add)
            r_, k_, v_ = us[:, 0:2, :], us[:, 2:4, :], us[:, 4:6, :]
            vv = A["vv"]
            if l == 0:
                P.copy(vv, v_, e="act")
                if d == 0:
                    P.copy(vfT[:, :, t0:t0 + 256], v_, e="act")
            else:
                ps = psA()
                for g in range(2):
                    P.mm(ps[0:32, 0:256], v1sb[:, g, :], us[:, 4 + g, :], start=(g == 0), stop=(g == 1))
                P.copy(vv1, ps[0:32, 0:256], e="act")
                for g in range(2):
                    ps = psA()
                    P.mm(ps[:, 0:256], v2sb[:, g * 128:(g + 1) * 128], vv1)
                    P.act(A["tmp1"][:, g, :], ps[:, 0:256], AF.Sigmoid, bias=v0[:, g:g + 1])
                P.tt(A["tmp2"], vfT[:, :, t0:t0 + 256], v_, ALU.subtract)
                P.tt(A["tmp2"], A["tmp2"], A["tmp1"], ALU.mult)
                P.tt(vv, v_, A["tmp2"], ALU.add)
            kap = A["kap"]
            for g in range(2):
                P.ts(kap[:, g, :], k_[:, g, :], k_k[:, g:g + 1], None, op0=ALU.mult)
            P.tt(A["tmp1"], kap, kap, ALU.mult)
            for g in range(2):
                ps = psA()
                P.mm(ps[:, 0:256], blockones, A["tmp1"][:, g, :])
                P.ts(A["tmp2"][:, g, :], ps[:, 0:256], 1e-24, None, op0=ALU.max)
            P.rsqrt(A["tmp2"], A["tmp2"], 0.0)
            P.tt(kap, kap, A["tmp2"], ALU.mult)
            P.act(th, us[:, 6, :], AF.Tanh)
            sg, aa = A["sg"], A["aa"]
            for g in range(2):
                ps = psA()
                P.mm(ps[:, 0:256], w2sb[64 * d:64 * d + 64, g * 128:(g + 1) * 128], th[64 * d:64 * d + 64, :])
                P.act(sg[:, g, :], ps[:, 0:256], AF.Sigmoid, bias=w0[d][:, g:g + 1])
            for g in range(2):
                ps = psA()
                P.mm(ps[:, 0:256], a2sb[64 * d:64 * d + 64, g * 128:(g + 1) * 128], us[64 * d:64 * d + 64, 7, :])
                P.act(aa[:, g, :], ps[:, 0:256], AF.Sigmoid, bias=a0[d][:, g:g + 1])
            if full:
                for g in range(2):
                    ps = psA()
                    P.mm(ps[:, 0:256], a2sb[0:64, g * 128:(g + 1) * 128], us[0:64, 7, :])
                    P.act(A["aaf"][:, g, :], ps[:, 0:256], AF.Sigmoid, bias=a0[0][:, g:g + 1])
            key, bb = A["key"], A["bb"]
            for g in range(2):
                P.ts(A["tmp1"][:, g, :], aa[:, g, :], k_a[:, g:g + 1], omka[:, g:g + 1], op0=ALU.mult, op1=ALU.add)
            P.tt(key, k_, A["tmp1"], ALU.mult)
            P.tt(bb, kap, aa, ALU.mult)
            cs = A["cs"]
            csf = cs.v(cs.ap.rearrange("p g t -> p (g t)"))
            sgf = sg.v(sg.ap.rearrange("p g t -> p (g t)"))
            P.op("dve", lambda en: en.tensor_tensor_scan(csf.ap, rst.ap, sgf.ap, 0.0, ALU.mult, ALU.add), r=[rst, sg], w=[cs])
            cs4 = cs.v(cs.ap.rearrange("p g (c t) -> p g c t", t=64))
            P.copy(tot8, cs4[:, :, :, 63], e="dve")
            totb = Tile(tot8.ap.unsqueeze(3).to_broadcast([128, 2, 4, 64]), tot8.bufs)
            if d == 1:
                c2 = A["cs2"]
                c24 = c2.v(c2.ap.rearrange("p g (c t) -> p g c t", t=64))
                P.tt(c24, totb, cs4, ALU.subtract)
                P.tt(c2, c2, sg, ALU.add)
                cs = c2
                cs4 = c24
            P.act(etot, tot8, AF.Exp, scale=-DEC)
            P.act(A["e1"], cs, AF.Exp, scale=-DEC)
            P.act(A["e2"], cs, AF.Exp, scale=DEC)
            P.tt(A["tmp1"], cs, sg, ALU.subtract)
            P.act(A["eex"], A["tmp1"], AF.Exp, scale=-DEC)
            t24 = A["tmp2"].v(A["tmp2"].ap.rearrange("p g (c t) -> p g c t", t=64))
            P.tt(t24, totb, cs4, ALU.subtract)
            P.act(A["e3"], A["tmp2"], AF.Exp, scale=-DEC)
            P.tt(A["rt"], r_, A["e1"], ALU.mult)
            P.tt(A["kt"], kap, A["eex"], ALU.mult)
            P.tt(A["bh"], bb, A["e2"], ALU.mult)
            P.tt(A["kh"], key, A["e2"], ALU.mult)
            P.tt(A["bck"], bb, A["e3"], ALU.mult)
            P.tt(A["kck"], key, A["e3"], ALU.mult)

        def hs(h):
            return (h % 2) * 2 + h // 2

        def pair(ps, n):
            return ps.v(ps.ap[:, :, 0:2 * n].rearrange("p a (b n) -> p a b n", n=n))

        def sv(t, n0=None, n1=None):
            ap = t.ap if n0 is None else t.ap[:, :, n0:n1]
            return t.v(ap.rearrange("p (a b) n -> p a b n", a=2))

        def bc4(m, d, p, n):
            return Tile(m.ap[:, d, :].unsqueeze(1).unsqueeze(1).to_broadcast([p, 2, 2, n]), m.bufs)

        def machinery(sl, c, d, zk):
            rt, kt, bh, kh, bck, kck, vv = A["rt"], A["kt"], A["bh"], A["kh"], A["bck"], A["kck"], A["vv"]
            b0 = 2 + 2 * sl
            ps = P.psum_at(b0, 2)
            pv = pair(ps, 256)
            for h in range(4):
                po, a, b = 64 * (h % 2), h % 2, h // 2
                P.mm(pv[0:64, a, b, 0:64], fm(kt, h, c), idbb[po:po + 64, :])
                P.mm(pv[0:64, a, b, 64:128], fm(kt, h, c), fm(kh, h, c))
                P.mm(pv[0:64, a, b, 128:192], fm(kt, h, c), fm(bh, h, c))
                P.mm(pv[0:64, a, b, 192:256], fm(bh, h, c), fm(kt, h, c))
            P.tt(sv(XA[sl]), pv[0:64], bc4(rmask2, d, 64, 256), ALU.mult)
            ps = P.psum_at(b0)
            pv = v4(ps, 128)
            for h in range(4):
                P.mm(pv[0:64, hs(h), :], XA[sl][:, hs(h), 192:256], XA[sl][:, hs(h), 0:128])
            P.tt(W32[sl], XA[sl][:, :, 0:128], pv[0:64], ALU.subtract)
            P.copy(Wb[sl], W32[sl], e="act")
            for lev in range(5):
                ps = P.psum_at(b0 + 1)
                pv = v4(ps, 128)
                for h in range(4):
                    q = hs(h)
                    if lev == 0:
                        Ah, ATh = XA[sl][:, q, 128:192], XA[sl][:, q, 192:256]
                    else:
                        Ah, ATh = Pw[sl][:, q, 0:64], Pw[sl][:, q, 64:128]
                    P.mm(pv[0:64, q, 0:64], ATh, Ah)
                    P.mm(pv[0:64, q, 64:128], Ah, ATh)
                P.copy(Pw[sl], pv[0:64], e="act")
                ps = P.psum_at(b0)
                pv = v4(ps, 128)
                for h in range(4):
                    q = hs(h)
                    P.mm(pv[0:64, q, :], Pw[sl][:, q, 64:128], Wb[sl][:, q, :])
                P.tt(W32[sl], W32[sl], pv[0:64], ALU.add)
                P.copy(Wb[sl], W32[sl], e="act")
            ps = P.psum_at(b0, 2)
            pv = pair(ps, 128)
            for h in range(4):
                po, a, b = 64 * (h % 2), h % 2, h // 2
                P.mm(pv[0:64, a, b, 0:64], fm(bh, h, c), fm(rt, h, c))
                P.mm(pv[0:64, a, b, 64:128], fm(bck, h, c), idbb[po:po + 64, :])
            P.tt(sv(RB[sl]), pv[0:64], bc4(rmask5, d, 64, 128), ALU.mult)
            for g in range(2):
                P.ts(Dg[sl][:, g, :], idb, etot[:, g, c:c + 1], None, op0=ALU.mult)
            ps = P.psum_at(b0, 2)
            pv = pair(ps, 128)
            for h in range(4):
                po, a, b = 64 * (h % 2), h % 2, h // 2
                P.mm(pv[0:64, a, b, 0:64], idbb[po:po + 64, :], fm(rt, h, c))
                P.mm(pv[0:64, a, b, 64:128], idbb[po:po + 64, :], Dg[sl][po:po + 64, h // 2, :])
                P.mm(pv[64:128, a, b, 0:64], fm(kh, h, c), fm(rt, h, c))
                P.mm(pv[64:128, a, b, 64:128], fm(kck, h, c), idbb[po:po + 64, :])
            P.tt(sv(Rt[sl]), pv, bc4(rmaskr, d, 128, 128), ALU.mult)
            ps = P.psum_at(b0 + 1)
            pv = v4(ps, 128)
            for h in range(4):
                q = hs(h)
                P.mm(pv[:, q, :], Wb[sl][:, q, :], RB[sl][:, q, :])
            P.tt(Rt[sl], Rt[sl], pv, ALU.subtract)
            ps = P.psum_at(b0, 2)
            pv = pair(ps, 64)
            for h in range(4):
                po, a, b = 64 * (h % 2), h % 2, h // 2
                P.mm(pv[64:128, a, b, :], fm(vv, h, c), idb[po:po + 64, :])
            zv = Zr[zk % 8][2]
            P.copy(zv.v(zv.ap.rearrange("p (a b) n -> p a b n", a=2)), pv[64:128], e="act")

        def sequential(sl, c, d, zk, t0, emit_y, ysum):
            Z = Zr[zk % 8][0]
            if emit_y:
                ps = P.psum_at(6)
                pv = ps.v(ps.ap[:, 0:128].rearrange("p (g n) -> p g n", n=64))
                for h in range(4):
                    po = 64 * (h % 2)
                    P.mm(pv[po:po + 64, h // 2, :], Z[:, hs(h), :], Rt[sl][:, hs(h), 0:64])
                tk = t0 + c * 64
                if d == 0:
                    P.copy(yfwd[:, :, tk:tk + 64], pv, e="act")
                else:
                    P.tt(ysum[:, :, c * 64:(c + 1) * 64], pv, yfwd[:, :, tk:tk + 64], ALU.add)
            ps = P.psum_at(7)
            pv = ps.v(ps.ap[:, 0:256].rearrange("p (h n) -> p h n", n=64))
            for h in range(4):
                P.mm(pv[0:64, hs(h), :], Rt[sl][:, hs(h), 64:128], Z[:, hs(h), :])
            P.copy(Zr[(zk + 1) % 8][1], pv[0:64], e="dve")

        def readout(bi):
            t0 = 256 * bi
            r_, k_ = us[:, 0:2, :], us[:, 2:4, :]
            ysum, yc, sq, rstd = A["e1"], A["e2"], A["eex"], A["e3"]
            for g in range(2):
                ps = psA()
                P.mm(ps[:, 0:256], blockones, ysum[:, g, :])
                P.stt(yc[:, g, :], ps[:, 0:256], -1.0 / 64, ysum[:, g, :], ALU.mult, ALU.add)
            P.tt(sq, yc, yc, ALU.mult)
            for g in range(2):
                ps = psA()
                P.mm(ps[:, 0:256], blockones, sq[:, g, :])
                P.ts(rstd[:, g, :], ps[:, 0:256], 1.0 / 64, None, op0=ALU.mult)
            P.rsqrt(rstd, rstd, LNX_EPS)
            P.tt(yc, yc, rstd, ALU.mult)
            for g in range(2):
                P.ts(yc[:, g, :], yc[:, g, :], lnx_w[:, g:g + 1], lnx_b[:, g:g + 1], op0=ALU.mult, op1=ALU.add)
            tk = A["tmp1"]
            P.tt(tk, A["aa"], A["aaf"], ALU.add)
            for g in range(2):
                P.ts(tk[:, g, :], tk[:, g, :], kah[:, g:g + 1], omka[:, g:g + 1], op0=ALU.mult, op1=ALU.add)
            rk = A["tmp2"]
            P.tt(rk, r_, k_, ALU.mult)
            P.tt(rk, rk, tk, ALU.mult)
            for g in range(2):
                P.ts(rk[:, g, :], rk[:, g, :], r_k[:, g:g + 1], None, op0=ALU.mult)
            for g in range(2):
                ps = psA()
                P.mm(ps[:, 0:256], blockones, rk[:, g, :])
                P.tt(sq[:, g, :], ps[:, 0:256], A["vv"][:, g, :], ALU.mult)
            P.tt(yc, yc, sq, ALU.add)
            P.act(sgx, us[:, 8, :], AF.Sigmoid)
            for g in range(2):
                ps = psA()
                P.mm(ps[:, 0:256], g2sb[:, g * 128:(g + 1) * 128], sgx)
                P.tt(yrT[:, g, t0:t0 + 256], yc[:, g, :], ps[:, 0:256], ALU.mult)

        nblk = TT // 256
        import os
        RWC = int(os.environ.get("RW_CUT", "99"))
        if RWC < 99:
            prep(0, 0, False)
            if RWC >= 2:
                machinery(0, 0, 0, 0)
            if RWC >= 3:
                sequential(0, 0, 0, 0, 0, True, A["e1"])
            if RWC >= 4:
                prep(1, 1, True)
                machinery(0, 0, 1, 0)
                sequential(0, 0, 1, 0, 256, True, A["e1"])
                readout(1)
            P.barrier()
            return
        for d in range(2):
            order = list(range(nblk)) if d == 0 else [0] + list(range(nblk - 1, 0, -1))
            zk = 0
            P.memset(Zr[0][1], 0.0)
            for bi in order:
                emit_y = need_ctx or bi > 0
                prep(bi, d, full=(d == 1))
                chunks = [0, 1, 2, 3] if d == 0 else [3, 2, 1, 0]
                for g0 in range(0, 4, NLOCK):
                    grp = chunks[g0:g0 + NLOCK]
                    for sl, c in enumerate(grp):
                        machinery(sl, c, d, zk + sl)
                    for sl, c in enumerate(grp):
                        sequential(sl, c, d, zk + sl, 256 * bi, emit_y, A["e1"])
                    zk += len(grp)
                if d == 1 and emit_y:
                    readout(bi)
        P.barrier()


def pack_vecs(inp):
    v = np.zeros((128, 2, NV), np.float32)

    def put(l, name, arr):
        o, c = VO[name]
        a = np.asarray(arr, np.float32).reshape(c, 128).T
        v[:, l, o:o + c] = a

    for l in range(2):
        put(l, "modb", inp["mod_b"][l])
        put(l, "g_mpre", inp["norm_mix_pre"][l])
        put(l, "g_mpost", inp["norm_mix_post"][l])
        put(l, "g_fpre", inp["norm_ffn_pre"][l])
        put(l, "g_fpost", inp["norm_ffn_post"][l])
        put(l, "mu0", inp["rwkv_mu"][l, 0])
        put(l, "mu1", inp["rwkv_mu"][l, 1])
        put(l, "w0f", inp["rwkv_w0"][l, 0])
        put(l, "w0b", inp["rwkv_w0"][l, 1])
        put(l, "a0f", inp["rwkv_a0"][l, 0])
        put(l, "a0b", inp["rwkv_a0"][l, 1])
        put(l, "k_k", inp["rwkv_k_k"][l])
        put(l, "k_a", inp["rwkv_k_a"][l])
        put(l, "r_k", inp["rwkv_r_k"][l])
        put(l, "lnx_w", inp["rwkv_lnx_w"][l])
        put(l, "lnx_b", inp["rwkv_lnx_b"][l])
        if l > 0:
            put(l, "v0", inp["rwkv_v0"][l - 1])
        put(l, "cw0", inp["ffn_conv_w"][l, 0])
        put(l, "cw1", inp["ffn_conv_w"][l, 1])
        put(l, "cw2", inp["ffn_conv_w"][l, 2])
        put(l, "cb", inp["ffn_conv_b"][l])
        o, c = VO["sink"]
        v[:, l, o:o + c] = np.broadcast_to(np.asarray(inp["attn_sink"][l], np.float32)[None, :], (128, 8))
    return v


def make_in_maps(inp, ncores=8):
    global HC
    if HC is None:
        HC = host_consts()
    vecs = pack_vecs(inp)
    shared = {k: np.ascontiguousarray(np.asarray(inp[k], np.float32)) for k in WEIGHT_SHAPES}
    shared.update(HC)
    shared["vecs"] = vecs
    maps = []
    c = np.asarray(inp["c"], np.float32)
    cc = np.asarray(inp["c_ctx"], np.float32)
    for ci in range(ncores):
        b0 = ci * NSEQ
        cs = np.zeros((128, 8, 5), np.float32)
        for j in range(NSEQ):
            cs[:, :, j] = c[b0 + j].reshape(8, 128).T
        cs[:, :, 4] = cc.reshape(8, 128).T
        m = dict(shared)
        m["x"] = np.ascontiguousarray(np.asarray(inp["x"][b0:b0 + NSEQ], np.float32))
        m["ctx"] = np.ascontiguousarray(np.asarray(inp["ctx"][b0:b0 + NSEQ], np.float32))
        m["cs"] = cs
        maps.append(m)
    return maps


DEFAULT_STAGES = ("rwkv", "att", "fou", "merge", "ffn")


def kernel(**inputs):
    nc, P = build_program(stages=DEFAULT_STAGES)
    maps = make_in_maps(inputs)
    res = run_bass_kernel_spmd(nc, maps, core_ids=list(range(8)))
    return np.concatenate([np.asarray(r["y"], np.float32) for r in res.results], axis=0)
```

```python
import numpy as np
from contextlib import ExitStack
import ml_dtypes
import concourse.bass as bass
import concourse.mybir as mybir
from concourse.bass_utils import run_bass_kernel_spmd

F32 = mybir.dt.float32
BF16 = mybir.dt.bfloat16
AF = mybir.ActivationFunctionType
ALU = mybir.AluOpType

EPOCH = 30000
NDMA = 8
SAME_ENGINE_SYNC = True

D = 1024
T = 2048
NCTX = 256
TT = T + NCTX
DFF = 2816
NSEQ = 4
EPS = 1e-6
LNX_EPS = 64e-5
DEC = 0.6065306597126334
C_R, C_Q, C_K, C_V, C_F, C_G = 0, 1152, 1664, 1792, 1920, 2176

VO = {}
_o = 0
for _n, _c in (("modb", 48), ("g_mpre", 8), ("g_mpost", 8), ("g_fpre", 8), ("g_fpost", 8), ("mu0", 9), ("mu1", 9),
               ("w0f", 2), ("w0b", 2), ("a0f", 2), ("a0b", 2), ("k_k", 2), ("k_a", 2), ("r_k", 2), ("lnx_w", 2),
               ("lnx_b", 2), ("v0", 2), ("cw0", 22), ("cw1", 22), ("cw2", 22), ("cb", 22), ("sink", 8)):
    VO[_n] = (_o, _c)
    _o += _c
NV = _o


class Buf:
    __slots__ = ("name", "w", "rd")

    def __init__(self, name):
        self.name = name
        self.w = None
        self.rd = {}


class Tile:
    __slots__ = ("ap", "bufs")

    def __init__(self, ap, bufs):
        self.ap = ap
        self.bufs = bufs

    def __getitem__(self, idx):
        return Tile(self.ap[idx], self.bufs)

    def v(self, ap):
        return Tile(ap, self.bufs)

    def bc(self, shape):
        return Tile(self.ap.to_broadcast(list(shape)), self.bufs)


class Prog:
    def __init__(self, nc, dbg=()):
        self.nc = nc
        self.es = ExitStack()
        self.eng = {"pe": nc.tensor, "act": nc.scalar, "dve": nc.vector, "pool": nc.gpsimd, "sp": nc.sync}
        self.cnt = {e: 0 for e in self.eng}
        self.esem = {e: [] for e in self.eng}
        self.wait_e = {e: {} for e in self.eng}
        self.wait_d = {e: {} for e in self.eng}
        self.dsem = {}
        self.duse = {}
        self.dnext = {e: 0 for e in self.eng}
        self.nbuf = 0
        self.psum_i = 0
        self.ninstr = 0
        self.dbg = set(dbg)
        self.dumps = {}

    def sem(self, name):
        return self.es.enter_context(self.nc.semaphore(name))

    def sb(self, name, shape, dtype, es=None):
        self.nbuf += 1
        t = (es or self.es).enter_context(self.nc.sbuf_tensor(f"{name}_s{self.nbuf}", list(shape), dtype))
        return Tile(t[:], [Buf(name)])

    def dram(self, name, shape, dtype, kind="Internal"):
        t = self.nc.dram_tensor(name, list(shape), dtype, kind=kind)
        return Tile(t.ap(), [Buf(name)])

    def sub(self, tile, idx):
        self.nbuf += 1
        return Tile(tile.ap[idx], [Buf(f"sub{self.nbuf}")])

    def init_psum(self):
        t = self.es.enter_context(self.nc.psum_tensor("psall", [128, 8, 512], F32))
        self.ps_all = t[:]
        self.ps_bufs = [Buf(f"psb{i}") for i in range(8)]

    def psum(self):
        i = self.psum_i % 8
        self.psum_i += 1
        return Tile(self.ps_all[:, i, :], [self.ps_bufs[i]])

    def psum_at(self, i, n=1):
        if n == 1:
            return Tile(self.ps_all[:, i, :], [self.ps_bufs[i]])
        return Tile(self.ps_all[:, i:i + n, :], self.ps_bufs[i:i + n])

    def psum2(self):
        if self.psum_i % 2:
            self.psum_i += 1
        i = self.psum_i % 8
        self.psum_i += 2
        return Tile(self.ps_all[:, i:i + 2, :], [self.ps_bufs[i], self.ps_bufs[i + 1]])

    def dump(self, name, tile, shape, dtype=F32):
        if name not in self.dbg:
            return
        k = f"dbg_{name}_{len(self.dumps)}"
        d = self.dram(k, shape, dtype, kind="ExternalOutput")
        self.dumps[k] = name
        self.dma(d, tile, q="sp")

    def _esem(self, e, epoch):
        while len(self.esem[e]) <= epoch:
            self.esem[e].append(self.sem(f"s_{e}_{len(self.esem[e])}"))
        return self.esem[e][epoch]

    def _wait(self, f, ev):
        if ev is None:
            return
        if ev[0] == "e":
            _, e, idx = ev
            if e == f and (not SAME_ENGINE_SYNC or f == "pe"):
                return
            if self.wait_e[f].get(e, 0) >= idx:
                return
            self.wait_e[f][e] = idx
            self.eng[f].wait_ge(self._esem(e, (idx - 1) // EPOCH), (idx - 1) % EPOCH + 1)
        else:
            _, key, val = ev
            if self.wait_d[f].get(key, 0) >= val:
                return
            self.wait_d[f][key] = val
            self.eng[f].wait_ge(self.dsem[key], val)

    def _deps(self, f, r, w):
        for t in r:
            for b in t.bufs:
                self._wait(f, b.w)
        for t in w:
            for b in t.bufs:
                self._wait(f, b.w)
                for ev in list(b.rd.values()):
                    self._wait(f, ev)

    def _commit(self, ev, key, r, w):
        for t in r:
            for b in t.bufs:
                b.rd[key] = ev
        for t in w:
            for b in t.bufs:
                b.w = ev
                b.rd = {}

    def op(self, e, fn, r=(), w=()):
        self._deps(e, r, w)
        ins = fn(self.eng[e])
        self.cnt[e] += 1
        idx = self.cnt[e]
        ins.then_inc(self._esem(e, (idx - 1) // EPOCH), 1)
        self._commit(("e", e, idx), e, r, w)
        self.ninstr += 1
        return ins

    def dma(self, out, in_, q="sp", **kw):
        j = self.dnext[q] % (2 if q == 'pool' else NDMA)
        self.dnext[q] += 1
        key = (q, j)
        if key not in self.dsem:
            self.dsem[key] = self.sem(f"d_{q}_{j}")
            self.duse[key] = 0
        if self.duse[key] > 0:
            self._wait(q, ("d", key, 16 * self.duse[key]))
        self._deps(q, [in_], [out])
        self.duse[key] += 1
        val = 16 * self.duse[key]
        self.eng[q].dma_start(out=out.ap, in_=in_.ap, **kw).then_inc(self.dsem[key], 16)
        self._commit(("d", key, val), ("d", q, j, val), [in_], [out])
        self.ninstr += 1

    def barrier(self):
        for f in self.eng:
            for e in self.eng:
                if e != f and self.cnt[e] > 0:
                    self._wait(f, ("e", e, self.cnt[e]))
            for key, n in self.duse.items():
                if n > 0:
                    self._wait(f, ("d", key, 16 * n))

    def finish(self):
        self.barrier()
        self.es.close()

    def mm(self, out, lhsT, rhs, start=True, stop=True):
        return self.op("pe", lambda e: e.matmul(out.ap, lhsT.ap, rhs.ap, start=start, stop=stop),
                       r=[lhsT, rhs], w=[out])

    def tr(self, out, in_, ident):
        return self.op("pe", lambda e: e.transpose(out.ap, in_.ap, ident.ap), r=[in_, ident], w=[out])

    def act(self, out, in_, func, bias=None, scale=1.0, e="act"):
        r = [in_]
        kw = {}
        if bias is not None:
            if isinstance(bias, Tile):
                r.append(bias)
                kw["bias"] = bias.ap
            else:
                kw["bias"] = bias
        if isinstance(scale, Tile):
            r.append(scale)
            kw["scale"] = scale.ap
        else:
            kw["scale"] = scale
        return self.op(e, lambda en: en.activation(out.ap, in_.ap, func, **kw), r=r, w=[out])

    def tt(self, out, a, b, op, e="dve"):
        return self.op(e, lambda en: en.tensor_tensor(out.ap, a.ap, b.ap, op), r=[a, b], w=[out])

    def ts(self, out, a, s1, s2=None, op0=ALU.mult, op1=None, e="dve"):
        r = [a]
        if isinstance(s1, Tile):
            r.append(s1)
            s1 = s1.ap
        if isinstance(s2, Tile):
            r.append(s2)
            s2 = s2.ap
        if op1 is None:
            return self.op(e, lambda en: en.tensor_scalar(out.ap, a.ap, s1, None, op0), r=r, w=[out])
        return self.op(e, lambda en: en.tensor_scalar(out.ap, a.ap, s1, s2, op0, op1), r=r, w=[out])

    def stt(self, out, in0, scalar, in1, op0, op1, e="dve"):
        r = [in0, in1]
        if isinstance(scalar, Tile):
            r.append(scalar)
            scalar = scalar.ap
        return self.op(e, lambda en: en.scalar_tensor_tensor(out.ap, in0.ap, scalar, in1.ap, op0, op1),
                       r=r, w=[out])

    def rsqrt(self, out, in_, addc):
        self.act(out, in_, AF.Sqrt, bias=addc)
        return self.op("dve", lambda en: en.reciprocal(out.ap, out.ap), r=[out], w=[out])

    def copy(self, out, in_, e="dve"):
        if e == "act":
            return self.op(e, lambda en: en.activation(out.ap, in_.ap, AF.Identity), r=[in_], w=[out])
        return self.op(e, lambda en: en.tensor_copy(out.ap, in_.ap), r=[in_], w=[out])

    def memset(self, out, val, e="dve"):
        return self.op(e, lambda en: en.memset(out.ap, val), r=[], w=[out])


def host_consts():
    c = {}
    c["ident"] = np.eye(128, dtype=np.float32)
    idb = np.zeros((128, 64), np.float32)
    idb[np.arange(128), np.arange(128) % 64] = 1.0
    c["idb"] = idb
    bo = np.zeros((128, 128), np.float32)
    bo[:64, :64] = 1.0
    bo[64:, 64:] = 1.0
    c["blockones"] = bo
    pm = np.zeros((128, 128), np.float32)
    pm[np.arange(128), np.arange(128) ^ 16] = 1.0
    c["pm"] = pm
    p = np.arange(128)
    axis = (p % 64) // 32
    half = (p % 32) // 16
    f = p % 16
    inv = (10000.0 ** (-np.arange(16, dtype=np.float32) / 16)).astype(np.float32)
    t = np.arange(T)
    pos = np.where(axis[:, None] == 0, (t // 64)[None, :], (t % 64)[None, :]).astype(np.float32)
    ang = pos * inv[f][:, None]
    c["ropec"] = np.cos(ang).astype(np.float32)
    c["ropes"] = (np.sin(ang) * np.where(half[:, None] == 0, -1.0, 1.0)).astype(np.float32)
    kk = np.arange(128)[:, None]
    qq = np.arange(128)[None, :]
    am = np.ones((128, 5, 128), np.float32)
    am[:, 2, :] = (kk >= qq)
    am[:, 4, :] = (kk <= qq)
    c["amask"] = am.astype(ml_dtypes.bfloat16)
    i = np.arange(64)[:, None]
    j = np.arange(64)[None, :]
    SL = (j < i).astype(np.float32)
    SU = (j > i).astype(np.float32)
    UI = (j >= i).astype(np.float32)
    LI = (j <= i).astype(np.float32)
    one = np.ones((64, 64), np.float32)
    m2 = np.stack([np.concatenate([one, SL, SL, SU], 1), np.concatenate([one, SU, SU, SL], 1)], 0)
    c["rmask2"] = np.ascontiguousarray(m2.transpose(1, 0, 2))
    m5 = np.stack([np.concatenate([UI, one], 1), np.concatenate([LI, one], 1)], 0)
    c["rmask5"] = np.ascontiguousarray(m5.transpose(1, 0, 2))
    mr = np.ones((2, 128, 128), np.float32)
    mr[0, 64:, :64] = UI
    mr[1, 64:, :64] = LI
    c["rmaskr"] = np.ascontiguousarray(mr.transpose(1, 0, 2))
    rst = np.ones((128, 512), np.float32)
    rst[:, ::64] = 0.0
    c["rst"] = rst
    def dft(n, scale):
        a = 2 * np.pi * (np.outer(np.arange(n), np.arange(n)) % n) / n
        return (np.cos(a) * scale), (-np.sin(a) * scale)
    ct, nst = dft(T, T ** -0.5)
    def blk(a):
        return np.ascontiguousarray(a.reshape(16, 128, 8, 256).transpose(2, 1, 0, 3)).astype(ml_dtypes.bfloat16)
    c["dft_ct"] = blk(ct)
    c["dft_nst"] = blk(nst)
    cc, ncs = dft(NCTX, NCTX ** -0.5)
    c["dft_cc"] = cc.astype(ml_dtypes.bfloat16)
    c["dft_ncs"] = ncs.astype(ml_dtypes.bfloat16)
    c64, ns64 = dft(64, 0.125)
    bdc = np.zeros((256, 256))
    bds = np.zeros((256, 256))
    for g in range(4):
        bdc[g * 64:(g + 1) * 64, g * 64:(g + 1) * 64] = c64
        bds[g * 64:(g + 1) * 64, g * 64:(g + 1) * 64] = -ns64
    c["bdc"] = bdc.astype(ml_dtypes.bfloat16)
    c["bds"] = bds.astype(ml_dtypes.bfloat16)
    return c


CONST_DT = {"amask": BF16, "dft_ct": BF16, "dft_nst": BF16, "dft_cc": BF16, "dft_ncs": BF16, "bdc": BF16, "bds": BF16}

WEIGHT_SHAPES = {
    "mod_w": [2, 1024, 6144], "w_in": [2, 1024, 5248], "rwkv_w2": [2, 2, 64, 256], "rwkv_a2": [2, 2, 64, 256],
    "rwkv_g2": [2, 128, 256], "rwkv_v1": [1, 256, 32], "rwkv_v2": [1, 32, 256], "w_branch_rwkv": [2, 256, 1024],
    "w_branch_attn": [2, 512, 1024], "w_branch_fourier": [2, 256, 1024], "w_out": [2, 1024, 1024],
    "ffn_up": [2, 1024, 5632], "ffn_down": [2, 2816, 1024],
}


HC = None


def build_program(dbg=(), nseq=NSEQ, nlayers=2, stages=("rwkv", "att", "fou", "merge", "ffn")):
    global HC
    if HC is None:
        HC = host_consts()
    nc = bass.Bass("TRN2", target_bir_lowering=False)
    P = Prog(nc, dbg)
    P.init_psum()

    def ext(name, shape, dt=F32):
        return Tile(nc.dram_tensor(name, list(shape), dt, kind="ExternalInput").ap(), [Buf(name)])

    x_d = ext("x", [NSEQ, T, D])
    ctx_d = ext("ctx", [NSEQ, NCTX, D])
    cs_d = ext("cs", [128, 8, 5])
    vecs_d = ext("vecs", [128, 2, NV])
    W = {k: ext(k, s) for k, s in WEIGHT_SHAPES.items()}
    C = {k: ext(k, list(v.shape), CONST_DT.get(k, F32)) for k, v in HC.items()}
    y_d = Tile(nc.dram_tensor("y", [NSEQ, T, D], F32, kind="ExternalOutput").ap(), [Buf("y")])
    hbuf = [P.dram("hT0", [128, 8, TT], F32), P.dram("hT1", [128, 8, TT], F32)]

    def ldc(name, shape, dt=F32, src=None, q="sp"):
        t = P.sb(name, shape, dt)
        P.dma(t, src if src is not None else C[name], q=q)
        return t

    ident = ldc("ident", [128, 128])
    idb = ldc("idb", [128, 64])
    blockones = ldc("blockones", [128, 128])
    amask = ldc("amask", [128, 5, 128], BF16)
    rmask2 = ldc("rmask2", [64, 2, 256])
    rmask5 = ldc("rmask5", [64, 2, 128])
    rmaskr = ldc("rmaskr", [128, 2, 128])
    rst = ldc("rst", [128, 512])
    bdc = ldc("bdc", [128, 2, 256], BF16, C["bdc"].v(C["bdc"].ap.rearrange("(kc p) n -> p kc n", p=128)))
    bds = ldc("bds", [128, 2, 256], BF16, C["bds"].v(C["bds"].ap.rearrange("(kc p) n -> p kc n", p=128)))
    dcc = ldc("dcc", [128, 2, 256], BF16, C["dft_cc"].v(C["dft_cc"].ap.rearrange("(kc p) n -> p kc n", p=128)))
    dncs = ldc("dncs", [128, 2, 256], BF16, C["dft_ncs"].v(C["dft_ncs"].ap.rearrange("(kc p) n -> p kc n", p=128)))
    vecs = ldc("vecs_sb", [128, 2, NV], F32, vecs_d)
    ones_bf = P.sb("ones_bf", [128, 128], BF16)
    P.memset(ones_bf, 1.0)
    ident_bf = P.sb("ident_bf", [128, 128], BF16)
    P.copy(ident_bf, ident)
    mall = P.sb("mall", [128, 2, 5, 48], F32)
    vfT = P.sb("vfT", [128, 2, TT], BF16)

    def V(l, name):
        o, c = VO[name]
        return vecs[:, l, o:o + c]

    def ring(name, shape, dt, n, es):
        tiles = [P.sb(f"{name}{i}", shape, dt, es) for i in range(n)]
        st = {"i": 0}

        def nxt():
            t = tiles[st["i"] % n]
            st["i"] += 1
            return t
        return nxt

    with ExitStack() as es:
        cs = P.sb("cs_sb", [128, 8, 5], F32, es)
        P.dma(cs, cs_d)
        sg = P.sb("cs_sg", [128, 8, 5], F32, es)
        P.act(sg, cs, AF.Sigmoid)
        scs = P.sb("scs", [128, 8, 5], F32, es)
        P.tt(scs, cs, sg, ALU.mult)
        mw = ring("mw", [128, 8, 768], F32, 2, es)
        for l in range(nlayers):
            src = W["mod_w"][l]
            srcv = src.v(src.ap.rearrange("(kc p) n -> p kc n", p=128))
            for t8 in range(8):
                mwt = mw()
                P.dma(mwt, srcv[:, :, t8 * 768:(t8 + 1) * 768])
                for o6 in range(6):
                    oc = t8 * 6 + o6
                    ps = P.psum()
                    for kc in range(8):
                        P.mm(ps[:, 0:5], mwt[:, kc, o6 * 128:(o6 + 1) * 128], scs[:, kc, :], start=(kc == 0), stop=(kc == 7))
                    mb = V(l, "modb")
                    P.ts(mall[:, l, :, oc], ps[:, 0:5], mb[:, oc:oc + 1], None, op0=ALU.add)
        P.barrier()
    P.dump("mall", mall, [128, 2, 5, 48])

    def norm_stats(h, n, es_tiles):
        sq, rs = es_tiles
        sqt = sq()
        P.act(sqt[:, :, 0:n], h, AF.Square)
        ps = P.psum()
        for c in range(8):
            P.mm(ps[:, 0:n], ones_bf, sqt[:, c, 0:n], start=(c == 0), stop=(c == 7))
        rst_ = rs()
        P.rsqrt(rst_[:, 0:n], ps[:, 0:n], 1024 * EPS)
        return rst_

    def bc3(t2, n, k):
        return Tile(t2.ap.unsqueeze(1).to_broadcast([t2.ap.shape[0], k, n]), t2.bufs)

    def gains(name, l, j, gname, scale_i, es, shift_i=None):
        gs = P.sb(name, [128, 8], F32, es)
        g = V(l, gname)
        msc = mall[:, l, j, scale_i * 8:(scale_i + 1) * 8]
        if shift_i is not None:
            P.ts(gs, msc, 1.0, 32.0, op0=ALU.add, op1=ALU.mult)
        else:
            P.ts(gs, msc, 32.0, None, op0=ALU.mult)
        P.tt(gs, gs, g, ALU.mult)
        sh = mall[:, l, j, shift_i * 8:(shift_i + 1) * 8] if shift_i is not None else None
        return gs, sh

    LAT_BLKS = [(NCTX + 512 * i, 512) for i in range(4)]
    ALL_BLKS = [(0, NCTX)] + LAT_BLKS

    def load_seq(s, cur):
        with ExitStack() as es:
            xt = ring("xt", [128, 1024], F32, 2, es)
            stg = ring("stg", [128, 8, 512], F32, 2, es)
            for (t0, n) in ALL_BLKS:
                st = stg()
                for tt in range(n // 128):
                    x = xt()
                    if t0 == 0:
                        P.dma(x, ctx_d[s, tt * 128:(tt + 1) * 128, :])
                    else:
                        r0 = t0 - NCTX + tt * 128
                        P.dma(x, x_d[s, r0:r0 + 128, :])
                    for half in range(2):
                        ps = P.psum()
                        for c4 in range(4):
                            c = half * 4 + c4
                            P.tr(ps[:, c4 * 128:(c4 + 1) * 128], x[:, c * 128:(c + 1) * 128], ident)
                        P.copy(st[:, half * 4:(half + 1) * 4, tt * 128:(tt + 1) * 128],
                               ps.v(ps.ap.rearrange("p (a b) -> p a b", b=128)), e=("act" if half else "dve"))
                P.dma(hbuf[cur][:, :, t0:t0 + n], st[:, :, 0:n], q="pool")
            P.barrier()

    def store_seq(s, cur):
        with ExitStack() as es:
            ht = ring("ht", [128, 8, 128], F32, 2, es)
            ot = ring("ot", [128, 1024], F32, 2, es)
            for tt in range(T // 128):
                h = ht()
                P.dma(h, hbuf[cur][:, :, NCTX + tt * 128:NCTX + (tt + 1) * 128])
                o = ot()
                for half in range(2):
                    ps = P.psum()
                    for c4 in range(4):
                        P.tr(ps[:, c4 * 128:(c4 + 1) * 128], h[:, half * 4 + c4, :], ident)
                    P.copy(o[:, half * 512:(half + 1) * 512], ps, e=("act" if half else "dve"))
                P.dma(y_d[s, tt * 128:(tt + 1) * 128, :], o, q="pool")
            P.barrier()

    def stage_a(l, s, cur, aT, es0):
        with ExitStack() as es:
            gsl, shl = gains("gsl", l, s, "g_mpre", 1, es, 0)
            gsc, shc = gains("gsc", l, 4, "g_mpre", 1, es, 0)
            hb = ring("hb", [128, 8, 512], F32, 2, es)
            sq = ring("sq", [128, 8, 512], BF16, 1, es)
            rs = ring("rs", [128, 512], F32, 2, es)
            tmp = ring("tmpa", [128, 8, 512], F32, 1, es)
            for (t0, n) in ALL_BLKS:
                h = hb()
                P.dma(h[:, :, 0:n], hbuf[cur][:, :, t0:t0 + n])
                r = norm_stats(h[:, :, 0:n], n, (sq, rs))
                tm = tmp()
                P.tt(tm[:, :, 0:n], h[:, :, 0:n], bc3(r[:, 0:n], n, 8), ALU.mult)
                gs, sh = (gsc, shc) if t0 == 0 else (gsl, shl)
                for c in range(8):
                    P.act(aT[:, c, t0:t0 + n], tm[:, c, 0:n], AF.Identity, bias=sh[:, c:c + 1], scale=gs[:, c:c + 1])
            P.barrier()

    def stage_ffn(l, s, cur):
        src, dst = hbuf[cur], hbuf[1 - cur]
        nblk_ctx = 1 if l < nlayers - 1 or nlayers == 1 and False else 0
        with ExitStack() as es:
            up = P.sb("up_sb", [128, 8, 2 * DFF], BF16, es)
            upsrc = W["ffn_up"][l]
            upv = upsrc.v(upsrc.ap.rearrange("(kc p) n -> p kc n", p=128))
            for q4 in range(4):
                P.dma(up[:, :, q4 * 1408:(q4 + 1) * 1408], upv[:, :, q4 * 1408:(q4 + 1) * 1408], q="pool")
            dnr = ring("dn_sb", [128, 22, 128], BF16, 2, es)
            dsrc = W["ffn_down"][l]
            dnv = dsrc.v(dsrc.ap.rearrange("(kc p) n -> p kc n", p=128))
            gains_l = gains("fgsl", l, s, "g_fpre", 4, es, 3)
            gains_c = gains("fgsc", l, 4, "g_fpre", 4, es, 3)
            gpost_l, _ = gains("fgpl", l, s, "g_fpost", 5, es)
            gpost_c, _ = gains("fgpc", l, 4, "g_fpost", 5, es)
            cw0, cw1, cw2, cb = V(l, "cw0"), V(l, "cw1"), V(l, "cw2"), V(l, "cb")
            hb = ring("fhb", [128, 8, 258], F32, 1, es)
            sq = ring("fsq", [128, 8, 258], BF16, 1, es)
            rs = ring("frs", [128, 258], F32, 2, es)
            tmp = ring("ftmp", [128, 258], F32, 2, es)
            fT = ring("fT", [128, 8, 258], BF16, 1, es)
            zg = ring("zg", [128, 258], F32, 2, es)
            cz = ring("cz", [128, 256], F32, 2, es)
            gz = ring("gz", [128, 256], F32, 2, es)
            hid = ring("hid", [128, 22, 256], BF16, 1, es)
            mo = ring("fmo", [128, 8, 256], F32, 1, es)
            blks = [(256 * bi, 256) for bi in range(TT // 256)]
            for (t0, n) in blks:
                isctx = (t0 == 0)
                if isctx and l == nlayers - 1:
                    continue
                seq_lo, seq_hi = (0, NCTX) if isctx else (NCTX, TT)
                lo, hi = max(t0 - 1, seq_lo), min(t0 + n + 1, seq_hi)
                c0 = lo - (t0 - 1)
                ncol = hi - lo
                h = hb()
                P.dma(h[:, :, c0:c0 + ncol], src[:, :, lo:hi])
                hv = h[:, :, c0:c0 + ncol]
                sqt = sq()
                P.act(sqt[:, :, 0:ncol], hv, AF.Square)
                ps = P.psum()
                for c in range(8):
                    P.mm(ps[:, 0:ncol], ones_bf, sqt[:, c, 0:ncol], start=(c == 0), stop=(c == 7))
                r = rs()
                P.rsqrt(r[:, 0:ncol], ps[:, 0:ncol], 1024 * EPS)
                gs, sh = gains_c if isctx else gains_l
                f = fT()
                for c in range(8):
                    tm = tmp()
                    P.stt(tm[:, 0:ncol], h[:, c, c0:c0 + ncol], gs[:, c:c + 1], r[:, 0:ncol], ALU.mult, ALU.mult)
                    P.act(f[:, c, c0:c0 + ncol], tm[:, 0:ncol], AF.Identity, bias=sh[:, c:c + 1])
                hd = hid()
                for j in range(22):
                    psg = P.psum()
                    for kc in range(8):
                        P.mm(psg[:, 0:ncol], up[:, kc, j * 128:(j + 1) * 128], f[:, kc, c0:c0 + ncol], start=(kc == 0), stop=(kc == 7))
                    psv = P.psum()
                    for kc in range(8):
                        P.mm(psv[:, 0:n], up[:, kc, DFF + j * 128:DFF + (j + 1) * 128], f[:, kc, 1:1 + n], start=(kc == 0), stop=(kc == 7))
                    z = zg()
                    if c0 == 1:
                        P.memset(z[:, 0:1], 0.0)
                    if c0 + ncol < n + 2:
                        P.memset(z[:, n + 1:n + 2], 0.0)
                    P.copy(z[:, c0:c0 + ncol], psg[:, 0:ncol], e="act")
                    c_ = cz()
                    P.ts(c_, z[:, 1:1 + n], cw1[:, j:j + 1], cb[:, j:j + 1], op0=ALU.mult, op1=ALU.add)
                    P.stt(c_, z[:, 0:n], cw0[:, j:j + 1], c_, ALU.mult, ALU.add)
                    P.stt(c_, z[:, 2:2 + n], cw2[:, j:j + 1], c_, ALU.mult, ALU.add)
                    g_ = gz()
                    P.act(g_, c_, AF.Gelu_apprx_tanh)
                    P.tt(hd[:, j, :], g_, psv[:, 0:n], ALU.mult)
                m = mo()
                for oc in range(8):
                    dn = dnr()
                    P.dma(dn, dnv[:, :, oc * 128:(oc + 1) * 128], q="pool")
                    ps = P.psum()
                    for j in range(22):
                        P.mm(ps[:, 0:n], dn[:, j, :], hd[:, j, :], start=(j == 0), stop=(j == 21))
                    P.copy(m[:, oc, :], ps[:, 0:n], e="act")
                s2 = sq()
                P.act(s2[:, :, 0:n], m, AF.Square)
                ps = P.psum()
                for c in range(8):
                    P.mm(ps[:, 0:n], ones_bf, s2[:, c, 0:n], start=(c == 0), stop=(c == 7))
                r2 = rs()
                P.rsqrt(r2[:, 0:n], ps[:, 0:n], 1024 * EPS)
                P.tt(m, m, bc3(r2[:, 0:n], n, 8), ALU.mult)
                gp = gpost_c if isctx else gpost_l
                for c in range(8):
                    P.stt(m[:, c, :], m[:, c, :], gp[:, c:c + 1], h[:, c, 1:1 + n], ALU.mult, ALU.add)
                P.dma(dst[:, :, t0:t0 + n], m, q="pool")
            P.barrier()

    B = dict(P=P, W=W, C=C, V=V, ring=ring, mall=mall, hbuf=hbuf, vfT=vfT, ident=ident, ident_bf=ident_bf, idb=idb,
             blockones=blockones, amask=amask, rmask2=rmask2, rmask5=rmask5, rmaskr=rmaskr, rst=rst, bdc=bdc, bds=bds,
             dcc=dcc, dncs=dncs, ones_bf=ones_bf, bc3=bc3, gains=gains, norm_stats=norm_stats, nlayers=nlayers,
             LAT_BLKS=LAT_BLKS, ALL_BLKS=ALL_BLKS)

    for s in range(nseq):
        cur = 0
        load_seq(s, cur)
        for l in range(nlayers):
            with ExitStack() as esl:
                aT = P.sb("aT", [128, 8, TT], BF16, esl)
                stage_a(l, s, cur, aT, esl)
                if s == 0:
                    P.dump(f"aT{l}", aT, [128, 8, TT], BF16)
                yrT = P.sb("yrT", [128, 2, TT], BF16, esl)
                if "rwkv" in stages:
                    stage_rwkv(B, l, s, aT, yrT)
                    if s == 0:
                        P.dump(f"yrT{l}", yrT, [128, 2, TT], BF16)
                yaT = P.sb("yaT", [128, 4, TT], BF16, esl)
                yfT = P.sb("yfT", [128, 2, TT], BF16, esl)
                if "att" in stages:
                    stage_att(B, l, s, aT, yaT)
                    if s == 0:
                        P.dump(f"yaT{l}", yaT, [128, 4, TT], BF16)
                if "fou" in stages:
                    stage_fou(B, l, s, aT, yfT)
                    if s == 0:
                        P.dump(f"yfT{l}", yfT, [128, 2, TT], BF16)
                if "merge" in stages:
                    stage_merge(B, l, s, cur, aT, yrT, yaT, yfT)
                P.barrier()
            if s == 0:
                P.dump(f"hmix{l}", hbuf[cur], [128, 8, TT])
            if "ffn" in stages:
                stage_ffn(l, s, cur)
                cur = 1 - cur
            if s == 0:
                P.dump(f"hffn{l}", hbuf[cur], [128, 8, TT])
        store_seq(s, cur)
    P.finish()
    return nc, P


def _wsrc(B, l):
    src = B["W"]["w_in"][l]
    return src.v(src.ap.rearrange("(kc p) n -> p kc n", p=128))


def stage_fou(B, l, s, aT, yfT):
    P, ring, C = B["P"], B["ring"], B["C"]
    need_ctx = l < B["nlayers"] - 1
    bdc, bds, dcc, dncs = B["bdc"], B["bds"], B["dcc"], B["dncs"]
    with ExitStack() as es:
        wf = P.sb("wf", [128, 8, 256], BF16, es)
        P.dma(wf, _wsrc(B, l)[:, :, C_F:C_F + 256], q="pool")
        ufT = P.sb("ufT", [128, 2, TT], BF16, es)
        blks = B["ALL_BLKS"] if need_ctx else B["LAT_BLKS"]
        import os
        if os.environ.get("FOU_CUT") == "2":
            P.barrier()
            return
        for (t0, n) in blks:
            for j in range(2):
                ps = P.psum()
                for kc in range(8):
                    P.mm(ps[:, 0:n], wf[:, kc, j * 128:(j + 1) * 128], aT[:, kc, t0:t0 + n], start=(kc == 0), stop=(kc == 7))
                P.copy(ufT[:, j, t0:t0 + n], ps[:, 0:n], e=("act" if j else "dve"))
        if os.environ.get("FOU_CUT") == "3":
            P.barrier()
            return
        Zc = P.sb("Zc", [128, 18, 256], BF16, es)
        Zs = P.sb("Zs", [128, 18, 256], BF16, es)
        for tt in range(0 if need_ctx else 2, int(os.environ.get("FOU_NT", "18"))):
            ps = P.psum()
            for kc in range(2):
                P.mm(ps[:, 0:256], ufT[:, kc, tt * 128:(tt + 1) * 128], bdc[:, kc, :], start=(kc == 0), stop=(kc == 1))
            psb = P.psum()
            for kc in range(2):
                P.mm(psb[:, 0:256], ufT[:, kc, tt * 128:(tt + 1) * 128], bds[:, kc, :], start=(kc == 0), stop=(kc == 1))
            P.copy(Zc[:, tt, :], ps[:, 0:256], e="act")
            P.copy(Zs[:, tt, :], psb[:, 0:256], e="dve")
        import os
        if os.environ.get("FOU_CUT") == "1":
            P.barrier()
            return
        dct = ring("dct", [128, 16, 256], BF16, 2, es)
        dnst = ring("dnst", [128, 16, 256], BF16, 2, es)
        for nb in range(8):
            ct = dct()
            P.dma(ct, C["dft_ct"][nb])
            st = dnst()
            P.dma(st, C["dft_nst"][nb])
            for ch in range(2):
                ps = P.psum()
                for tt in range(16):
                    P.mm(ps[:, 0:256], Zc[:, 2 + tt, ch * 128:(ch + 1) * 128], ct[:, tt, :], start=(tt == 0), stop=False)
                for tt in range(16):
                    P.mm(ps[:, 0:256], Zs[:, 2 + tt, ch * 128:(ch + 1) * 128], st[:, tt, :], start=False, stop=(tt == 15))
                P.copy(yfT[:, ch, NCTX + nb * 256:NCTX + (nb + 1) * 256], ps[:, 0:256], e=("act" if ch else "dve"))
        if need_ctx:
            for ch in range(2):
                ps = P.psum()
                for tt in range(2):
                    P.mm(ps[:, 0:256], Zc[:, tt, ch * 128:(ch + 1) * 128], dcc[:, tt, :], start=(tt == 0), stop=False)
                for tt in range(2):
                    P.mm(ps[:, 0:256], Zs[:, tt, ch * 128:(ch + 1) * 128], dncs[:, tt, :], start=False, stop=(tt == 1))
                P.copy(yfT[:, ch, 0:256], ps[:, 0:256], e=("act" if ch else "dve"))
        P.barrier()


def stage_merge(B, l, s, cur, aT, yrT, yaT, yfT):
    P, ring, W = B["P"], B["ring"], B["W"]
    need_ctx = l < B["nlayers"] - 1
    blks = B["ALL_BLKS"] if need_ctx else B["LAT_BLKS"]
    hb_d = B["hbuf"][cur]
    with ExitStack() as es0:
        mixpre = P.sb("mixpre", [128, 8, TT], BF16, es0)
        with ExitStack() as es:
            wb = P.sb("wb", [128, 8, 1024], BF16, es)
            for nm, k0, nk in (("w_branch_rwkv", 0, 2), ("w_branch_attn", 2, 4), ("w_branch_fourier", 6, 2)):
                src = W[nm][l]
                P.dma(wb[:, k0:k0 + nk, :], src.v(src.ap.rearrange("(kc p) n -> p kc n", p=128)), q="pool")
            wgr = ring("wg", [128, 8, 3, 128], BF16, 2, es)
            sgr = ring("sgm", [128, 512], F32, 3, es)
            accr = ring("accm", [128, 512], F32, 2, es)
            srcv = _wsrc(B, l)
            for oc in range(8):
                wg = wgr()
                for br in range(3):
                    c0 = C_G + br * 1024 + oc * 128
                    P.dma(wg[:, :, br, :], srcv[:, :, c0:c0 + 128], q="pool")
                for (t0, n) in blks:
                    acc = accr()
                    for br, (yt, k0, nk) in enumerate(((yrT, 0, 2), (yaT, 2, 4), (yfT, 6, 2))):
                        psg = P.psum()
                        for kc in range(8):
                            P.mm(psg[:, 0:n], wg[:, kc, br, :], aT[:, kc, t0:t0 + n], start=(kc == 0), stop=(kc == 7))
                        sg = sgr()
                        P.act(sg[:, 0:n], psg[:, 0:n], AF.Sigmoid)
                        psp = P.psum()
                        for kc in range(nk):
                            P.mm(psp[:, 0:n], wb[:, k0 + kc, oc * 128:(oc + 1) * 128], yt[:, kc, t0:t0 + n],
                                 start=(kc == 0), stop=(kc == nk - 1))
                        if br == 0:
                            P.tt(acc[:, 0:n], sg[:, 0:n], psp[:, 0:n], ALU.mult)
                        else:
                            P.tt(sg[:, 0:n], sg[:, 0:n], psp[:, 0:n], ALU.mult)
                            if br == 1:
                                P.tt(acc[:, 0:n], acc[:, 0:n], sg[:, 0:n], ALU.add)
                            else:
                                P.tt(mixpre[:, oc, t0:t0 + n], acc[:, 0:n], sg[:, 0:n], ALU.add)
            P.barrier()
        with ExitStack() as es:
            wo = P.sb("wo", [128, 8, 1024], BF16, es)
            src = W["w_out"][l]
            P.dma(wo, src.v(src.ap.rearrange("(kc p) n -> p kc n", p=128)), q="pool")
            gm_l, _ = B["gains"]("gml", l, s, "g_mpost", 2, es)
            gm_c, _ = B["gains"]("gmc", l, 4, "g_mpost", 2, es)
            mo = ring("mmo", [128, 8, 512], F32, 1, es)
            hb = ring("mhb", [128, 8, 512], F32, 1, es)
            sq = ring("msq", [128, 8, 512], BF16, 1, es)
            rs = ring("mrs", [128, 512], F32, 2, es)
            for (t0, n) in blks:
                h = hb()
                P.dma(h[:, :, 0:n], hb_d[:, :, t0:t0 + n])
                m = mo()
                for oc in range(8):
                    ps = P.psum()
                    for kc in range(8):
                        P.mm(ps[:, 0:n], wo[:, kc, oc * 128:(oc + 1) * 128], mixpre[:, kc, t0:t0 + n], start=(kc == 0), stop=(kc == 7))
                    P.copy(m[:, oc, 0:n], ps[:, 0:n], e=("act" if oc % 2 else "dve"))
                r = B["norm_stats"](m[:, :, 0:n], n, (sq, rs))
                P.tt(m[:, :, 0:n], m[:, :, 0:n], B["bc3"](r[:, 0:n], n, 8), ALU.mult)
                gm = gm_c if t0 == 0 else gm_l
                for c in range(8):
                    P.stt(h[:, c, 0:n], m[:, c, 0:n], gm[:, c:c + 1], h[:, c, 0:n], ALU.mult, ALU.add)
                P.dma(hb_d[:, :, t0:t0 + n], h[:, :, 0:n], q="pool")
            P.barrier()


def stage_att(B, l, s, aT, yaT):
    P, ring, C, V = B["P"], B["ring"], B["C"], B["V"]
    need_ctx = l < B["nlayers"] - 1
    amask, ident_bf = B["amask"], B["ident_bf"]
    with ExitStack() as es:
        ropec = P.sb("ropec", [128, T], F32, es)
        P.dma(ropec, C["ropec"])
        ropes = P.sb("ropes", [128, T], F32, es)
        P.dma(ropes, C["ropes"])
        pm = P.sb("pm", [128, 128], F32, es)
        P.dma(pm, C["pm"])
        srcv = _wsrc(B, l)
        wq = P.sb("wq", [128, 8, 512], BF16, es)
        P.dma(wq, srcv[:, :, C_Q:C_Q + 512], q="pool")
        wkd = P.sb("wkd", [128, 8, 2, 128], BF16, es)
        for kv in range(2):
            for dup in range(2):
                P.dma(wkd[:, :, kv, dup * 64:(dup + 1) * 64], srcv[:, :, C_K + kv * 64:C_K + (kv + 1) * 64], q="pool")
        wv = P.sb("wv", [128, 8, 128], BF16, es)
        P.dma(wv, srcv[:, :, C_V:C_V + 128], q="pool")
        qrT = P.sb("qrT", [128, 4, TT], BF16, es)
        kdT = P.sb("kdT", [128, 2, TT], BF16, es)
        Vaug = P.sb("Vaug", [128, 18, 2, 65], BF16, es)
        P.memset(Vaug[:, :, :, 64:65], 1.0)
        esink = P.sb("esink", [128, 8], F32, es)
        P.act(esink, V(l, "sink"), AF.Exp)
        qsb = ring("qsb", [128, 512], F32, 2, es)
        t1 = ring("ropet1", [128, 512], F32, 2, es)
        t2 = ring("ropet2", [128, 512], F32, 2, es)
        for (t0, n) in B["ALL_BLKS"]:
            isctx = (t0 == 0)
            for m in range(6):
                if isctx and m < 4 and not need_ctx:
                    continue
                ps = P.psum()
                for kc in range(8):
                    lhsT = wq[:, kc, m * 128:(m + 1) * 128] if m < 4 else wkd[:, kc, m - 4, :]
                    P.mm(ps[:, 0:n], lhsT, aT[:, kc, t0:t0 + n], start=(kc == 0), stop=(kc == 7))
                dst = qrT[:, m, t0:t0 + n] if m < 4 else kdT[:, m - 4, t0:t0 + n]
                if isctx:
                    P.copy(dst, ps[:, 0:n], e="act")
                else:
                    q = qsb()
                    P.copy(q[:, 0:n], ps[:, 0:n], e="act")
                    ps2 = P.psum()
                    P.mm(ps2[:, 0:n], pm, q[:, 0:n])
                    p0 = t0 - NCTX
                    a = t1()
                    P.tt(a[:, 0:n], q[:, 0:n], ropec[:, p0:p0 + n], ALU.mult)
                    b = t2()
                    P.tt(b[:, 0:n], ps2[:, 0:n], ropes[:, p0:p0 + n], ALU.mult)
                    P.tt(dst, a[:, 0:n], b[:, 0:n], ALU.add)
            for tt in range(n // 128):
                ti = t0 // 128 + tt
                ps = P.psum()
                for kc in range(8):
                    P.mm(ps[:, 0:128], aT[:, kc, t0 + tt * 128:t0 + (tt + 1) * 128], wv[:, kc, :], start=(kc == 0), stop=(kc == 7))
                P.copy(Vaug[:, ti, :, 0:64], ps.v(ps.ap[:, 0:128].rearrange("p (a b) -> p a b", b=64)), e="dve")
        if s == 0:
            P.dump(f"qrT{l}", qrT, [128, 4, TT], BF16)
            P.dump(f"kdT{l}", kdT, [128, 2, TT], BF16)
            P.dump(f"Vaug{l}", Vaug, [128, 18, 2, 65], BF16)
        PTr = ring("PT", [128, 5, 128], BF16, 3, es)
        yat = ring("yatok", [128, 8, 64], BF16, 2, es)
        den = ring("den", [128, 8], F32, 2, es)
        qblocks = ([("c", 0), ("c", 1)] if need_ctx else []) + [("l", i) for i in range(16)]
        cnt = {"pss": 0, "pso": 0, "pst": 0}
        for kind, i in qblocks:
            q0 = i * 128 if kind == "c" else NCTX + i * 128
            slots = [(0, 0), (1, 1)]
            if kind == "l":
                if i > 0:
                    slots.append((2, 2 + i - 1))
                slots.append((3, 2 + i))
                if i < 15:
                    slots.append((4, 2 + i + 1))
            ya = yat()
            dn = den()
            for kvg in range(2):
                pso = P.psum_at(4 + cnt["pso"] % 2)
                cnt["pso"] += 1
                pso_v = pso.v(pso.ap[:, 0:260].rearrange("p (h d) -> p h d", d=65))
                for hh in range(4):
                    h = kvg * 4 + hh
                    m = h // 2
                    po = 64 * (h % 2)
                    pss = P.psum_at(2 * (cnt["pss"] % 2), 2)
                    cnt["pss"] += 1
                    pss_v = pss.v(pss.ap.rearrange("p a (b n) -> p (a b) n", n=128))
                    for (sl, kt) in slots:
                        P.mm(pss_v[:, sl, :], kdT[po:po + 64, kvg, kt * 128:(kt + 1) * 128], qrT[po:po + 64, m, q0:q0 + 128])
                    pt = PTr()
                    P.act(pt, pss_v[:, 0:5, :], AF.Exp, scale=0.125)
                    if kind == "l":
                        P.tt(pt, pt, amask, ALU.mult, e="pool")
                    for si, (sl, kt) in enumerate(slots):
                        P.mm(pso_v[:, hh, :], pt[:, sl, :], Vaug[:, kt, kvg, :], start=(si == 0), stop=(si == len(slots) - 1))
                dn4 = dn[:, kvg * 4:(kvg + 1) * 4]
                P.tt(dn4, pso_v[:, :, 64], esink[:, kvg * 4:(kvg + 1) * 4], ALU.add)
                P.op("dve", lambda en: en.reciprocal(dn4.ap, dn4.ap), r=[dn4], w=[dn4])
                P.tt(ya[:, kvg * 4:(kvg + 1) * 4, :], pso_v[:, :, 0:64],
                     Tile(dn4.ap.unsqueeze(2).to_broadcast([128, 4, 64]), dn4.bufs), ALU.mult)
            pst = P.psum_at(6 + cnt["pst"] % 2)
            cnt["pst"] += 1
            pst_bf = pst.v(pst.ap.bitcast(BF16))
            ya_f = ya.v(ya.ap.rearrange("p h d -> p (h d)"))
            for m in range(4):
                P.tr(pst_bf[:, m * 128:(m + 1) * 128], ya_f[:, m * 128:(m + 1) * 128], ident_bf)
            P.copy(yaT[:, :, q0:q0 + 128], pst_bf.v(pst_bf.ap[:, 0:512].rearrange("p (a b) -> p a b", b=128)), e="act")
        P.barrier()


NLOCK = 2


def stage_rwkv(B, l, s, aT, yrT):
    P, ring, W, V = B["P"], B["ring"], B["W"], B["V"]
    nlayers = B["nlayers"]
    need_ctx = l < nlayers - 1
    idb, blockones, rst, vfT = B["idb"], B["blockones"], B["rst"], B["vfT"]
    rmask2, rmask5, rmaskr = B["rmask2"], B["rmask5"], B["rmaskr"]
    with ExitStack() as es:
        w_r = P.sb("w_r", [128, 8, 1152], BF16, es)
        srcv = _wsrc(B, l)
        P.dma(w_r[:, :, 0:576], srcv[:, :, 0:576], q="pool")
        P.dma(w_r[:, :, 576:1152], srcv[:, :, 576:1152], q="pool")
        w2sb = P.sb("w2sb", [128, 256], F32, es)
        P.dma(w2sb, W["rwkv_w2"][l].v(W["rwkv_w2"][l].ap.rearrange("d k n -> (d k) n")))
        a2sb = P.sb("a2sb", [128, 256], F32, es)
        P.dma(a2sb, W["rwkv_a2"][l].v(W["rwkv_a2"][l].ap.rearrange("d k n -> (d k) n")))
        g2sb = P.sb("g2sb", [128, 256], F32, es)
        P.dma(g2sb, W["rwkv_g2"][l])
        if l > 0:
            v1sb = P.sb("v1sb", [128, 2, 32], F32, es)
            P.dma(v1sb, W["rwkv_v1"][l - 1].v(W["rwkv_v1"][l - 1].ap.rearrange("(g p) m -> p g m", p=128)))
            v2sb = P.sb("v2sb", [32, 256], F32, es)
            P.dma(v2sb, W["rwkv_v2"][l - 1])
        mu0, mu1 = V(l, "mu0"), V(l, "mu1")
        k_k, k_a, r_k, lnx_w, lnx_b, v0 = V(l, "k_k"), V(l, "k_a"), V(l, "r_k"), V(l, "lnx_w"), V(l, "lnx_b"), V(l, "v0")
        w0 = [V(l, "w0f"), V(l, "w0b")]
        a0 = [V(l, "a0f"), V(l, "a0b")]
        omka = P.sb("omka", [128, 2], F32, es)
        P.ts(omka, k_a, -1.0, 1.0, op0=ALU.mult, op1=ALU.add)
        kah = P.sb("kah", [128, 2], F32, es)
        P.ts(kah, k_a, 0.5, None, op0=ALU.mult)
        yfwd = P.sb("yfwd", [128, 2, TT], BF16, es)
        u_blk = P.sb("u_blk", [128, 9, 258], F32, es)
        us = P.sb("us", [128, 9, 256], F32, es)
        dtm = ring("dtm", [128, 256], F32, 2, es)
        A = {}
        for nm in ("vv", "kap", "sg", "aa", "aaf", "key", "bb", "cs", "cs2", "e1", "e2", "eex", "e3", "rt", "kt", "bh",
                   "kh", "bck", "kck", "tmp1", "tmp2"):
            A[nm] = P.sb("r_" + nm, [128, 2, 256], BF16 if nm in ("rt", "kt", "bh", "kh", "bck", "kck") else F32, es)
        idbb = P.sb("idb_bf", [128, 64], BF16, es)
        P.copy(idbb, idb)
        th = P.sb("r_th", [128, 256], F32, es)
        sgx = P.sb("r_sgx", [128, 256], F32, es)
        vv1 = P.sb("r_vv1", [32, 256], F32, es)
        etot = P.sb("r_etot", [128, 2, 4], F32, es)
        tot8 = P.sb("r_tot8", [128, 2, 4], F32, es)
        XA = [P.sb(f"XA{i}", [64, 4, 256], BF16, es) for i in range(NLOCK)]
        Wb = [P.sb(f"Wb{i}", [64, 4, 128], BF16, es) for i in range(NLOCK)]
        W32 = [P.sb(f"W32{i}", [64, 4, 128], F32, es) for i in range(NLOCK)]
        Pw = [P.sb(f"Pw{i}", [64, 4, 128], BF16, es) for i in range(NLOCK)]
        RB = [P.sb(f"RB{i}", [64, 4, 128], BF16, es) for i in range(NLOCK)]
        Rt = [P.sb(f"Rt{i}", [128, 4, 128], F32, es) for i in range(NLOCK)]
        Dg = [P.sb(f"Dg{i}", [128, 2, 64], BF16, es) for i in range(NLOCK)]
        Zr = []
        for i in range(8):
            z = P.sb(f"Zr{i}", [128, 4, 64], F32, es)
            zs = Tile(z.ap[0:64], [Buf(f"Zs{i}")])
            zv = Tile(z.ap[64:128], [Buf(f"Zv{i}")])
            Zr.append((Tile(z.ap, zs.bufs + zv.bufs), zs, zv))
        pa = {"i": 0}

        def psA():
            pa["i"] += 1
            return P.psum_at(pa["i"] % 2)

        def fm(arr, h, c):
            po = 64 * (h % 2)
            return arr[po:po + 64, h // 2, c * 64:(c + 1) * 64]

        def v4(ps, n):
            return ps.v(ps.ap.rearrange("p (h n) -> p h n", n=n))

        def prep(bi, d, full):
            t0 = 256 * bi
            seq_lo, seq_hi = (0, NCTX) if bi == 0 else (NCTX, TT)
            lo, hi = max(t0 - 1, seq_lo), min(t0 + 257, seq_hi)
            c0 = lo - (t0 - 1)
            ncol = hi - lo
            if c0 == 1:
                P.memset(u_blk[:, :, 0:1], 0.0)
            if c0 + ncol < 258:
                P.memset(u_blk[:, :, 257:258], 0.0)
            for j in range(9):
                ps = psA()
                for kc in range(8):
                    P.mm(ps[:, 0:ncol], w_r[:, kc, j * 128:(j + 1) * 128], aT[:, kc, lo:hi], start=(kc == 0), stop=(kc == 7))
                P.copy(u_blk[:, j, c0:c0 + ncol], ps[:, 0:ncol], e=("act" if j % 2 else "dve"))
            for j in range(9):
                d0 = dtm()
                P.tt(d0, u_blk[:, j, 0:256], u_blk[:, j, 1:257], ALU.subtract)
                P.stt(us[:, j, :], d0, mu0[:, j:j + 1], u_blk[:, j, 1:257], ALU.mult, ALU.add)
                d1 = dtm()
                P.tt(d1, u_blk[:, j, 2:258], u_blk[:, j, 1:257], ALU.subtract)
                P.stt(us[:, j, :], d1, mu1[:, j:j + 1], us[:, j, :], ALU.mult, ALU.add)
            r_, k_, v_ = us[:, 0:2, :], us[:, 2:4, :], us[:, 4:6, :]
            vv = A["vv"]
            if l == 0:
                P.copy(vv, v_, e="act")
                if d == 0:
                    P.copy(vfT[:, :, t0:t0 + 256], v_, e="act")
            else:
                ps = psA()
                for g in range(2):
                    P.mm(ps[0:32, 0:256], v1sb[:, g, :], us[:, 4 + g, :], start=(g == 0), stop=(g == 1))
                P.copy(vv1, ps[0:32, 0:256], e="act")
                for g in range(2):
                    ps = psA()
                    P.mm(ps[:, 0:256], v2sb[:, g * 128:(g + 1) * 128], vv1)
                    P.act(A["tmp1"][:, g, :], ps[:, 0:256], AF.Sigmoid, bias=v0[:, g:g + 1])
                P.tt(A["tmp2"], vfT[:, :, t0:t0 + 256], v_, ALU.subtract)
                P.tt(A["tmp2"], A["tmp2"], A["tmp1"], ALU.mult)
                P.tt(vv, v_, A["tmp2"], ALU.add)
            kap = A["kap"]
            for g in range(2):
                P.ts(kap[:, g, :], k_[:, g, :], k_k[:, g:g + 1], None, op0=ALU.mult)
            P.tt(A["tmp1"], kap, kap, ALU.mult)
            for g in range(2):
                ps = psA()
                P.mm(ps[:, 0:256], blockones, A["tmp1"][:, g, :])
                P.ts(A["tmp2"][:, g, :], ps[:, 0:256], 1e-24, None, op0=ALU.max)
            P.rsqrt(A["tmp2"], A["tmp2"], 0.0)
            P.tt(kap, kap, A["tmp2"], ALU.mult)
            P.act(th, us[:, 6, :], AF.Tanh)
            sg, aa = A["sg"], A["aa"]
            for g in range(2):
                ps = psA()
                P.mm(ps[:, 0:256], w2sb[64 * d:64 * d + 64, g * 128:(g + 1) * 128], th[64 * d:64 * d + 64, :])
                P.act(sg[:, g, :], ps[:, 0:256], AF.Sigmoid, bias=w0[d][:, g:g + 1])
            for g in range(2):
                ps = psA()
                P.mm(ps[:, 0:256], a2sb[64 * d:64 * d + 64, g * 128:(g + 1) * 128], us[64 * d:64 * d + 64, 7, :])
                P.act(aa[:, g, :], ps[:, 0:256], AF.Sigmoid, bias=a0[d][:, g:g + 1])
            if full:
                for g in range(2):
                    ps = psA()
                    P.mm(ps[:, 0:256], a2sb[0:64, g * 128:(g + 1) * 128], us[0:64, 7, :])
                    P.act(A["aaf"][:, g, :], ps[:, 0:256], AF.Sigmoid, bias=a0[0][:, g:g + 1])
            key, bb = A["key"], A["bb"]
            for g in range(2):
                P.ts(A["tmp1"][:, g, :], aa[:, g, :], k_a[:, g:g + 1], omka[:, g:g + 1], op0=ALU.mult, op1=ALU.add)
            P.tt(key, k_, A["tmp1"], ALU.mult)
            P.tt(bb, kap, aa, ALU.mult)
            cs = A["cs"]
            csf = cs.v(cs.ap.rearrange("p g t -> p (g t)"))
            sgf = sg.v(sg.ap.rearrange("p g t -> p (g t)"))
            P.op("dve", lambda en: en.tensor_tensor_scan(csf.ap, rst.ap, sgf.ap, 0.0, ALU.mult, ALU.add), r=[rst, sg], w=[cs])
            cs4 = cs.v(cs.ap.rearrange("p g (c t) -> p g c t", t=64))
            P.copy(tot8, cs4[:, :, :, 63], e="dve")
            totb = Tile(tot8.ap.unsqueeze(3).to_broadcast([128, 2, 4, 64]), tot8.bufs)
            if d == 1:
                c2 = A["cs2"]
                c24 = c2.v(c2.ap.rearrange("p g (c t) -> p g c t", t=64))
                P.tt(c24, totb, cs4, ALU.subtract)
                P.tt(c2, c2, sg, ALU.add)
                cs = c2
                cs4 = c24
            P.act(etot, tot8, AF.Exp, scale=-DEC)
            P.act(A["e1"], cs, AF.Exp, scale=-DEC)
            P.act(A["e2"], cs, AF.Exp, scale=DEC)
            P.tt(A["tmp1"], cs, sg, ALU.subtract)
            P.act(A["eex"], A["tmp1"], AF.Exp, scale=-DEC)
            t24 = A["tmp2"].v(A["tmp2"].ap.rearrange("p g (c t) -> p g c t", t=64))
            P.tt(t24, totb, cs4, ALU.subtract)
            P.act(A["e3"], A["tmp2"], AF.Exp, scale=-DEC)
            P.tt(A["rt"], r_, A["e1"], ALU.mult)
            P.tt(A["kt"], kap, A["eex"], ALU.mult)
            P.tt(A["bh"], bb, A["e2"], ALU.mult)
            P.tt(A["kh"], key, A["e2"], ALU.mult)
            P.tt(A["bck"], bb, A["e3"], ALU.mult)
            P.tt(A["kck"], key, A["e3"], ALU.mult)

        def hs(h):
            return (h % 2) * 2 + h // 2

        def pair(ps, n):
            return ps.v(ps.ap[:, :, 0:2 * n].rearrange("p a (b n) -> p a b n", n=n))

        def sv(t, n0=None, n1=None):
            ap = t.ap if n0 is None else t.ap[:, :, n0:n1]
            return t.v(ap.rearrange("p (a b) n -> p a b n", a=2))

        def bc4(m, d, p, n):
            return Tile(m.ap[:, d, :].unsqueeze(1).unsqueeze(1).to_broadcast([p, 2, 2, n]), m.bufs)

        def machinery(sl, c, d, zk):
            rt, kt, bh, kh, bck, kck, vv = A["rt"], A["kt"], A["bh"], A["kh"], A["bck"], A["kck"], A["vv"]
            b0 = 2 + 2 * (sl % 2)
            ps = P.psum_at(b0, 2)
            pv = pair(ps, 256)
            for h in range(4):
                po, a, b = 64 * (h % 2), h % 2, h // 2
                P.mm(pv[0:64, a, b, 0:64], fm(kt, h, c), idbb[po:po + 64, :])
                P.mm(pv[0:64, a, b, 64:128], fm(kt, h, c), fm(kh, h, c))
                P.mm(pv[0:64, a, b, 128:192], fm(kt, h, c), fm(bh, h, c))
                P.mm(pv[0:64, a, b, 192:256], fm(bh, h, c), fm(kt, h, c))
            P.tt(sv(XA[sl]), pv[0:64], bc4(rmask2, d, 64, 256), ALU.mult)
            yield
            ps = P.psum_at(b0, 2)
            pv = pair(ps, 128)
            for h in range(4):
                po, a, b = 64 * (h % 2), h % 2, h // 2
                P.mm(pv[0:64, a, b, 0:64], fm(bh, h, c), fm(rt, h, c))
                P.mm(pv[0:64, a, b, 64:128], fm(bck, h, c), idbb[po:po + 64, :])
            P.tt(sv(RB[sl]), pv[0:64], bc4(rmask5, d, 64, 128), ALU.mult)
            yield
            ps = P.psum_at(b0)
            pv = v4(ps, 128)
            for h in range(4):
                P.mm(pv[0:64, hs(h), :], XA[sl][:, hs(h), 192:256], XA[sl][:, hs(h), 0:128])
            P.tt(W32[sl], XA[sl][:, :, 0:128], pv[0:64], ALU.subtract)
            P.copy(Wb[sl], W32[sl], e="act")
            yield
            for g in range(2):
                P.ts(Dg[sl][:, g, :], idb, etot[:, g, c:c + 1], None, op0=ALU.mult)
            ps = P.psum_at(b0, 2)
            pv = pair(ps, 128)
            for h in range(4):
                po, a, b = 64 * (h % 2), h % 2, h // 2
                P.mm(pv[0:64, a, b, 0:64], idbb[po:po + 64, :], fm(rt, h, c))
                P.mm(pv[0:64, a, b, 64:128], idbb[po:po + 64, :], Dg[sl][po:po + 64, h // 2, :])
                P.mm(pv[64:128, a, b, 0:64], fm(kh, h, c), fm(rt, h, c))
                P.mm(pv[64:128, a, b, 64:128], fm(kck, h, c), idbb[po:po + 64, :])
            P.tt(sv(Rt[sl]), pv, bc4(rmaskr, d, 128, 128), ALU.mult)
            yield
            for lev in range(5):
                ps = P.psum_at(b0 + 1)
                pv = v4(ps, 128)
                for h in range(4):
                    q = hs(h)
                    if lev == 0:
                        Ah, ATh = XA[sl][:, q, 128:192], XA[sl][:, q, 192:256]
                    else:
                        Ah, ATh = Pw[sl][:, q, 0:64], Pw[sl][:, q, 64:128]
                    P.mm(pv[0:64, q, 0:64], ATh, Ah)
                    P.mm(pv[0:64, q, 64:128], Ah, ATh)
                P.copy(Pw[sl], pv[0:64], e="act")
                yield
                ps = P.psum_at(b0)
                pv = v4(ps, 128)
                for h in range(4):
                    q = hs(h)
                    P.mm(pv[0:64, q, :], Pw[sl][:, q, 64:128], Wb[sl][:, q, :])
                P.tt(W32[sl], W32[sl], pv[0:64], ALU.add)
                P.copy(Wb[sl], W32[sl], e="act")
                yield
                if lev == 0:
                    ps = P.psum_at(b0, 2)
                    pv = pair(ps, 64)
                    for h in range(4):
                        po, a, b = 64 * (h % 2), h % 2, h // 2
                        P.mm(pv[64:128, a, b, :], fm(vv, h, c), idb[po:po + 64, :])
                    zv = Zr[zk % 8][2]
                    P.copy(zv.v(zv.ap.rearrange("p (a b) n -> p a b n", a=2)), pv[64:128], e="act")
                    yield
            ps = P.psum_at(b0 + 1)
            pv = v4(ps, 128)
            for h in range(4):
                q = hs(h)
                P.mm(pv[:, q, :], Wb[sl][:, q, :], RB[sl][:, q, :])
            P.tt(Rt[sl], Rt[sl], pv, ALU.subtract)
            yield

        def sequential(sl, c, d, zk, t0, emit_y, ysum):
            Z = Zr[zk % 8][0]
            if emit_y:
                ps = P.psum_at(6)
                pv = ps.v(ps.ap[:, 0:128].rearrange("p (g n) -> p g n", n=64))
                for h in range(4):
                    po = 64 * (h % 2)
                    P.mm(pv[po:po + 64, h // 2, :], Z[:, hs(h), :], Rt[sl][:, hs(h), 0:64])
                tk = t0 + c * 64
                if d == 0:
                    P.copy(yfwd[:, :, tk:tk + 64], pv, e="act")
                else:
                    P.tt(ysum[:, :, c * 64:(c + 1) * 64], pv, yfwd[:, :, tk:tk + 64], ALU.add)
            ps = P.psum_at(7)
            pv = ps.v(ps.ap[:, 0:256].rearrange("p (h n) -> p h n", n=64))
            for h in range(4):
                P.mm(pv[0:64, hs(h), :], Rt[sl][:, hs(h), 64:128], Z[:, hs(h), :])
            P.copy(Zr[(zk + 1) % 8][1], pv[0:64], e="dve")

        def readout(bi):
            t0 = 256 * bi
            r_, k_ = us[:, 0:2, :], us[:, 2:4, :]
            ysum, yc, sq, rstd = A["e1"], A["e2"], A["eex"], A["e3"]
            for g in range(2):
                ps = psA()
                P.mm(ps[:, 0:256], blockones, ysum[:, g, :])
                P.stt(yc[:, g, :], ps[:, 0:256], -1.0 / 64, ysum[:, g, :], ALU.mult, ALU.add)
            P.tt(sq, yc, yc, ALU.mult)
            for g in range(2):
                ps = psA()
                P.mm(ps[:, 0:256], blockones, sq[:, g, :])
                P.ts(rstd[:, g, :], ps[:, 0:256], 1.0 / 64, None, op0=ALU.mult)
            P.rsqrt(rstd, rstd, LNX_EPS)
            P.tt(yc, yc, rstd, ALU.mult)
            for g in range(2):
                P.ts(yc[:, g, :], yc[:, g, :], lnx_w[:, g:g + 1], lnx_b[:, g:g + 1], op0=ALU.mult, op1=ALU.add)
            tk = A["tmp1"]
            P.tt(tk, A["aa"], A["aaf"], ALU.add)
            for g in range(2):
                P.ts(tk[:, g, :], tk[:, g, :], kah[:, g:g + 1], omka[:, g:g + 1], op0=ALU.mult, op1=ALU.add)
            rk = A["tmp2"]
            P.tt(rk, r_, k_, ALU.mult)
            P.tt(rk, rk, tk, ALU.mult)
            for g in range(2):
                P.ts(rk[:, g, :], rk[:, g, :], r_k[:, g:g + 1], None, op0=ALU.mult)
            for g in range(2):
                ps = psA()
                P.mm(ps[:, 0:256], blockones, rk[:, g, :])
                P.tt(sq[:, g, :], ps[:, 0:256], A["vv"][:, g, :], ALU.mult)
            P.tt(yc, yc, sq, ALU.add)
            P.act(sgx, us[:, 8, :], AF.Sigmoid)
            for g in range(2):
                ps = psA()
                P.mm(ps[:, 0:256], g2sb[:, g * 128:(g + 1) * 128], sgx)
                P.tt(yrT[:, g, t0:t0 + 256], yc[:, g, :], ps[:, 0:256], ALU.mult)

        nblk = TT // 256
        import os
        RWC = int(os.environ.get("RW_CUT", "99"))
        if RWC < 99:
            prep(0, 0, False)
            if RWC >= 2:
                list(machinery(0, 0, 0, 0))
            if RWC >= 3:
                sequential(0, 0, 0, 0, 0, True, A["e1"])
            if RWC >= 4:
                prep(1, 1, True)
                list(machinery(0, 0, 1, 0))
                sequential(0, 0, 1, 0, 256, True, A["e1"])
                readout(1)
            P.barrier()
            return
        for d in range(2):
            order = list(range(nblk)) if d == 0 else [0] + list(range(nblk - 1, 0, -1))
            zk = 0
            P.memset(Zr[0][1], 0.0)
            for bi in order:
                emit_y = need_ctx or bi > 0
                prep(bi, d, full=(d == 1))
                chunks = [0, 1, 2, 3] if d == 0 else [3, 2, 1, 0]
                for g0 in range(0, 4, NLOCK):
                    grp = chunks[g0:g0 + NLOCK]
                    gens = [machinery(sl, c, d, zk + sl) for sl, c in enumerate(grp)]
                    while gens:
                        for gn in list(gens):
                            try:
                                next(gn)
                            except StopIteration:
                                gens.remove(gn)
                    for sl, c in enumerate(grp):
                        sequential(sl, c, d, zk + sl, 256 * bi, emit_y, A["e1"])
                    zk += len(grp)
                if d == 1 and emit_y:
                    readout(bi)
        P.barrier()


def pack_vecs(inp):
    v = np.zeros((128, 2, NV), np.float32)

    def put(l, name, arr):
        o, c = VO[name]
        a = np.asarray(arr, np.float32).reshape(c, 128).T
        v[:, l, o:o + c] = a

    for l in range(2):
        put(l, "modb", inp["mod_b"][l])
        put(l, "g_mpre", inp["norm_mix_pre"][l])
        put(l, "g_mpost", inp["norm_mix_post"][l])
        put(l, "g_fpre", inp["norm_ffn_pre"][l])
        put(l, "g_fpost", inp["norm_ffn_post"][l])
        put(l, "mu0", inp["rwkv_mu"][l, 0])
        put(l, "mu1", inp["rwkv_mu"][l, 1])
        put(l, "w0f", inp["rwkv_w0"][l, 0])
        put(l, "w0b", inp["rwkv_w0"][l, 1])
        put(l, "a0f", inp["rwkv_a0"][l, 0])
        put(l, "a0b", inp["rwkv_a0"][l, 1])
        put(l, "k_k", inp["rwkv_k_k"][l])
        put(l, "k_a", inp["rwkv_k_a"][l])
        put(l, "r_k", inp["rwkv_r_k"][l])
        put(l, "lnx_w", inp["rwkv_lnx_w"][l])
        put(l, "lnx_b", inp["rwkv_lnx_b"][l])
        if l > 0:
            put(l, "v0", inp["rwkv_v0"][l - 1])
        put(l, "cw0", inp["ffn_conv_w"][l, 0])
        put(l, "cw1", inp["ffn_conv_w"][l, 1])
        put(l, "cw2", inp["ffn_conv_w"][l, 2])
        put(l, "cb", inp["ffn_conv_b"][l])
        o, c = VO["sink"]
        v[:, l, o:o + c] = np.broadcast_to(np.asarray(inp["attn_sink"][l], np.float32)[None, :], (128, 8))
    return v


def make_in_maps(inp, ncores=8):
    global HC
    if HC is None:
        HC = host_consts()
    vecs = pack_vecs(inp)
    shared = {k: np.ascontiguousarray(np.asarray(inp[k], np.float32)) for k in WEIGHT_SHAPES}
    shared.update(HC)
    shared["vecs"] = vecs
    maps = []
    c = np.asarray(inp["c"], np.float32)
    cc = np.asarray(inp["c_ctx"], np.float32)
    for ci in range(ncores):
        b0 = ci * NSEQ
        cs = np.zeros((128, 8, 5), np.float32)
        for j in range(NSEQ):
            cs[:, :, j] = c[b0 + j].reshape(8, 128).T
        cs[:, :, 4] = cc.reshape(8, 128).T
        m = dict(shared)
        m["x"] = np.ascontiguousarray(np.asarray(inp["x"][b0:b0 + NSEQ], np.float32))
        m["ctx"] = np.ascontiguousarray(np.asarray(inp["ctx"][b0:b0 + NSEQ], np.float32))
        m["cs"] = cs
        maps.append(m)
    return maps


DEFAULT_STAGES = ("rwkv", "att", "fou", "merge", "ffn")


def kernel(**inputs):
    nc, P = build_program(stages=DEFAULT_STAGES)
    maps = make_in_maps(inputs)
    res = run_bass_kernel_spmd(nc, maps, core_ids=list(range(8)))
    return np.concatenate([np.asarray(r["y"], np.float32) for r in res.results], axis=0)
```

```python
import numpy as np
from contextlib import ExitStack
import ml_dtypes
import concourse.bass as bass
import concourse.mybir as mybir
from concourse.bass_utils import run_bass_kernel_spmd

F32 = mybir.dt.float32
BF16 = mybir.dt.bfloat16
AF = mybir.ActivationFunctionType
ALU = mybir.AluOpType

EPOCH = 30000
NDMA = 8
SAME_ENGINE_SYNC = True

D = 1024
T = 2048
NCTX = 256
TT = T + NCTX
DFF = 2816
NSEQ = 4
EPS = 1e-6
LNX_EPS = 64e-5
DEC = 0.6065306597126334
C_R, C_Q, C_K, C_V, C_F, C_G = 0, 1152, 1664, 1792, 1920, 2176

VO = {}
_o = 0
for _n, _c in (("modb", 48), ("g_mpre", 8), ("g_mpost", 8), ("g_fpre", 8), ("g_fpost", 8), ("mu0", 9), ("mu1", 9),
               ("w0f", 2), ("w0b", 2), ("a0f", 2), ("a0b", 2), ("k_k", 2), ("k_a", 2), ("r_k", 2), ("lnx_w", 2),
               ("lnx_b", 2), ("v0", 2), ("cw0", 22), ("cw1", 22), ("cw2", 22), ("cb", 22), ("sink", 8)):
    VO[_n] = (_o, _c)
    _o += _c
NV = _o


class Buf:
    __slots__ = ("name", "w", "rd")

    def __init__(self, name):
        self.name = name
        self.w = None
        self.rd = {}


class Tile:
    __slots__ = ("ap", "bufs")

    def __init__(self, ap, bufs):
        self.ap = ap
        self.bufs = bufs

    def __getitem__(self, idx):
        return Tile(self.ap[idx], self.bufs)

    def v(self, ap):
        return Tile(ap, self.bufs)

    def bc(self, shape):
        return Tile(self.ap.to_broadcast(list(shape)), self.bufs)


class Prog:
    def __init__(self, nc, dbg=()):
        self.nc = nc
        self.es = ExitStack()
        self.eng = {"pe": nc.tensor, "act": nc.scalar, "dve": nc.vector, "pool": nc.gpsimd, "sp": nc.sync}
        self.cnt = {e: 0 for e in self.eng}
        self.esem = {e: [] for e in self.eng}
        self.wait_e = {e: {} for e in self.eng}
        self.wait_d = {e: {} for e in self.eng}
        self.dsem = {}
        self.duse = {}
        self.dnext = {e: 0 for e in self.eng}
        self.nbuf = 0
        self.psum_i = 0
        self.ninstr = 0
        self.dbg = set(dbg)
        self.dumps = {}

    def sem(self, name):
        return self.es.enter_context(self.nc.semaphore(name))

    def sb(self, name, shape, dtype, es=None):
        self.nbuf += 1
        t = (es or self.es).enter_context(self.nc.sbuf_tensor(f"{name}_s{self.nbuf}", list(shape), dtype))
        return Tile(t[:], [Buf(name)])

    def dram(self, name, shape, dtype, kind="Internal"):
        t = self.nc.dram_tensor(name, list(shape), dtype, kind=kind)
        return Tile(t.ap(), [Buf(name)])

    def sub(self, tile, idx):
        self.nbuf += 1
        return Tile(tile.ap[idx], [Buf(f"sub{self.nbuf}")])

    def init_psum(self):
        t = self.es.enter_context(self.nc.psum_tensor("psall", [128, 8, 512], F32))
        self.ps_all = t[:]
        self.ps_bufs = [Buf(f"psb{i}") for i in range(8)]

    def psum(self):
        i = self.psum_i % 8
        self.psum_i += 1
        return Tile(self.ps_all[:, i, :], [self.ps_bufs[i]])

    def psum_at(self, i, n=1):
        if n == 1:
            return Tile(self.ps_all[:, i, :], [self.ps_bufs[i]])
        return Tile(self.ps_all[:, i:i + n, :], self.ps_bufs[i:i + n])

    def psum2(self):
        if self.psum_i % 2:
            self.psum_i += 1
        i = self.psum_i % 8
        self.psum_i += 2
        return Tile(self.ps_all[:, i:i + 2, :], [self.ps_bufs[i], self.ps_bufs[i + 1]])

    def dump(self, name, tile, shape, dtype=F32):
        if name not in self.dbg:
            return
        k = f"dbg_{name}_{len(self.dumps)}"
        d = self.dram(k, shape, dtype, kind="ExternalOutput")
        self.dumps[k] = name
        self.dma(d, tile, q="sp")

    def _esem(self, e, epoch):
        while len(self.esem[e]) <= epoch:
            self.esem[e].append(self.sem(f"s_{e}_{len(self.esem[e])}"))
        return self.esem[e][epoch]

    def _wait(self, f, ev):
        if ev is None:
            return
        if ev[0] == "e":
            _, e, idx = ev
            if e == f and (not SAME_ENGINE_SYNC or f == "pe"):
                return
            if self.wait_e[f].get(e, 0) >= idx:
                return
            self.wait_e[f][e] = idx
            self.eng[f].wait_ge(self._esem(e, (idx - 1) // EPOCH), (idx - 1) % EPOCH + 1)
        else:
            _, key, val = ev
            if self.wait_d[f].get(key, 0) >= val:
                return
            self.wait_d[f][key] = val
            self.eng[f].wait_ge(self.dsem[key], val)

    def _deps(self, f, r, w):
        for t in r:
            for b in t.bufs:
                self._wait(f, b.w)
        for t in w:
            for b in t.bufs:
                self._wait(f, b.w)
                for ev in list(b.rd.values()):
                    self._wait(f, ev)

    def _commit(self, ev, key, r, w):
        for t in r:
            for b in t.bufs:
                b.rd[key] = ev
        for t in w:
            for b in t.bufs:
                b.w = ev
                b.rd = {}

    def op(self, e, fn, r=(), w=()):
        self._deps(e, r, w)
        ins = fn(self.eng[e])
        self.cnt[e] += 1
        idx = self.cnt[e]
        ins.then_inc(self._esem(e, (idx - 1) // EPOCH), 1)
        self._commit(("e", e, idx), e, r, w)
        self.ninstr += 1
        return ins

    def dma(self, out, in_, q="sp", **kw):
        j = self.dnext[q] % (2 if q == 'pool' else NDMA)
        self.dnext[q] += 1
        key = (q, j)
        if key not in self.dsem:
            self.dsem[key] = self.sem(f"d_{q}_{j}")
            self.duse[key] = 0
        if self.duse[key] > 0:
            self._wait(q, ("d", key, 16 * self.duse[key]))
        self._deps(q, [in_], [out])
        self.duse[key] += 1
        val = 16 * self.duse[key]
        self.eng[q].dma_start(out=out.ap, in_=in_.ap, **kw).then_inc(self.dsem[key], 16)
        self._commit(("d", key, val), ("d", q, j, val), [in_], [out])
        self.ninstr += 1

    def barrier(self):
        for f in self.eng:
            for e in self.eng:
                if e != f and self.cnt[e] > 0:
                    self._wait(f, ("e", e, self.cnt[e]))
            for key, n in self.duse.items():
                if n > 0:
                    self._wait(f, ("d", key, 16 * n))

    def finish(self):
        self.barrier()
        self.es.close()

    def mm(self, out, lhsT, rhs, start=True, stop=True):
        return self.op("pe", lambda e: e.matmul(out.ap, lhsT.ap, rhs.ap, start=start, stop=stop),
                       r=[lhsT, rhs], w=[out])

    def tr(self, out, in_, ident):
        return self.op("pe", lambda e: e.transpose(out.ap, in_.ap, ident.ap), r=[in_, ident], w=[out])

    def act(self, out, in_, func, bias=None, scale=1.0, e="act"):
        r = [in_]
        kw = {}
        if bias is not None:
            if isinstance(bias, Tile):
                r.append(bias)
                kw["bias"] = bias.ap
            else:
                kw["bias"] = bias
        if isinstance(scale, Tile):
            r.append(scale)
            kw["scale"] = scale.ap
        else:
            kw["scale"] = scale
        return self.op(e, lambda en: en.activation(out.ap, in_.ap, func, **kw), r=r, w=[out])

    def tt(self, out, a, b, op, e="dve"):
        return self.op(e, lambda en: en.tensor_tensor(out.ap, a.ap, b.ap, op), r=[a, b], w=[out])

    def ts(self, out, a, s1, s2=None, op0=ALU.mult, op1=None, e="dve"):
        r = [a]
        if isinstance(s1, Tile):
            r.append(s1)
            s1 = s1.ap
        if isinstance(s2, Tile):
            r.append(s2)
            s2 = s2.ap
        if op1 is None:
            return self.op(e, lambda en: en.tensor_scalar(out.ap, a.ap, s1, None, op0), r=r, w=[out])
        return self.op(e, lambda en: en.tensor_scalar(out.ap, a.ap, s1, s2, op0, op1), r=r, w=[out])

    def stt(self, out, in0, scalar, in1, op0, op1, e="dve"):
        r = [in0, in1]
        if isinstance(scalar, Tile):
            r.append(scalar)
            scalar = scalar.ap
        return self.op(e, lambda en: en.scalar_tensor_tensor(out.ap, in0.ap, scalar, in1.ap, op0, op1),
                       r=r, w=[out])

    def rsqrt(self, out, in_, addc):
        self.act(out, in_, AF.Sqrt, bias=addc)
        return self.op("dve", lambda en: en.reciprocal(out.ap, out.ap), r=[out], w=[out])

    def copy(self, out, in_, e="dve"):
        if e == "act":
            return self.op(e, lambda en: en.activation(out.ap, in_.ap, AF.Identity), r=[in_], w=[out])
        return self.op(e, lambda en: en.tensor_copy(out.ap, in_.ap), r=[in_], w=[out])

    def memset(self, out, val, e="dve"):
        return self.op(e, lambda en: en.memset(out.ap, val), r=[], w=[out])


def host_consts():
    c = {}
    c["ident"] = np.eye(128, dtype=np.float32)
    idb = np.zeros((128, 64), np.float32)
    idb[np.arange(128), np.arange(128) % 64] = 1.0
    c["idb"] = idb
    bo = np.zeros((128, 128), np.float32)
    bo[:64, :64] = 1.0
    bo[64:, 64:] = 1.0
    c["blockones"] = bo
    pm = np.zeros((128, 128), np.float32)
    pm[np.arange(128), np.arange(128) ^ 16] = 1.0
    c["pm"] = pm
    p = np.arange(128)
    axis = (p % 64) // 32
    half = (p % 32) // 16
    f = p % 16
    inv = (10000.0 ** (-np.arange(16, dtype=np.float32) / 16)).astype(np.float32)
    t = np.arange(T)
    pos = np.where(axis[:, None] == 0, (t // 64)[None, :], (t % 64)[None, :]).astype(np.float32)
    ang = pos * inv[f][:, None]
    c["ropec"] = np.cos(ang).astype(np.float32)
    c["ropes"] = (np.sin(ang) * np.where(half[:, None] == 0, -1.0, 1.0)).astype(np.float32)
    kk = np.arange(128)[:, None]
    qq = np.arange(128)[None, :]
    am = np.ones((128, 5, 128), np.float32)
    am[:, 2, :] = (kk >= qq)
    am[:, 4, :] = (kk <= qq)
    c["amask"] = am.astype(ml_dtypes.bfloat16)
    i = np.arange(64)[:, None]
    j = np.arange(64)[None, :]
    SL = (j < i).astype(np.float32)
    SU = (j > i).astype(np.float32)
    UI = (j >= i).astype(np.float32)
    LI = (j <= i).astype(np.float32)
    one = np.ones((64, 64), np.float32)
    m2 = np.stack([np.concatenate([one, SL, SL, SU], 1), np.concatenate([one, SU, SU, SL], 1)], 0)
    c["rmask2"] = np.ascontiguousarray(m2.transpose(1, 0, 2))
    m5 = np.stack([np.concatenate([UI, one], 1), np.concatenate([LI, one], 1)], 0)
    c["rmask5"] = np.ascontiguousarray(m5.transpose(1, 0, 2))
    mr = np.ones((2, 128, 128), np.float32)
    mr[0, 64:, :64] = UI
    mr[1, 64:, :64] = LI
    c["rmaskr"] = np.ascontiguousarray(mr.transpose(1, 0, 2))
    rst = np.ones((128, 512), np.float32)
    rst[:, ::64] = 0.0
    c["rst"] = rst
    def dft(n, scale):
        a = 2 * np.pi * (np.outer(np.arange(n), np.arange(n)) % n) / n
        return (np.cos(a) * scale), (-np.sin(a) * scale)
    ct, nst = dft(T, T ** -0.5)
    def blk(a):
        return np.ascontiguousarray(a.reshape(16, 128, 8, 256).transpose(2, 1, 0, 3)).astype(ml_dtypes.bfloat16)
    c["dft_ct"] = blk(ct)
    c["dft_nst"] = blk(nst)
    cc, ncs = dft(NCTX, NCTX ** -0.5)
    c["dft_cc"] = cc.astype(ml_dtypes.bfloat16)
    c["dft_ncs"] = ncs.astype(ml_dtypes.bfloat16)
    c64, ns64 = dft(64, 0.125)
    bdc = np.zeros((256, 256))
    bds = np.zeros((256, 256))
    for g in range(4):
        bdc[g * 64:(g + 1) * 64, g * 64:(g + 1) * 64] = c64
        bds[g * 64:(g + 1) * 64, g * 64:(g + 1) * 64] = -ns64
    c["bdc"] = bdc.astype(ml_dtypes.bfloat16)
    c["bds"] = bds.astype(ml_dtypes.bfloat16)
    return c


CONST_DT = {"amask": BF16, "dft_ct": BF16, "dft_nst": BF16, "dft_cc": BF16, "dft_ncs": BF16, "bdc": BF16, "bds": BF16}

WEIGHT_SHAPES = {
    "mod_w": [2, 1024, 6144], "w_in": [2, 1024, 5248], "rwkv_w2": [2, 2, 64, 256], "rwkv_a2": [2, 2, 64, 256],
    "rwkv_g2": [2, 128, 256], "rwkv_v1": [1, 256, 32], "rwkv_v2": [1, 32, 256], "w_branch_rwkv": [2, 256, 1024],
    "w_branch_attn": [2, 512, 1024], "w_branch_fourier": [2, 256, 1024], "w_out": [2, 1024, 1024],
    "ffn_up": [2, 1024, 5632], "ffn_down": [2, 2816, 1024],
}


HC = None


def build_program(dbg=(), nseq=NSEQ, nlayers=2, stages=("rwkv", "att", "fou", "merge", "ffn")):
    global HC
    if HC is None:
        HC = host_consts()
    nc = bass.Bass("TRN2", target_bir_lowering=False)
    P = Prog(nc, dbg)
    P.init_psum()

    def ext(name, shape, dt=F32):
        return Tile(nc.dram_tensor(name, list(shape), dt, kind="ExternalInput").ap(), [Buf(name)])

    x_d = ext("x", [NSEQ, T, D])
    ctx_d = ext("ctx", [NSEQ, NCTX, D])
    cs_d = ext("cs", [128, 8, 5])
    vecs_d = ext("vecs", [128, 2, NV])
    W = {k: ext(k, s) for k, s in WEIGHT_SHAPES.items()}
    C = {k: ext(k, list(v.shape), CONST_DT.get(k, F32)) for k, v in HC.items()}
    y_d = Tile(nc.dram_tensor("y", [NSEQ, T, D], F32, kind="ExternalOutput").ap(), [Buf("y")])
    hbuf = [P.dram("hT0", [128, 8, TT], F32), P.dram("hT1", [128, 8, TT], F32)]

    def ldc(name, shape, dt=F32, src=None, q="sp"):
        t = P.sb(name, shape, dt)
        P.dma(t, src if src is not None else C[name], q=q)
        return t

    ident = ldc("ident", [128, 128])
    idb = ldc("idb", [128, 64])
    blockones = ldc("blockones", [128, 128])
    amask = ldc("amask", [128, 5, 128], BF16)
    rmask2 = ldc("rmask2", [64, 2, 256])
    rmask5 = ldc("rmask5", [64, 2, 128])
    rmaskr = ldc("rmaskr", [128, 2, 128])
    rst = ldc("rst", [128, 512])
    bdc = ldc("bdc", [128, 2, 256], BF16, C["bdc"].v(C["bdc"].ap.rearrange("(kc p) n -> p kc n", p=128)))
    bds = ldc("bds", [128, 2, 256], BF16, C["bds"].v(C["bds"].ap.rearrange("(kc p) n -> p kc n", p=128)))
    dcc = ldc("dcc", [128, 2, 256], BF16, C["dft_cc"].v(C["dft_cc"].ap.rearrange("(kc p) n -> p kc n", p=128)))
    dncs = ldc("dncs", [128, 2, 256], BF16, C["dft_ncs"].v(C["dft_ncs"].ap.rearrange("(kc p) n -> p kc n", p=128)))
    vecs = ldc("vecs_sb", [128, 2, NV], F32, vecs_d)
    ones_bf = P.sb("ones_bf", [128, 128], BF16)
    P.memset(ones_bf, 1.0)
    ident_bf = P.sb("ident_bf", [128, 128], BF16)
    P.copy(ident_bf, ident)
    mall = P.sb("mall", [128, 2, 5, 48], F32)
    vfT = P.sb("vfT", [128, 2, TT], BF16)

    def V(l, name):
        o, c = VO[name]
        return vecs[:, l, o:o + c]

    def ring(name, shape, dt, n, es):
        tiles = [P.sb(f"{name}{i}", shape, dt, es) for i in range(n)]
        st = {"i": 0}

        def nxt():
            t = tiles[st["i"] % n]
            st["i"] += 1
            return t
        return nxt

    with ExitStack() as es:
        cs = P.sb("cs_sb", [128, 8, 5], F32, es)
        P.dma(cs, cs_d)
        sg = P.sb("cs_sg", [128, 8, 5], F32, es)
        P.act(sg, cs, AF.Sigmoid)
        scs = P.sb("scs", [128, 8, 5], F32, es)
        P.tt(scs, cs, sg, ALU.mult)
        mw = ring("mw", [128, 8, 768], F32, 2, es)
        for l in range(nlayers):
            src = W["mod_w"][l]
            srcv = src.v(src.ap.rearrange("(kc p) n -> p kc n", p=128))
            for t8 in range(8):
                mwt = mw()
                P.dma(mwt, srcv[:, :, t8 * 768:(t8 + 1) * 768])
                for o6 in range(6):
                    oc = t8 * 6 + o6
                    ps = P.psum()
                    for kc in range(8):
                        P.mm(ps[:, 0:5], mwt[:, kc, o6 * 128:(o6 + 1) * 128], scs[:, kc, :], start=(kc == 0), stop=(kc == 7))
                    mb = V(l, "modb")
                    P.ts(mall[:, l, :, oc], ps[:, 0:5], mb[:, oc:oc + 1], None, op0=ALU.add)
        P.barrier()
    P.dump("mall", mall, [128, 2, 5, 48])

    def norm_stats(h, n, es_tiles):
        sq, rs = es_tiles
        sqt = sq()
        P.act(sqt[:, :, 0:n], h, AF.Square)
        ps = P.psum()
        for c in range(8):
            P.mm(ps[:, 0:n], ones_bf, sqt[:, c, 0:n], start=(c == 0), stop=(c == 7))
        rst_ = rs()
        P.rsqrt(rst_[:, 0:n], ps[:, 0:n], 1024 * EPS)
        return rst_

    def bc3(t2, n, k):
        return Tile(t2.ap.unsqueeze(1).to_broadcast([t2.ap.shape[0], k, n]), t2.bufs)

    def gains(name, l, j, gname, scale_i, es, shift_i=None):
        gs = P.sb(name, [128, 8], F32, es)
        g = V(l, gname)
        msc = mall[:, l, j, scale_i * 8:(scale_i + 1) * 8]
        if shift_i is not None:
            P.ts(gs, msc, 1.0, 32.0, op0=ALU.add, op1=ALU.mult)
        else:
            P.ts(gs, msc, 32.0, None, op0=ALU.mult)
        P.tt(gs, gs, g, ALU.mult)
        sh = mall[:, l, j, shift_i * 8:(shift_i + 1) * 8] if shift_i is not None else None
        return gs, sh

    LAT_BLKS = [(NCTX + 512 * i, 512) for i in range(4)]
    ALL_BLKS = [(0, NCTX)] + LAT_BLKS

    def load_seq(s, cur):
        with ExitStack() as es:
            xt = ring("xt", [128, 1024], F32, 2, es)
            stg = ring("stg", [128, 8, 512], F32, 2, es)
            for (t0, n) in ALL_BLKS:
                st = stg()
                for tt in range(n // 128):
                    x = xt()
                    if t0 == 0:
                        P.dma(x, ctx_d[s, tt * 128:(tt + 1) * 128, :])
                    else:
                        r0 = t0 - NCTX + tt * 128
                        P.dma(x, x_d[s, r0:r0 + 128, :])
                    for half in range(2):
                        ps = P.psum()
                        for c4 in range(4):
                            c = half * 4 + c4
                            P.tr(ps[:, c4 * 128:(c4 + 1) * 128], x[:, c * 128:(c + 1) * 128], ident)
                        P.copy(st[:, half * 4:(half + 1) * 4, tt * 128:(tt + 1) * 128],
                               ps.v(ps.ap.rearrange("p (a b) -> p a b", b=128)), e=("act" if half else "dve"))
                P.dma(hbuf[cur][:, :, t0:t0 + n], st[:, :, 0:n], q="pool")
            P.barrier()

    def store_seq(s, cur):
        with ExitStack() as es:
            ht = ring("ht", [128, 8, 128], F32, 2, es)
            ot = ring("ot", [128, 1024], F32, 2, es)
            for tt in range(T // 128):
                h = ht()
                P.dma(h, hbuf[cur][:, :, NCTX + tt * 128:NCTX + (tt + 1) * 128])
                o = ot()
                for half in range(2):
                    ps = P.psum()
                    for c4 in range(4):
                        P.tr(ps[:, c4 * 128:(c4 + 1) * 128], h[:, half * 4 + c4, :], ident)
                    P.copy(o[:, half * 512:(half + 1) * 512], ps, e=("act" if half else "dve"))
                P.dma(y_d[s, tt * 128:(tt + 1) * 128, :], o, q="pool")
            P.barrier()

    def stage_a(l, s, cur, aT, es0):
        with ExitStack() as es:
            gsl, shl = gains("gsl", l, s, "g_mpre", 1, es, 0)
            gsc, shc = gains("gsc", l, 4, "g_mpre", 1, es, 0)
            hb = ring("hb", [128, 8, 512], F32, 2, es)
            sq = ring("sq", [128, 8, 512], BF16, 1, es)
            rs = ring("rs", [128, 512], F32, 2, es)
            tmp = ring("tmpa", [128, 8, 512], F32, 1, es)
            for (t0, n) in ALL_BLKS:
                h = hb()
                P.dma(h[:, :, 0:n], hbuf[cur][:, :, t0:t0 + n])
                r = norm_stats(h[:, :, 0:n], n, (sq, rs))
                tm = tmp()
                P.tt(tm[:, :, 0:n], h[:, :, 0:n], bc3(r[:, 0:n], n, 8), ALU.mult)
                gs, sh = (gsc, shc) if t0 == 0 else (gsl, shl)
                for c in range(8):
                    P.act(aT[:, c, t0:t0 + n], tm[:, c, 0:n], AF.Identity, bias=sh[:, c:c + 1], scale=gs[:, c:c + 1])
            P.barrier()

    def stage_ffn(l, s, cur):
        src, dst = hbuf[cur], hbuf[1 - cur]
        nblk_ctx = 1 if l < nlayers - 1 or nlayers == 1 and False else 0
        with ExitStack() as es:
            up = P.sb("up_sb", [128, 8, 2 * DFF], BF16, es)
            upsrc = W["ffn_up"][l]
            upv = upsrc.v(upsrc.ap.rearrange("(kc p) n -> p kc n", p=128))
            for q4 in range(4):
                P.dma(up[:, :, q4 * 1408:(q4 + 1) * 1408], upv[:, :, q4 * 1408:(q4 + 1) * 1408], q="pool")
            dnr = ring("dn_sb", [128, 22, 128], BF16, 2, es)
            dsrc = W["ffn_down"][l]
            dnv = dsrc.v(dsrc.ap.rearrange("(kc p) n -> p kc n", p=128))
            gains_l = gains("fgsl", l, s, "g_fpre", 4, es, 3)
            gains_c = gains("fgsc", l, 4, "g_fpre", 4, es, 3)
            gpost_l, _ = gains("fgpl", l, s, "g_fpost", 5, es)
            gpost_c, _ = gains("fgpc", l, 4, "g_fpost", 5, es)
            cw0, cw1, cw2, cb = V(l, "cw0"), V(l, "cw1"), V(l, "cw2"), V(l, "cb")
            hb = ring("fhb", [128, 8, 258], F32, 1, es)
            sq = ring("fsq", [128, 8, 258], BF16, 1, es)
            rs = ring("frs", [128, 258], F32, 2, es)
            tmp = ring("ftmp", [128, 258], F32, 2, es)
            fT = ring("fT", [128, 8, 258], BF16, 1, es)
            zg = ring("zg", [128, 258], F32, 2, es)
            cz = ring("cz", [128, 256], F32, 2, es)
            gz = ring("gz", [128, 256], F32, 2, es)
            hid = ring("hid", [128, 22, 256], BF16, 1, es)
            mo = ring("fmo", [128, 8, 256], F32, 1, es)
            blks = [(256 * bi, 256) for bi in range(TT // 256)]
            for (t0, n) in blks:
                isctx = (t0 == 0)
                if isctx and l == nlayers - 1:
                    continue
                seq_lo, seq_hi = (0, NCTX) if isctx else (NCTX, TT)
                lo, hi = max(t0 - 1, seq_lo), min(t0 + n + 1, seq_hi)
                c0 = lo - (t0 - 1)
                ncol = hi - lo
                h = hb()
                P.dma(h[:, :, c0:c0 + ncol], src[:, :, lo:hi])
                hv = h[:, :, c0:c0 + ncol]
                sqt = sq()
                P.act(sqt[:, :, 0:ncol], hv, AF.Square)
                ps = P.psum()
                for c in range(8):
                    P.mm(ps[:, 0:ncol], ones_bf, sqt[:, c, 0:ncol], start=(c == 0), stop=(c == 7))
                r = rs()
                P.rsqrt(r[:, 0:ncol], ps[:, 0:ncol], 1024 * EPS)
                gs, sh = gains_c if isctx else gains_l
                f = fT()
                for c in range(8):
                    tm = tmp()
                    P.stt(tm[:, 0:ncol], h[:, c, c0:c0 + ncol], gs[:, c:c + 1], r[:, 0:ncol], ALU.mult, ALU.mult)
                    P.act(f[:, c, c0:c0 + ncol], tm[:, 0:ncol], AF.Identity, bias=sh[:, c:c + 1])
                hd = hid()
                for j in range(22):
                    psg = P.psum()
                    for kc in range(8):
                        P.mm(psg[:, 0:ncol], up[:, kc, j * 128:(j + 1) * 128], f[:, kc, c0:c0 + ncol], start=(kc == 0), stop=(kc == 7))
                    psv = P.psum()
                    for kc in range(8):
                        P.mm(psv[:, 0:n], up[:, kc, DFF + j * 128:DFF + (j + 1) * 128], f[:, kc, 1:1 + n], start=(kc == 0), stop=(kc == 7))
                    z = zg()
                    if c0 == 1:
                        P.memset(z[:, 0:1], 0.0)
                    if c0 + ncol < n + 2:
                        P.memset(z[:, n + 1:n + 2], 0.0)
                    P.copy(z[:, c0:c0 + ncol], psg[:, 0:ncol], e="act")
                    c_ = cz()
                    P.ts(c_, z[:, 1:1 + n], cw1[:, j:j + 1], cb[:, j:j + 1], op0=ALU.mult, op1=ALU.add)
                    P.stt(c_, z[:, 0:n], cw0[:, j:j + 1], c_, ALU.mult, ALU.add)
                    P.stt(c_, z[:, 2:2 + n], cw2[:, j:j + 1], c_, ALU.mult, ALU.add)
                    g_ = gz()
                    P.act(g_, c_, AF.Gelu_apprx_tanh)
                    P.tt(hd[:, j, :], g_, psv[:, 0:n], ALU.mult)
                m = mo()
                for oc in range(8):
                    dn = dnr()
                    P.dma(dn, dnv[:, :, oc * 128:(oc + 1) * 128], q="pool")
                    ps = P.psum()
                    for j in range(22):
                        P.mm(ps[:, 0:n], dn[:, j, :], hd[:, j, :], start=(j == 0), stop=(j == 21))
                    P.copy(m[:, oc, :], ps[:, 0:n], e="act")
                s2 = sq()
                P.act(s2[:, :, 0:n], m, AF.Square)
                ps = P.psum()
                for c in range(8):
                    P.mm(ps[:, 0:n], ones_bf, s2[:, c, 0:n], start=(c == 0), stop=(c == 7))
                r2 = rs()
                P.rsqrt(r2[:, 0:n], ps[:, 0:n], 1024 * EPS)
                P.tt(m, m, bc3(r2[:, 0:n], n, 8), ALU.mult)
                gp = gpost_c if isctx else gpost_l
                for c in range(8):
                    P.stt(m[:, c, :], m[:, c, :], gp[:, c:c + 1], h[:, c, 1:1 + n], ALU.mult, ALU.add)
                P.dma(dst[:, :, t0:t0 + n], m, q="pool")
            P.barrier()

    B = dict(P=P, W=W, C=C, V=V, ring=ring, mall=mall, hbuf=hbuf, vfT=vfT, ident=ident, ident_bf=ident_bf, idb=idb,
             blockones=blockones, amask=amask, rmask2=rmask2, rmask5=rmask5, rmaskr=rmaskr, rst=rst, bdc=bdc, bds=bds,
             dcc=dcc, dncs=dncs, ones_bf=ones_bf, bc3=bc3, gains=gains, norm_stats=norm_stats, nlayers=nlayers,
             LAT_BLKS=LAT_BLKS, ALL_BLKS=ALL_BLKS)

    for s in range(nseq):
        cur = 0
        load_seq(s, cur)
        for l in range(nlayers):
            with ExitStack() as esl:
                aT = P.sb("aT", [128, 8, TT], BF16, esl)
                stage_a(l, s, cur, aT, esl)
                if s == 0:
                    P.dump(f"aT{l}", aT, [128, 8, TT], BF16)
                yrT = P.sb("yrT", [128, 2, TT], BF16, esl)
                if "rwkv" in stages:
                    stage_rwkv(B, l, s, aT, yrT)
                    if s == 0:
                        P.dump(f"yrT{l}", yrT, [128, 2, TT], BF16)
                yaT = P.sb("yaT", [128, 4, TT], BF16, esl)
                yfT = P.sb("yfT", [128, 2, TT], BF16, esl)
                if "att" in stages:
                    stage_att(B, l, s, aT, yaT)
                    if s == 0:
                        P.dump(f"yaT{l}", yaT, [128, 4, TT], BF16)
                if "fou" in stages:
                    stage_fou(B, l, s, aT, yfT)
                    if s == 0:
                        P.dump(f"yfT{l}", yfT, [128, 2, TT], BF16)
                if "merge" in stages:
                    stage_merge(B, l, s, cur, aT, yrT, yaT, yfT)
                P.barrier()
            if s == 0:
                P.dump(f"hmix{l}", hbuf[cur], [128, 8, TT])
            if "ffn" in stages:
                stage_ffn(l, s, cur)
                cur = 1 - cur
            if s == 0:
                P.dump(f"hffn{l}", hbuf[cur], [128, 8, TT])
        store_seq(s, cur)
    P.finish()
    return nc, P


def _wsrc(B, l):
    src = B["W"]["w_in"][l]
    return src.v(src.ap.rearrange("(kc p) n -> p kc n", p=128))


def stage_fou(B, l, s, aT, yfT):
    P, ring, C = B["P"], B["ring"], B["C"]
    need_ctx = l < B["nlayers"] - 1
    bdc, bds, dcc, dncs = B["bdc"], B["bds"], B["dcc"], B["dncs"]
    with ExitStack() as es:
        wf = P.sb("wf", [128, 8, 256], BF16, es)
        P.dma(wf, _wsrc(B, l)[:, :, C_F:C_F + 256], q="pool")
        ufT = P.sb("ufT", [128, 2, TT], BF16, es)
        blks = B["ALL_BLKS"] if need_ctx else B["LAT_BLKS"]
        import os
        if os.environ.get("FOU_CUT") == "2":
            P.barrier()
            return
        for (t0, n) in blks:
            for j in range(2):
                ps = P.psum()
                for kc in range(8):
                    P.mm(ps[:, 0:n], wf[:, kc, j * 128:(j + 1) * 128], aT[:, kc, t0:t0 + n], start=(kc == 0), stop=(kc == 7))
                P.copy(ufT[:, j, t0:t0 + n], ps[:, 0:n], e=("act" if j else "dve"))
        if os.environ.get("FOU_CUT") == "3":
            P.barrier()
            return
        Zc = P.sb("Zc", [128, 18, 256], BF16, es)
        Zs = P.sb("Zs", [128, 18, 256], BF16, es)
        for tt in range(0 if need_ctx else 2, int(os.environ.get("FOU_NT", "18"))):
            ps = P.psum()
            for kc in range(2):
                P.mm(ps[:, 0:256], ufT[:, kc, tt * 128:(tt + 1) * 128], bdc[:, kc, :], start=(kc == 0), stop=(kc == 1))
            psb = P.psum()
            for kc in range(2):
                P.mm(psb[:, 0:256], ufT[:, kc, tt * 128:(tt + 1) * 128], bds[:, kc, :], start=(kc == 0), stop=(kc == 1))
            P.copy(Zc[:, tt, :], ps[:, 0:256], e="act")
            P.copy(Zs[:, tt, :], psb[:, 0:256], e="dve")
        import os
        if os.environ.get("FOU_CUT") == "1":
            P.barrier()
            return
        dct = ring("dct", [128, 16, 256], BF16, 2, es)
        dnst = ring("dnst", [128, 16, 256], BF16, 2, es)
        for nb in range(8):
            ct = dct()
            P.dma(ct, C["dft_ct"][nb])
            st = dnst()
            P.dma(st, C["dft_nst"][nb])
            for ch in range(2):
                ps = P.psum()
                for tt in range(16):
                    P.mm(ps[:, 0:256], Zc[:, 2 + tt, ch * 128:(ch + 1) * 128], ct[:, tt, :], start=(tt == 0), stop=False)
                for tt in range(16):
                    P.mm(ps[:, 0:256], Zs[:, 2 + tt, ch * 128:(ch + 1) * 128], st[:, tt, :], start=False, stop=(tt == 15))
                P.copy(yfT[:, ch, NCTX + nb * 256:NCTX + (nb + 1) * 256], ps[:, 0:256], e=("act" if ch else "dve"))
        if need_ctx:
            for ch in range(2):
                ps = P.psum()
                for tt in range(2):
                    P.mm(ps[:, 0:256], Zc[:, tt, ch * 128:(ch + 1) * 128], dcc[:, tt, :], start=(tt == 0), stop=False)
                for tt in range(2):
                    P.mm(ps[:, 0:256], Zs[:, tt, ch * 128:(ch + 1) * 128], dncs[:, tt, :], start=False, stop=(tt == 1))
                P.copy(yfT[:, ch, 0:256], ps[:, 0:256], e=("act" if ch else "dve"))
        P.barrier()


def stage_merge(B, l, s, cur, aT, yrT, yaT, yfT):
    P, ring, W = B["P"], B["ring"], B["W"]
    need_ctx = l < B["nlayers"] - 1
    blks = B["ALL_BLKS"] if need_ctx else B["LAT_BLKS"]
    hb_d = B["hbuf"][cur]
    with ExitStack() as es0:
        mixpre = P.sb("mixpre", [128, 8, TT], BF16, es0)
        with ExitStack() as es:
            wb = P.sb("wb", [128, 8, 1024], BF16, es)
            for nm, k0, nk in (("w_branch_rwkv", 0, 2), ("w_branch_attn", 2, 4), ("w_branch_fourier", 6, 2)):
                src = W[nm][l]
                P.dma(wb[:, k0:k0 + nk, :], src.v(src.ap.rearrange("(kc p) n -> p kc n", p=128)), q="pool")
            wgr = ring("wg", [128, 8, 3, 128], BF16, 2, es)
            sgr = ring("sgm", [128, 512], F32, 3, es)
            accr = ring("accm", [128, 512], F32, 2, es)
            srcv = _wsrc(B, l)
            for oc in range(8):
                wg = wgr()
                for br in range(3):
                    c0 = C_G + br * 1024 + oc * 128
                    P.dma(wg[:, :, br, :], srcv[:, :, c0:c0 + 128], q="pool")
                for (t0, n) in blks:
                    acc = accr()
                    for br, (yt, k0, nk) in enumerate(((yrT, 0, 2), (yaT, 2, 4), (yfT, 6, 2))):
                        psg = P.psum()
                        for kc in range(8):
                            P.mm(psg[:, 0:n], wg[:, kc, br, :], aT[:, kc, t0:t0 + n], start=(kc == 0), stop=(kc == 7))
                        sg = sgr()
                        P.act(sg[:, 0:n], psg[:, 0:n], AF.Sigmoid)
                        psp = P.psum()
                        for kc in range(nk):
                            P.mm(psp[:, 0:n], wb[:, k0 + kc, oc * 128:(oc + 1) * 128], yt[:, kc, t0:t0 + n],
                                 start=(kc == 0), stop=(kc == nk - 1))
                        if br == 0:
                            P.tt(acc[:, 0:n], sg[:, 0:n], psp[:, 0:n], ALU.mult)
                        else:
                            P.tt(sg[:, 0:n], sg[:, 0:n], psp[:, 0:n], ALU.mult)
                            if br == 1:
                                P.tt(acc[:, 0:n], acc[:, 0:n], sg[:, 0:n], ALU.add)
                            else:
                                P.tt(mixpre[:, oc, t0:t0 + n], acc[:, 0:n], sg[:, 0:n], ALU.add)
            P.barrier()
        with ExitStack() as es:
            wo = P.sb("wo", [128, 8, 1024], BF16, es)
            src = W["w_out"][l]
            P.dma(wo, src.v(src.ap.rearrange("(kc p) n -> p kc n", p=128)), q="pool")
            gm_l, _ = B["gains"]("gml", l, s, "g_mpost", 2, es)
            gm_c, _ = B["gains"]("gmc", l, 4, "g_mpost", 2, es)
            mo = ring("mmo", [128, 8, 512], F32, 1, es)
            hb = ring("mhb", [128, 8, 512], F32, 1, es)
            sq = ring("msq", [128, 8, 512], BF16, 1, es)
            rs = ring("mrs", [128, 512], F32, 2, es)
            for (t0, n) in blks:
                h = hb()
                P.dma(h[:, :, 0:n], hb_d[:, :, t0:t0 + n])
                m = mo()
                for oc in range(8):
                    ps = P.psum()
                    for kc in range(8):
                        P.mm(ps[:, 0:n], wo[:, kc, oc * 128:(oc + 1) * 128], mixpre[:, kc, t0:t0 + n], start=(kc == 0), stop=(kc == 7))
                    P.copy(m[:, oc, 0:n], ps[:, 0:n], e=("act" if oc % 2 else "dve"))
                r = B["norm_stats"](m[:, :, 0:n], n, (sq, rs))
                P.tt(m[:, :, 0:n], m[:, :, 0:n], B["bc3"](r[:, 0:n], n, 8), ALU.mult)
                gm = gm_c if t0 == 0 else gm_l
                for c in range(8):
                    P.stt(h[:, c, 0:n], m[:, c, 0:n], gm[:, c:c + 1], h[:, c, 0:n], ALU.mult, ALU.add)
                P.dma(hb_d[:, :, t0:t0 + n], h[:, :, 0:n], q="pool")
            P.barrier()


def stage_att(B, l, s, aT, yaT):
    P, ring, C, V = B["P"], B["ring"], B["C"], B["V"]
    need_ctx = l < B["nlayers"] - 1
    amask, ident_bf = B["amask"], B["ident_bf"]
    with ExitStack() as es:
        ropec = P.sb("ropec", [128, T], F32, es)
        P.dma(ropec, C["ropec"])
        ropes = P.sb("ropes", [128, T], F32, es)
        P.dma(ropes, C["ropes"])
        pm = P.sb("pm", [128, 128], F32, es)
        P.dma(pm, C["pm"])
        srcv = _wsrc(B, l)
        wq = P.sb("wq", [128, 8, 512], BF16, es)
        P.dma(wq, srcv[:, :, C_Q:C_Q + 512], q="pool")
        wkd = P.sb("wkd", [128, 8, 2, 128], BF16, es)
        for kv in range(2):
            for dup in range(2):
                P.dma(wkd[:, :, kv, dup * 64:(dup + 1) * 64], srcv[:, :, C_K + kv * 64:C_K + (kv + 1) * 64], q="pool")
        wv = P.sb("wv", [128, 8, 128], BF16, es)
        P.dma(wv, srcv[:, :, C_V:C_V + 128], q="pool")
        qrT = P.sb("qrT", [128, 4, TT], BF16, es)
        kdT = P.sb("kdT", [128, 2, TT], BF16, es)
        Vaug = P.sb("Vaug", [128, 18, 2, 65], BF16, es)
        P.memset(Vaug[:, :, :, 64:65], 1.0)
        esink = P.sb("esink", [128, 8], F32, es)
        P.act(esink, V(l, "sink"), AF.Exp)
        qsb = ring("qsb", [128, 512], F32, 2, es)
        t1 = ring("ropet1", [128, 512], F32, 2, es)
        t2 = ring("ropet2", [128, 512], F32, 2, es)
        for (t0, n) in B["ALL_BLKS"]:
            isctx = (t0 == 0)
            for m in range(6):
                if isctx and m < 4 and not need_ctx:
                    continue
                ps = P.psum()
                for kc in range(8):
                    lhsT = wq[:, kc, m * 128:(m + 1) * 128] if m < 4 else wkd[:, kc, m - 4, :]
                    P.mm(ps[:, 0:n], lhsT, aT[:, kc, t0:t0 + n], start=(kc == 0), stop=(kc == 7))
                dst = qrT[:, m, t0:t0 + n] if m < 4 else kdT[:, m - 4, t0:t0 + n]
                if isctx:
                    P.copy(dst, ps[:, 0:n], e="act")
                else:
                    q = qsb()
                    P.copy(q[:, 0:n], ps[:, 0:n], e="act")
                    ps2 = P.psum()
                    P.mm(ps2[:, 0:n], pm, q[:, 0:n])
                    p0 = t0 - NCTX
                    a = t1()
                    P.tt(a[:, 0:n], q[:, 0:n], ropec[:, p0:p0 + n], ALU.mult)
                    b = t2()
                    P.tt(b[:, 0:n], ps2[:, 0:n], ropes[:, p0:p0 + n], ALU.mult)
                    P.tt(dst, a[:, 0:n], b[:, 0:n], ALU.add)
            for tt in range(n // 128):
                ti = t0 // 128 + tt
                ps = P.psum()
                for kc in range(8):
                    P.mm(ps[:, 0:128], aT[:, kc, t0 + tt * 128:t0 + (tt + 1) * 128], wv[:, kc, :], start=(kc == 0), stop=(kc == 7))
                P.copy(Vaug[:, ti, :, 0:64], ps.v(ps.ap[:, 0:128].rearrange("p (a b) -> p a b", b=64)), e="dve")
        if s == 0:
            P.dump(f"qrT{l}", qrT, [128, 4, TT], BF16)
            P.dump(f"kdT{l}", kdT, [128, 2, TT], BF16)
            P.dump(f"Vaug{l}", Vaug, [128, 18, 2, 65], BF16)
        PTr = ring("PT", [128, 5, 128], BF16, 3, es)
        yat = ring("yatok", [128, 8, 64], BF16, 2, es)
        den = ring("den", [128, 8], F32, 2, es)
        qblocks = ([("c", 0), ("c", 1)] if need_ctx else []) + [("l", i) for i in range(16)]
        cnt = {"pss": 0, "pso": 0, "pst": 0}
        for kind, i in qblocks:
            q0 = i * 128 if kind == "c" else NCTX + i * 128
            slots = [(0, 0), (1, 1)]
            if kind == "l":
                if i > 0:
                    slots.append((2, 2 + i - 1))
                slots.append((3, 2 + i))
                if i < 15:
                    slots.append((4, 2 + i + 1))
            ya = yat()
            dn = den()
            for kvg in range(2):
                pso = P.psum_at(4 + cnt["pso"] % 2)
                cnt["pso"] += 1
                pso_v = pso.v(pso.ap[:, 0:260].rearrange("p (h d) -> p h d", d=65))
                for hh in range(4):
                    h = kvg * 4 + hh
                    m = h // 2
                    po = 64 * (h % 2)
                    pss = P.psum_at(2 * (cnt["pss"] % 2), 2)
                    cnt["pss"] += 1
                    pss_v = pss.v(pss.ap.rearrange("p a (b n) -> p (a b) n", n=128))
                    for (sl, kt) in slots:
                        P.mm(pss_v[:, sl, :], kdT[po:po + 64, kvg, kt * 128:(kt + 1) * 128], qrT[po:po + 64, m, q0:q0 + 128])
                    pt = PTr()
                    P.act(pt, pss_v[:, 0:5, :], AF.Exp, scale=0.125)
                    if kind == "l":
                        P.tt(pt, pt, amask, ALU.mult, e="pool")
                    for si, (sl, kt) in enumerate(slots):
                        P.mm(pso_v[:, hh, :], pt[:, sl, :], Vaug[:, kt, kvg, :], start=(si == 0), stop=(si == len(slots) - 1))
                dn4 = dn[:, kvg * 4:(kvg + 1) * 4]
                P.tt(dn4, pso_v[:, :, 64], esink[:, kvg * 4:(kvg + 1) * 4], ALU.add)
                P.op("dve", lambda en: en.reciprocal(dn4.ap, dn4.ap), r=[dn4], w=[dn4])
                P.tt(ya[:, kvg * 4:(kvg + 1) * 4, :], pso_v[:, :, 0:64],
                     Tile(dn4.ap.unsqueeze(2).to_broadcast([128, 4, 64]), dn4.bufs), ALU.mult)
            pst = P.psum_at(6 + cnt["pst"] % 2)
            cnt["pst"] += 1
            pst_bf = pst.v(pst.ap.bitcast(BF16))
            ya_f = ya.v(ya.ap.rearrange("p h d -> p (h d)"))
            for m in range(4):
                P.tr(pst_bf[:, m * 128:(m + 1) * 128], ya_f[:, m * 128:(m + 1) * 128], ident_bf)
            P.copy(yaT[:, :, q0:q0 + 128], pst_bf.v(pst_bf.ap[:, 0:512].rearrange("p (a b) -> p a b", b=128)), e="act")
        P.barrier()


NLOCK = 4


def stage_rwkv(B, l, s, aT, yrT):
    P, ring, W, V = B["P"], B["ring"], B["W"], B["V"]
    nlayers = B["nlayers"]
    need_ctx = l < nlayers - 1
    idb, blockones, rst, vfT = B["idb"], B["blockones"], B["rst"], B["vfT"]
    rmask2, rmask5, rmaskr = B["rmask2"], B["rmask5"], B["rmaskr"]
    with ExitStack() as es:
        w_r = P.sb("w_r", [128, 8, 1152], BF16, es)
        srcv = _wsrc(B, l)
        P.dma(w_r[:, :, 0:576], srcv[:, :, 0:576], q="pool")
        P.dma(w_r[:, :, 576:1152], srcv[:, :, 576:1152], q="pool")
        w2sb = P.sb("w2sb", [128, 256], F32, es)
        P.dma(w2sb, W["rwkv_w2"][l].v(W["rwkv_w2"][l].ap.rearrange("d k n -> (d k) n")))
        a2sb = P.sb("a2sb", [128, 256], F32, es)
        P.dma(a2sb, W["rwkv_a2"][l].v(W["rwkv_a2"][l].ap.rearrange("d k n -> (d k) n")))
        g2sb = P.sb("g2sb", [128, 256], F32, es)
        P.dma(g2sb, W["rwkv_g2"][l])
        if l > 0:
            v1sb = P.sb("v1sb", [128, 2, 32], F32, es)
            P.dma(v1sb, W["rwkv_v1"][l - 1].v(W["rwkv_v1"][l - 1].ap.rearrange("(g p) m -> p g m", p=128)))
            v2sb = P.sb("v2sb", [32, 256], F32, es)
            P.dma(v2sb, W["rwkv_v2"][l - 1])
        mu0, mu1 = V(l, "mu0"), V(l, "mu1")
        k_k, k_a, r_k, lnx_w, lnx_b, v0 = V(l, "k_k"), V(l, "k_a"), V(l, "r_k"), V(l, "lnx_w"), V(l, "lnx_b"), V(l, "v0")
        w0 = [V(l, "w0f"), V(l, "w0b")]
        a0 = [V(l, "a0f"), V(l, "a0b")]
        omka = P.sb("omka", [128, 2], F32, es)
        P.ts(omka, k_a, -1.0, 1.0, op0=ALU.mult, op1=ALU.add)
        kah = P.sb("kah", [128, 2], F32, es)
        P.ts(kah, k_a, 0.5, None, op0=ALU.mult)
        yfwd = P.sb("yfwd", [128, 2, TT], BF16, es)
        u_blk = P.sb("u_blk", [128, 9, 258], F32, es)
        us = P.sb("us", [128, 9, 256], F32, es)
        dtm = ring("dtm", [128, 256], F32, 2, es)
        A = {}
        for nm in ("vv", "kap", "sg", "aa", "aaf", "key", "bb", "cs", "cs2", "e1", "e2", "eex", "e3", "rt", "kt", "bh",
                   "kh", "bck", "kck", "tmp1", "tmp2"):
            A[nm] = P.sb("r_" + nm, [128, 2, 256], BF16 if nm in ("rt", "kt", "bh", "kh", "bck", "kck") else F32, es)
        idbb = P.sb("idb_bf", [128, 64], BF16, es)
        P.copy(idbb, idb)
        th = P.sb("r_th", [128, 256], F32, es)
        sgx = P.sb("r_sgx", [128, 256], F32, es)
        vv1 = P.sb("r_vv1", [32, 256], F32, es)
        etot = P.sb("r_etot", [128, 2, 4], F32, es)
        tot8 = P.sb("r_tot8", [128, 2, 4], F32, es)
        XA = [P.sb(f"XA{i}", [64, 4, 256], BF16, es) for i in range(NLOCK)]
        Wb = [P.sb(f"Wb{i}", [64, 4, 128], BF16, es) for i in range(NLOCK)]
        W32 = [P.sb(f"W32{i}", [64, 4, 128], F32, es) for i in range(NLOCK)]
        Pw = [P.sb(f"Pw{i}", [64, 4, 128], BF16, es) for i in range(NLOCK)]
        RB = [P.sb(f"RB{i}", [64, 4, 128], BF16, es) for i in range(NLOCK)]
        Rt = [P.sb(f"Rt{i}", [128, 4, 128], F32, es) for i in range(NLOCK)]
        Dg = [P.sb(f"Dg{i}", [128, 2, 64], BF16, es) for i in range(NLOCK)]
        Zr = []
        for i in range(8):
            z = P.sb(f"Zr{i}", [128, 4, 64], F32, es)
            zs = Tile(z.ap[0:64], [Buf(f"Zs{i}")])
            zv = Tile(z.ap[64:128], [Buf(f"Zv{i}")])
            Zr.append((Tile(z.ap, zs.bufs + zv.bufs), zs, zv))
        pa = {"i": 0}

        def psA():
            pa["i"] += 1
            return P.psum_at(pa["i"] % 2)

        def fm(arr, h, c):
            po = 64 * (h % 2)
            return arr[po:po + 64, h // 2, c * 64:(c + 1) * 64]

        def v4(ps, n):
            return ps.v(ps.ap.rearrange("p (h n) -> p h n", n=n))

        def prep(bi, d, full):
            t0 = 256 * bi
            seq_lo, seq_hi = (0, NCTX) if bi == 0 else (NCTX, TT)
            lo, hi = max(t0 - 1, seq_lo), min(t0 + 257, seq_hi)
            c0 = lo - (t0 - 1)
            ncol = hi - lo
            if c0 == 1:
                P.memset(u_blk[:, :, 0:1], 0.0)
            if c0 + ncol < 258:
                P.memset(u_blk[:, :, 257:258], 0.0)
            for j in range(9):
                ps = psA()
                for kc in range(8):
                    P.mm(ps[:, 0:ncol], w_r[:, kc, j * 128:(j + 1) * 128], aT[:, kc, lo:hi], start=(kc == 0), stop=(kc == 7))
                P.copy(u_blk[:, j, c0:c0 + ncol], ps[:, 0:ncol], e=("act" if j % 2 else "dve"))
            for j in range(9):
                d0 = dtm()
                P.tt(d0, u_blk[:, j, 0:256], u_blk[:, j, 1:257], ALU.subtract)
                P.stt(us[:, j, :], d0, mu0[:, j:j + 1], u_blk[:, j, 1:257], ALU.mult, ALU.add)
                d1 = dtm()
                P.tt(d1, u_blk[:, j, 2:258], u_blk[:, j, 1:257], ALU.subtract)
                P.stt(us[:, j, :], d1, mu1[:, j:j + 1], us[:, j, :], ALU.mult, ALU.add)
            r_, k_, v_ = us[:, 0:2, :], us[:, 2:4, :], us[:, 4:6, :]
            vv = A["vv"]
            if l == 0:
                P.copy(vv, v_, e="act")
                if d == 0:
                    P.copy(vfT[:, :, t0:t0 + 256], v_, e="act")
            else:
                ps = psA()
                for g in range(2):
                    P.mm(ps[0:32, 0:256], v1sb[:, g, :], us[:, 4 + g, :], start=(g == 0), stop=(g == 1))
                P.copy(vv1, ps[0:32, 0:256], e="act")
                for g in range(2):
                    ps = psA()
                    P.mm(ps[:, 0:256], v2sb[:, g * 128:(g + 1) * 128], vv1)
                    P.act(A["tmp1"][:, g, :], ps[:, 0:256], AF.Sigmoid, bias=v0[:, g:g + 1])
                P.tt(A["tmp2"], vfT[:, :, t0:t0 + 256], v_, ALU.subtract)
                P.tt(A["tmp2"], A["tmp2"], A["tmp1"], ALU.mult)
                P.tt(vv, v_, A["tmp2"], ALU.add)
            kap = A["kap"]
            for g in range(2):
                P.ts(kap[:, g, :], k_[:, g, :], k_k[:, g:g + 1], None, op0=ALU.mult)
            P.tt(A["tmp1"], kap, kap, ALU.mult)
            for g in range(2):
                ps = psA()
                P.mm(ps[:, 0:256], blockones, A["tmp1"][:, g, :])
                P.ts(A["tmp2"][:, g, :], ps[:, 0:256], 1e-24, None, op0=ALU.max)
            P.rsqrt(A["tmp2"], A["tmp2"], 0.0)
            P.tt(kap, kap, A["tmp2"], ALU.mult)
            P.act(th, us[:, 6, :], AF.Tanh)
            sg, aa = A["sg"], A["aa"]
            for g in range(2):
                ps = psA()
                P.mm(ps[:, 0:256], w2sb[64 * d:64 * d + 64, g * 128:(g + 1) * 128], th[64 * d:64 * d + 64, :])
                P.act(sg[:, g, :], ps[:, 0:256], AF.Sigmoid, bias=w0[d][:, g:g + 1])
            for g in range(2):
                ps = psA()
                P.mm(ps[:, 0:256], a2sb[64 * d:64 * d + 64, g * 128:(g + 1) * 128], us[64 * d:64 * d + 64, 7, :])
                P.act(aa[:, g, :], ps[:, 0:256], AF.Sigmoid, bias=a0[d][:, g:g + 1])
            if full:
                for g in range(2):
                    ps = psA()
                    P.mm(ps[:, 0:256], a2sb[0:64, g * 128:(g + 1) * 128], us[0:64, 7, :])
                    P.act(A["aaf"][:, g, :], ps[:, 0:256], AF.Sigmoid, bias=a0[0][:, g:g + 1])
            key, bb = A["key"], A["bb"]
            for g in range(2):
                P.ts(A["tmp1"][:, g, :], aa[:, g, :], k_a[:, g:g + 1], omka[:, g:g + 1], op0=ALU.mult, op1=ALU.add)
            P.tt(key, k_, A["tmp1"], ALU.mult)
            P.tt(bb, kap, aa, ALU.mult)
            cs = A["cs"]
            csf = cs.v(cs.ap.rearrange("p g t -> p (g t)"))
            sgf = sg.v(sg.ap.rearrange("p g t -> p (g t)"))
            P.op("dve", lambda en: en.tensor_tensor_scan(csf.ap, rst.ap, sgf.ap, 0.0, ALU.mult, ALU.add), r=[rst, sg], w=[cs])
            cs4 = cs.v(cs.ap.rearrange("p g (c t) -> p g c t", t=64))
            P.copy(tot8, cs4[:, :, :, 63], e="dve")
            totb = Tile(tot8.ap.unsqueeze(3).to_broadcast([128, 2, 4, 64]), tot8.bufs)
            if d == 1:
                c2 = A["cs2"]
                c24 = c2.v(c2.ap.rearrange("p g (c t) -> p g c t", t=64))
                P.tt(c24, totb, cs4, ALU.subtract)
                P.tt(c2, c2, sg, ALU.add)
                cs = c2
                cs4 = c24
            P.act(etot, tot8, AF.Exp, scale=-DEC)
            P.act(A["e1"], cs, AF.Exp, scale=-DEC)
            P.act(A["e2"], cs, AF.Exp, scale=DEC)
            P.tt(A["tmp1"], cs, sg, ALU.subtract)
            P.act(A["eex"], A["tmp1"], AF.Exp, scale=-DEC)
            t24 = A["tmp2"].v(A["tmp2"].ap.rearrange("p g (c t) -> p g c t", t=64))
            P.tt(t24, totb, cs4, ALU.subtract)
            P.act(A["e3"], A["tmp2"], AF.Exp, scale=-DEC)
            P.tt(A["rt"], r_, A["e1"], ALU.mult)
            P.tt(A["kt"], kap, A["eex"], ALU.mult)
            P.tt(A["bh"], bb, A["e2"], ALU.mult)
            P.tt(A["kh"], key, A["e2"], ALU.mult)
            P.tt(A["bck"], bb, A["e3"], ALU.mult)
            P.tt(A["kck"], key, A["e3"], ALU.mult)

        def hs(h):
            return (h % 2) * 2 + h // 2

        def pair(ps, n):
            return ps.v(ps.ap[:, :, 0:2 * n].rearrange("p a (b n) -> p a b n", n=n))

        def sv(t, n0=None, n1=None):
            ap = t.ap if n0 is None else t.ap[:, :, n0:n1]
            return t.v(ap.rearrange("p (a b) n -> p a b n", a=2))

        def bc4(m, d, p, n):
            return Tile(m.ap[:, d, :].unsqueeze(1).unsqueeze(1).to_broadcast([p, 2, 2, n]), m.bufs)

        def machinery(sl, c, d, zk):
            rt, kt, bh, kh, bck, kck, vv = A["rt"], A["kt"], A["bh"], A["kh"], A["bck"], A["kck"], A["vv"]
            b0 = 2 * sl
            ps = P.psum_at(b0, 2)
            pv = pair(ps, 256)
            for h in range(4):
                po, a, b = 64 * (h % 2), h % 2, h // 2
                P.mm(pv[0:64, a, b, 0:64], fm(kt, h, c), idbb[po:po + 64, :])
                P.mm(pv[0:64, a, b, 64:128], fm(kt, h, c), fm(kh, h, c))
                P.mm(pv[0:64, a, b, 128:192], fm(kt, h, c), fm(bh, h, c))
                P.mm(pv[0:64, a, b, 192:256], fm(bh, h, c), fm(kt, h, c))
            P.tt(sv(XA[sl]), pv[0:64], bc4(rmask2, d, 64, 256), ALU.mult)
            yield
            ps = P.psum_at(b0, 2)
            pv = pair(ps, 128)
            for h in range(4):
                po, a, b = 64 * (h % 2), h % 2, h // 2
                P.mm(pv[0:64, a, b, 0:64], fm(bh, h, c), fm(rt, h, c))
                P.mm(pv[0:64, a, b, 64:128], fm(bck, h, c), idbb[po:po + 64, :])
            P.tt(sv(RB[sl]), pv[0:64], bc4(rmask5, d, 64, 128), ALU.mult)
            yield
            ps = P.psum_at(b0)
            pv = v4(ps, 128)
            for h in range(4):
                P.mm(pv[0:64, hs(h), :], XA[sl][:, hs(h), 192:256], XA[sl][:, hs(h), 0:128])
            P.tt(W32[sl], XA[sl][:, :, 0:128], pv[0:64], ALU.subtract)
            P.copy(Wb[sl], W32[sl], e="act")
            yield
            for g in range(2):
                P.ts(Dg[sl][:, g, :], idb, etot[:, g, c:c + 1], None, op0=ALU.mult)
            ps = P.psum_at(b0, 2)
            pv = pair(ps, 128)
            for h in range(4):
                po, a, b = 64 * (h % 2), h % 2, h // 2
                P.mm(pv[0:64, a, b, 0:64], idbb[po:po + 64, :], fm(rt, h, c))
                P.mm(pv[0:64, a, b, 64:128], idbb[po:po + 64, :], Dg[sl][po:po + 64, h // 2, :])
                P.mm(pv[64:128, a, b, 0:64], fm(kh, h, c), fm(rt, h, c))
                P.mm(pv[64:128, a, b, 64:128], fm(kck, h, c), idbb[po:po + 64, :])
            P.tt(sv(Rt[sl]), pv, bc4(rmaskr, d, 128, 128), ALU.mult)
            yield
            for lev in range(5):
                ps = P.psum_at(b0 + 1)
                pv = v4(ps, 128)
                for h in range(4):
                    q = hs(h)
                    if lev == 0:
                        Ah, ATh = XA[sl][:, q, 128:192], XA[sl][:, q, 192:256]
                    else:
                        Ah, ATh = Pw[sl][:, q, 0:64], Pw[sl][:, q, 64:128]
                    P.mm(pv[0:64, q, 0:64], ATh, Ah)
                    P.mm(pv[0:64, q, 64:128], Ah, ATh)
                P.copy(Pw[sl], pv[0:64], e="act")
                yield
                ps = P.psum_at(b0)
                pv = v4(ps, 128)
                for h in range(4):
                    q = hs(h)
                    P.mm(pv[0:64, q, :], Pw[sl][:, q, 64:128], Wb[sl][:, q, :])
                P.tt(W32[sl], W32[sl], pv[0:64], ALU.add)
                P.copy(Wb[sl], W32[sl], e="act")
                yield
                if lev == 0:
                    ps = P.psum_at(b0, 2)
                    pv = pair(ps, 64)
                    for h in range(4):
                        po, a, b = 64 * (h % 2), h % 2, h // 2
                        P.mm(pv[64:128, a, b, :], fm(vv, h, c), idb[po:po + 64, :])
                    zv = Zr[zk % 8][2]
                    P.copy(zv.v(zv.ap.rearrange("p (a b) n -> p a b n", a=2)), pv[64:128], e="act")
                    yield
            ps = P.psum_at(b0 + 1)
            pv = v4(ps, 128)
            for h in range(4):
                q = hs(h)
                P.mm(pv[:, q, :], Wb[sl][:, q, :], RB[sl][:, q, :])
            P.tt(Rt[sl], Rt[sl], pv, ALU.subtract)
            yield

        def sequential(sl, c, d, zk, t0, emit_y, ysum):
            Z = Zr[zk % 8][0]
            if emit_y:
                ps = P.psum_at(6)
                pv = ps.v(ps.ap[:, 0:128].rearrange("p (g n) -> p g n", n=64))
                for h in range(4):
                    po = 64 * (h % 2)
                    P.mm(pv[po:po + 64, h // 2, :], Z[:, hs(h), :], Rt[sl][:, hs(h), 0:64])
                tk = t0 + c * 64
                if d == 0:
                    P.copy(yfwd[:, :, tk:tk + 64], pv, e="act")
                else:
                    P.tt(ysum[:, :, c * 64:(c + 1) * 64], pv, yfwd[:, :, tk:tk + 64], ALU.add)
            ps = P.psum_at(7)
            pv = ps.v(ps.ap[:, 0:256].rearrange("p (h n) -> p h n", n=64))
            for h in range(4):
                P.mm(pv[0:64, hs(h), :], Rt[sl][:, hs(h), 64:128], Z[:, hs(h), :])
            P.copy(Zr[(zk + 1) % 8][1], pv[0:64], e="dve")

        def readout(bi):
            t0 = 256 * bi
            r_, k_ = us[:, 0:2, :], us[:, 2:4, :]
            ysum, yc, sq, rstd = A["e1"], A["e2"], A["eex"], A["e3"]
            for g in range(2):
                ps = psA()
                P.mm(ps[:, 0:256], blockones, ysum[:, g, :])
                P.stt(yc[:, g, :], ps[:, 0:256], -1.0 / 64, ysum[:, g, :], ALU.mult, ALU.add)
            P.tt(sq, yc, yc, ALU.mult)
            for g in range(2):
                ps = psA()
                P.mm(ps[:, 0:256], blockones, sq[:, g, :])
                P.ts(rstd[:, g, :], ps[:, 0:256], 1.0 / 64, None, op0=ALU.mult)
            P.rsqrt(rstd, rstd, LNX_EPS)
            P.tt(yc, yc, rstd, ALU.mult)
            for g in range(2):
                P.ts(yc[:, g, :], yc[:, g, :], lnx_w[:, g:g + 1], lnx_b[:, g:g + 1], op0=ALU.mult, op1=ALU.add)
            tk = A["tmp1"]
            P.tt(tk, A["aa"], A["aaf"], ALU.add)
            for g in range(2):
                P.ts(tk[:, g, :], tk[:, g, :], kah[:, g:g + 1], omka[:, g:g + 1], op0=ALU.mult, op1=ALU.add)
            rk = A["tmp2"]
            P.tt(rk, r_, k_, ALU.mult)
            P.tt(rk, rk, tk, ALU.mult)
            for g in range(2):
                P.ts(rk[:, g, :], rk[:, g, :], r_k[:, g:g + 1], None, op0=ALU.mult)
            for g in range(2):
                ps = psA()
                P.mm(ps[:, 0:256], blockones, rk[:, g, :])
                P.tt(sq[:, g, :], ps[:, 0:256], A["vv"][:, g, :], ALU.mult)
            P.tt(yc, yc, sq, ALU.add)
            P.act(sgx, us[:, 8, :], AF.Sigmoid)
            for g in range(2):
                ps = psA()
                P.mm(ps[:, 0:256], g2sb[:, g * 128:(g + 1) * 128], sgx)
                P.tt(yrT[:, g, t0:t0 + 256], yc[:, g, :], ps[:, 0:256], ALU.mult)

        nblk = TT // 256
        import os
        RWC = int(os.environ.get("RW_CUT", "99"))
        if RWC < 99:
            prep(0, 0, False)
            if RWC >= 2:
                list(machinery(0, 0, 0, 0))
            if RWC >= 3:
                sequential(0, 0, 0, 0, 0, True, A["e1"])
            if RWC >= 4:
                prep(1, 1, True)
                list(machinery(0, 0, 1, 0))
                sequential(0, 0, 1, 0, 256, True, A["e1"])
                readout(1)
            P.barrier()
            return
        for d in range(2):
            order = list(range(nblk)) if d == 0 else [0] + list(range(nblk - 1, 0, -1))
            zk = 0
            P.memset(Zr[0][1], 0.0)
            for bi in order:
                emit_y = need_ctx or bi > 0
                prep(bi, d, full=(d == 1))
                chunks = [0, 1, 2, 3] if d == 0 else [3, 2, 1, 0]
                for g0 in range(0, 4, NLOCK):
                    grp = chunks[g0:g0 + NLOCK]
                    gens = [machinery(sl, c, d, zk + sl) for sl, c in enumerate(grp)]
                    while gens:
                        for gn in list(gens):
                            try:
                                next(gn)
                            except StopIteration:
                                gens.remove(gn)
                    for sl, c in enumerate(grp):
                        sequential(sl, c, d, zk + sl, 256 * bi, emit_y, A["e1"])
                    zk += len(grp)
                if d == 1 and emit_y:
                    readout(bi)
        P.barrier()


def pack_vecs(inp):
    v = np.zeros((128, 2, NV), np.float32)

    def put(l, name, arr):
        o, c = VO[name]
        a = np.asarray(arr, np.float32).reshape(c, 128).T
        v[:, l, o:o + c] = a

    for l in range(2):
        put(l, "modb", inp["mod_b"][l])
        put(l, "g_mpre", inp["norm_mix_pre"][l])
        put(l, "g_mpost", inp["norm_mix_post"][l])
        put(l, "g_fpre", inp["norm_ffn_pre"][l])
        put(l, "g_fpost", inp["norm_ffn_post"][l])
        put(l, "mu0", inp["rwkv_mu"][l, 0])
        put(l, "mu1", inp["rwkv_mu"][l, 1])
        put(l, "w0f", inp["rwkv_w0"][l, 0])
        put(l, "w0b", inp["rwkv_w0"][l, 1])
        put(l, "a0f", inp["rwkv_a0"][l, 0])
        put(l, "a0b", inp["rwkv_a0"][l, 1])
        put(l, "k_k", inp["rwkv_k_k"][l])
        put(l, "k_a", inp["rwkv_k_a"][l])
        put(l, "r_k", inp["rwkv_r_k"][l])
        put(l, "lnx_w", inp["rwkv_lnx_w"][l])
        put(l, "lnx_b", inp["rwkv_lnx_b"][l])
        if l > 0:
            put(l, "v0", inp["rwkv_v0"][l - 1])
        put(l, "cw0", inp["ffn_conv_w"][l, 0])
        put(l, "cw1", inp["ffn_conv_w"][l, 1])
        put(l, "cw2", inp["ffn_conv_w"][l, 2])
        put(l, "cb", inp["ffn_conv_b"][l])
        o, c = VO["sink"]
        v[:, l, o:o + c] = np.broadcast_to(np.asarray(inp["attn_sink"][l], np.float32)[None, :], (128, 8))
    return v


def make_in_maps(inp, ncores=8):
    global HC
    if HC is None:
        HC = host_consts()
    vecs = pack_vecs(inp)
    shared = {k: np.ascontiguousarray(np.asarray(inp[k], np.float32)) for k in WEIGHT_SHAPES}
    shared.update(HC)
    shared["vecs"] = vecs
    maps = []
    c = np.asarray(inp["c"], np.float32)
    cc = np.asarray(inp["c_ctx"], np.float32)
    for ci in range(ncores):
        b0 = ci * NSEQ
        cs = np.zeros((128, 8, 5), np.float32)
        for j in range(NSEQ):
            cs[:, :, j] = c[b0 + j].reshape(8, 128).T
        cs[:, :, 4] = cc.reshape(8, 128).T
        m = dict(shared)
        m["x"] = np.ascontiguousarray(np.asarray(inp["x"][b0:b0 + NSEQ], np.float32))
        m["ctx"] = np.ascontiguousarray(np.asarray(inp["ctx"][b0:b0 + NSEQ], np.float32))
        m["cs"] = cs
        maps.append(m)
    return maps


DEFAULT_STAGES = ("rwkv", "att", "fou", "merge", "ffn")


def kernel(**inputs):
    nc, P = build_program(stages=DEFAULT_STAGES)
    maps = make_in_maps(inputs)
    res = run_bass_kernel_spmd(nc, maps, core_ids=list(range(8)))
    return np.concatenate([np.asarray(r["y"], np.float32) for r in res.results], axis=0)
```
